# Optimizing a Trainium2 kernel written in Bass

```python
import math
import jax, jax.numpy as jnp
from jax import lax
import numpy as np


D_MODEL = 1024
BATCH = 2
SEQ = 16384
DEPTH = 4

CHUNK = 64
QBLOCK = 128
MIX_WIDTH = D_MODEL
HEAD_DIM = 64
CONV_WIDTH = MIX_WIDTH // 4
CONV_K = 3
ATT_WIDTH = (MIX_WIDTH - CONV_WIDTH) // 2
DSA_HEADS = ATT_WIDTH // HEAD_DIM
CHK_HEADS = ATT_WIDTH // HEAD_DIM
ROPE_DIM = HEAD_DIM // 4
ROPE_THETA = 500000.0
IDX_HEADS = 8
IDX_DIM = 32
IDX_ROPE_DIM = IDX_DIM // 4
TOPK_MAX = 256
BAND_CHUNKS = 8
REL_CLIP = 128
REL_SIZE = (CHUNK - 1) + REL_CLIP + 1
MEM_LEN = 256
MEM_HEADS = 4
MEM_HEAD_DIM = D_MODEL // MEM_HEADS
D_FF = 4 * D_MODEL
LN_EPS = 1e-5
ALPHA = (2.0 * DEPTH) ** 0.25
BETA = (8.0 * DEPTH) ** -0.25
IN_SIZES = (CONV_WIDTH, CONV_WIDTH, CONV_WIDTH,
            DSA_HEADS * HEAD_DIM, HEAD_DIM, HEAD_DIM,
            IDX_HEADS * IDX_DIM, IDX_DIM, IDX_HEADS,
            CHK_HEADS * HEAD_DIM, CHK_HEADS * HEAD_DIM, CHK_HEADS * HEAD_DIM)
IN_WIDTH = sum(IN_SIZES)

kernel_name = 'hybrid_conv_dsa_chunkband_deepnorm'


def layer_norm(x, g, b):
    xf = x.astype(jnp.float32)
    mu = jnp.mean(xf, axis=-1, keepdims=True)
    var = jnp.mean(jnp.square(xf - mu), axis=-1, keepdims=True)
    y = (xf - mu) * lax.rsqrt(var + LN_EPS) * g.astype(jnp.float32) + b.astype(jnp.float32)
    return y.astype(x.dtype)


def partial_rope(x, positions, rot_dim):
    half = rot_dim // 2
    inv_freq = jnp.power(jnp.float32(ROPE_THETA), -jnp.arange(half, dtype=jnp.float32) / half)
    ang = positions.astype(jnp.float32)[..., None] * inv_freq
    cos = jnp.cos(ang)[:, :, None, :]
    sin = jnp.sin(ang)[:, :, None, :]
    x1 = x[..., :half].astype(jnp.float32)
    x2 = x[..., half:rot_dim].astype(jnp.float32)
    rot = jnp.concatenate([x1 * cos - x2 * sin, x2 * cos + x1 * sin], axis=-1)
    return jnp.concatenate([rot.astype(x.dtype), x[..., rot_dim:]], axis=-1)


def short_conv_mixer(gate_b, gate_c, h, conv_w):
    u = gate_c * h
    S = u.shape[1]
    u_pad = jnp.pad(u, ((0, 0), (CONV_K - 1, 0), (0, 0)))
    y = conv_w[0] * u_pad[:, 0:S]
    for j in range(1, CONV_K):
        y = y + conv_w[j] * u_pad[:, j:j + S]
    return gate_b * y


def dsa_attention(q, k, v, iq, ik, iw):
    B_, S, H, Dh = q.shape
    nb = S // QBLOCK
    topk = min(TOPK_MAX, S // 4)
    scale = Dh ** -0.5
    key_chunk = jnp.arange(S) // CHUNK
    iw = iw * (IDX_HEADS ** -0.5)

    def to_blocks(a):
        return jnp.moveaxis(a.reshape((B_, nb, QBLOCK) + a.shape[2:]), 1, 0)

    def one_block(args):
        bi, qb, iqb, iwb = args
        q_chunk = (bi * QBLOCK + jnp.arange(QBLOCK)) // CHUNK
        logits = jnp.einsum('bqhd,bsd->bqhs', iqb, ik)
        score = jnp.einsum('bqh,bqhs->bqs', iwb, jax.nn.relu(logits)).astype(jnp.float32)
        admissible = key_chunk[None, :] <= q_chunk[:, None]
        score = jnp.where(admissible[None], score, -jnp.inf)
        _, idx = lax.top_k(score, topk)
        valid = (idx // CHUNK) <= q_chunk[None, :, None]
        k_sel = jax.vmap(lambda a, ii: a[ii])(k, idx)
        v_sel = jax.vmap(lambda a, ii: a[ii])(v, idx)
        s = jnp.einsum('bqhd,bqkd->bhqk', qb, k_sel).astype(jnp.float32) * scale
        s = jnp.where(valid[:, None], s, -jnp.inf)
        p = jax.nn.softmax(s, axis=-1)
        return jnp.einsum('bhqk,bqkd->bqhd', p.astype(v_sel.dtype), v_sel)

    out = lax.map(one_block, (jnp.arange(nb), to_blocks(q), to_blocks(iq), to_blocks(iw)))
    return jnp.moveaxis(out, 0, 1).reshape(B_, S, H * Dh)


def chunk_band_attention(q, k, v, rel_bias):
    B_, S, H, Dh = q.shape
    nc = S // CHUNK
    band = (BAND_CHUNKS + 1) * CHUNK
    pad = BAND_CHUNKS * CHUNK
    scale = Dh ** -0.5
    k_pad = jnp.pad(k, ((0, 0), (pad, 0), (0, 0), (0, 0)))
    v_pad = jnp.pad(v, ((0, 0), (pad, 0), (0, 0), (0, 0)))
    i = jnp.arange(CHUNK)[:, None]
    j = jnp.arange(band)[None, :]
    rel = i - j + pad
    rel_idx = jnp.clip(rel, -(CHUNK - 1), REL_CLIP) + (CHUNK - 1)
    bias = rel_bias[:, rel_idx].astype(jnp.float32)
    q_chunks = jnp.moveaxis(q.reshape(B_, nc, CHUNK, H, Dh), 1, 0)

    def one_chunk(args):
        c, qc = args
        start = c * CHUNK
        kc = lax.dynamic_slice_in_dim(k_pad, start, band, axis=1)
        vc = lax.dynamic_slice_in_dim(v_pad, start, band, axis=1)
        s = jnp.einsum('bqhd,bkhd->bhqk', qc, kc).astype(jnp.float32) * scale + bias
        valid = jnp.arange(band) >= (pad - start)
        s = jnp.where(valid, s, -jnp.inf)
        p = jax.nn.softmax(s, axis=-1)
        return jnp.einsum('bhqk,bkhd->bqhd', p.astype(vc.dtype), vc)

    out = lax.map(one_chunk, (jnp.arange(nc), q_chunks))
    return jnp.moveaxis(out, 0, 1).reshape(B_, S, H * Dh)


def hybrid_mixer(x, positions, w_in, conv_w, rel_bias, w_out):
    B_, S, _ = x.shape
    proj = jnp.einsum('bsd,dp->bsp', x, w_in)
    offs = np.cumsum(np.array(IN_SIZES))[:-1].tolist()
    (cb, cc, ch, dq, dk, dv, iq, ik, iw, cq, ck, cv) = jnp.split(proj, offs, axis=-1)
    y_conv = short_conv_mixer(cb, cc, ch, conv_w)
    dq = partial_rope(dq.reshape(B_, S, DSA_HEADS, HEAD_DIM), positions, ROPE_DIM)
    dk = partial_rope(dk.reshape(B_, S, 1, HEAD_DIM), positions, ROPE_DIM)[:, :, 0]
    iq = partial_rope(iq.reshape(B_, S, IDX_HEADS, IDX_DIM), positions, IDX_ROPE_DIM)
    ik = partial_rope(ik.reshape(B_, S, 1, IDX_DIM), positions, IDX_ROPE_DIM)[:, :, 0]
    y_dsa = dsa_attention(dq, dk, dv, iq, ik, iw)
    cq = cq.reshape(B_, S, CHK_HEADS, HEAD_DIM)
    ck = ck.reshape(B_, S, CHK_HEADS, HEAD_DIM)
    cv = cv.reshape(B_, S, CHK_HEADS, HEAD_DIM)
    y_chk = chunk_band_attention(cq, ck, cv, rel_bias)
    y = jnp.concatenate([y_conv, y_dsa, y_chk], axis=-1)
    return jnp.einsum('bsm,md->bsd', y, w_out)


def memory_cross_attention(x, mem, w_q, w_kv, w_o):
    B_, S, _ = x.shape
    M = mem.shape[1]
    q = jnp.einsum('bsd,de->bse', x, w_q).reshape(B_, S, MEM_HEADS, MEM_HEAD_DIM)
    k, v = jnp.split(jnp.einsum('bmd,de->bme', mem, w_kv), 2, axis=-1)
    k = k.reshape(B_, M, MEM_HEADS, MEM_HEAD_DIM)
    v = v.reshape(B_, M, MEM_HEADS, MEM_HEAD_DIM)
    s = jnp.einsum('bshd,bmhd->bhsm', q, k).astype(jnp.float32) * (MEM_HEAD_DIM ** -0.5)
    p = jax.nn.softmax(s, axis=-1)
    o = jnp.einsum('bhsm,bmhd->bshd', p.astype(v.dtype), v).reshape(B_, S, D_MODEL)
    return jnp.einsum('bse,ed->bsd', o, w_o)


def squared_relu_mlp(x, w1, w2):
    h = jnp.square(jax.nn.relu(jnp.einsum('bsd,df->bsf', x, w1)))
    return jnp.einsum('bsf,fd->bsd', h, w2)


def setup_inputs(seed: int = 0) -> dict:
    key = jax.random.key(seed)
    ks = jax.random.split(key, 20)
    f32 = jnp.float32
    nrm = lambda k, shape, s: jax.random.normal(k, shape, f32) * s
    x = nrm(ks[0], (BATCH, SEQ, D_MODEL), 1.0)
    mem = nrm(ks[1], (BATCH, MEM_LEN, D_MODEL), 1.0)
    offset = jax.random.randint(ks[2], (BATCH,), 0, 4096, dtype=jnp.int32)
    positions = (offset[:, None] + jnp.arange(SEQ, dtype=jnp.int32)[None, :]).astype(jnp.int32)
    return {
        'x': x,
        'mem': mem,
        'positions': positions,
        'w_in': nrm(ks[3], (DEPTH, D_MODEL, IN_WIDTH), D_MODEL ** -0.5),
        'conv_w': nrm(ks[4], (DEPTH, CONV_K, CONV_WIDTH), CONV_K ** -0.5),
        'rel_bias': nrm(ks[5], (DEPTH, CHK_HEADS, REL_SIZE), 0.5),
        'w_mix_out': nrm(ks[6], (DEPTH, MIX_WIDTH, D_MODEL), BETA * MIX_WIDTH ** -0.5),
        'ln1_g': 1.0 + nrm(ks[7], (DEPTH, D_MODEL), 0.02),
        'ln1_b': nrm(ks[8], (DEPTH, D_MODEL), 0.02),
        'w_mq': nrm(ks[9], (DEPTH, D_MODEL, D_MODEL), D_MODEL ** -0.5),
        'w_mkv': nrm(ks[10], (DEPTH, D_MODEL, 2 * D_MODEL), D_MODEL ** -0.5),
        'w_mo': nrm(ks[11], (DEPTH, D_MODEL, D_MODEL), BETA * D_MODEL ** -0.5),
        'ln2_g': 1.0 + nrm(ks[12], (DEPTH, D_MODEL), 0.02),
        'ln2_b': nrm(ks[13], (DEPTH, D_MODEL), 0.02),
        'w_ff1': nrm(ks[14], (DEPTH, D_MODEL, D_FF), D_MODEL ** -0.5),
        'w_ff2': nrm(ks[15], (DEPTH, D_FF, D_MODEL), BETA * D_FF ** -0.5),
        'ln3_g': 1.0 + nrm(ks[16], (DEPTH, D_MODEL), 0.02),
        'ln3_b': nrm(ks[17], (DEPTH, D_MODEL), 0.02),
    }


def reference(x, mem, positions, w_in, conv_w, rel_bias, w_mix_out, ln1_g, ln1_b,
              w_mq, w_mkv, w_mo, ln2_g, ln2_b, w_ff1, w_ff2, ln3_g, ln3_b):
    for l in range(DEPTH):
        mix = hybrid_mixer(x, positions, w_in[l], conv_w[l], rel_bias[l], w_mix_out[l])
        x = layer_norm(ALPHA * x + mix, ln1_g[l], ln1_b[l])
        cross = memory_cross_attention(x, mem, w_mq[l], w_mkv[l], w_mo[l])
        x = layer_norm(ALPHA * x + cross, ln2_g[l], ln2_b[l])
        ff = squared_relu_mlp(x, w_ff1[l], w_ff2[l])
        x = layer_norm(ALPHA * x + ff, ln3_g[l], ln3_b[l])
    return x
```

```python
import numpy as np
from contextlib import ExitStack
import concourse.bass as bass
import concourse.mybir as mybir
from concourse.bass_utils import run_bass_kernel_spmd

F32 = mybir.dt.float32
BF16 = mybir.dt.bfloat16
I32 = mybir.dt.int32
AF = mybir.ActivationFunctionType
ALU = mybir.AluOpType
AX = mybir.AxisListType

D = 1024
INW = 2728
DFF = 4096
NEG = -30000.0
ALPHA_L = lambda L: (2.0 * L) ** 0.25
LN_EPS = 1e-5
PI = float(np.pi)
NBIS = 20
SAME_ENG_SYNC = True
PAYR = 928

C_ID, C_PMQ, C_PMI, C_INVQ, C_INVI, C_SEL, C_NEGS, C_SELA, C_SELB, C_POW, C_MADM = 0, 128, 256, 384, 385, 386, 426, 434, 438, 442, 474
C_KTAB = C_MADM + 512


class Tr:
    def __init__(self, nc, es):
        self.nc = nc
        self.es = es
        self.E = {'pe': nc.tensor, 'dve': nc.vector, 'act': nc.scalar, 'pool': nc.gpsimd, 'sp': nc.sync}
        self.sem = {k: es.enter_context(nc.semaphore('s_' + k)) for k in self.E}
        self.cnt = {k: 0 for k in self.E}
        self.waited = {k: {} for k in self.E}
        self.W = {}
        self.R = {}
        self.dpool = []
        self.dmap = {}
        self.B1 = es.enter_context(nc.semaphore('bar1'))
        self.B2 = es.enter_context(nc.semaphore('bar2'))
        self.bk = 0
        self.nins = 0
        self.groups = {}
        self.gctr = 0

    def _dkey(self, key):
        if key not in self.dmap:
            idx = len(self.dmap)
            if idx >= len(self.dpool):
                self.dpool.append([self.es.enter_context(self.nc.semaphore('d_%d' % idx)), 0, idx])
            self.dmap[key] = idx
        return self.dpool[self.dmap[key]]

    def op(self, eng, fn, reads=(), writes=(), dma=None):
        deps = {}

        def merge(d):
            for k, sv in d.items():
                if k not in deps or deps[k][1] < sv[1]:
                    deps[k] = sv
        rl_ = []
        for r in reads:
            if r.startswith('@'):
                rl_.extend(self.groups.get(r, []))
            else:
                rl_.append(r)
        wl_ = []
        for w in writes:
            if w.startswith('@'):
                g = self.groups.setdefault(w, [])
                had_readers = False
                for sname in g:
                    if self.R.get(sname):
                        had_readers = True
                        merge(self.R[sname])
                if had_readers:
                    for sname in g:
                        merge(self.W.get(sname, {}))
                        self.W.pop(sname, None)
                        self.R.pop(sname, None)
                    del g[:]
                self.gctr += 1
                sname = '%s#%d' % (w, self.gctr)
                g.append(sname)
                wl_.append(sname)
            else:
                wl_.append(w)
        bk = [x for x in rl_ + wl_ if len(x) == 2 and x[0] == 'b' and x[1].isdigit()]
        rl_ = [x for x in rl_ if x not in bk]
        wl_ = [x for x in wl_ if x not in bk] + sorted(set(bk))
        reads, writes = rl_, wl_
        for r in reads:
            merge(self.W.get(r, {}))
        for w in writes:
            merge(self.W.get(w, {}))
            merge(self.R.get(w, {}))
        e = self.E[eng]
        wd = self.waited[eng]
        for k, (s, v) in deps.items():
            if k == eng and (eng == 'pe' or not SAME_ENG_SYNC):
                continue
            if wd.get(k, 0) >= v:
                continue
            e.wait_ge(s, v)
            wd[k] = v
        ins = fn(e)
        self.nins += 1
        if dma is not None:
            d = self._dkey(dma)
            d[1] += 16
            ins.then_inc(d[0], 16)
            ev = ('D%d' % d[2], (d[0], d[1]))
        else:
            self.cnt[eng] += 1
            ins.then_inc(self.sem[eng], 1)
            ev = (eng, (self.sem[eng], self.cnt[eng]))
        for w in writes:
            self.W[w] = {ev[0]: ev[1]}
            self.R[w] = {}
        for r in reads:
            if r not in writes:
                self.R.setdefault(r, {})[ev[0]] = ev[1]

    def new_epoch(self, tag):
        for k in self.E:
            self.sem[k] = self.es.enter_context(self.nc.semaphore('s_%s_%s' % (k, tag)))
            self.cnt[k] = 0
            for wd in self.waited.values():
                wd.pop(k, None)

    def barrier(self):
        for eng, e in self.E.items():
            wd = self.waited[eng]
            for k in self.E:
                if k == eng:
                    continue
                v = self.cnt[k]
                if v > wd.get(k, 0):
                    e.wait_ge(self.sem[k], v)
                    wd[k] = v
            for (s, v, idx) in self.dpool:
                k = 'D%d' % idx
                if v > wd.get(k, 0):
                    e.wait_ge(s, v)
                    wd[k] = v
        self.W.clear()
        self.R.clear()
        self.groups.clear()
        self.dmap.clear()


import os


class _Stop(Exception):
    pass


def build_program(S, L, debug=()):
    try:
        return _build_program(S, L, debug)
    except _Stop as e:
        return e.args[0]


def _build_program(S, L, debug=()):
    TL = S // 4
    NI = TL // 128
    TT = 512 if TL >= 512 else TL
    NT = TL // TT
    NB4 = TT // 128
    ALPHA = float(8.0 ** 0.25)
    NCST = C_KTAB + NI

    nc = bass.Bass("TRN2", target_bir_lowering=False)
    es = ExitStack()
    tr = Tr(nc, es)

    def din(name, shape, dt=F32):
        return nc.dram_tensor(name, list(shape), dt, kind="ExternalInput").ap()

    def dscr(name, shape, dt):
        return nc.dram_tensor(name, list(shape), dt).ap()

    xT_in = din("xT", [D, TL])
    pos_in = din("pos", [1, TL], I32)
    cst_in = din("cst", [128, NCST])
    memT_in = din("memT", [D, 256])
    w_in = din("w_in", [L, D, INW])
    conv_w = din("conv_w", [128, L * 3, 2])
    rel_bias = din("relb", [L, 128, 5 * 6 * 128])
    w_mix_out = din("w_mix_out", [L, D, D])
    lnp_in = din("lnp", [128, L * 6, 8])
    w_mq = din("w_mq", [L, D, D])
    w_mkv = din("w_mkv", [L, D, 2 * D])
    w_mo = din("w_mo", [L, D, D])
    w_ff1 = din("w_ff1", [L, D, DFF])
    w_ff2 = din("w_ff2", [L, DFF, D])
    outT = nc.dram_tensor("outT", [D, TL], F32, kind="ExternalOutput").ap()

    xres = dscr("xres", [8, 128, TL], F32)
    xb = dscr("xb", [8, 128, TL], BF16)
    tabs = dscr("tabs", [4, 128, TL], F32)
    qT_d = dscr("qT_d", [3, 128, TL], BF16)
    iqT_d = dscr("iqT_d", [3, 128, TL], BF16)
    iw_d = dscr("iw_d", [TL, 8], F32)
    cqT_d = dscr("cqT_d", [3, 128, TL], BF16)
    cbu_d = dscr("cbu_d", [4, 128, TL], F32)
    pay = dscr("pay", [NT, PAYR, TT], BF16)
    gath = dscr("gath", [NT, 4 * PAYR, TT], BF16)
    hal = dscr("hal", [256, NI * 2], F32)
    hgath = dscr("hgath", [4 * 256, NI * 2], F32)
    yT_d = dscr("yT_d", [8, 128, TL], BF16)
    hT_d = dscr("hT_d", [32, 128, TL], BF16)
    ext_d = dscr("ext_d", [6, 768], F32)
    dbg_out = {}
    for nm in debug:
        pass

    stage_ctr = [0]

    def stage_end(name):
        stage_ctr[0] += 1
        if os.environ.get('KSTOP') == name:
            tr.barrier()
            es.close()
            raise _Stop(nc)

    def fm(ap3, c0, c1, t0, t1):
        return ap3[c0:c1, :, t0:t1].rearrange("c p t -> p c t")

    uid = [0]

    def sb(st, name, shape, dt):
        uid[0] += 1
        return st.enter_context(nc.sbuf_tensor("s%d_%s" % (uid[0], name), list(shape), dt))

    op = tr.op

    def dma(q, out, in_, key, reads=(), writes=()):
        op(q, lambda e: e.dma_start(out=out, in_=in_), reads, writes, dma=key)

    def mm(out, lhsT, rhs, start, stop, reads, writes):
        op('pe', lambda e: e.matmul(out, lhsT, rhs, start=start, stop=stop, skip_group_check=True), reads, writes)

    def act(out, in_, func, reads, writes, scale=None):
        if scale is None:
            op('act', lambda e: e.activation(out=out, in_=in_, func=func), reads, writes)
        else:
            op('act', lambda e: e.activation(out=out, in_=in_, func=func, scale=scale), reads, writes)

    def tt_(eng, out, in0, in1, o, reads, writes):
        op(eng, lambda e: e.tensor_tensor(out=out, in0=in0, in1=in1, op=o), reads, writes)

    def ts_(eng, out, in0, s1, s2, o0, o1, reads, writes, accum=None):
        if o1 is None:
            op(eng, lambda e: e.tensor_scalar(out=out, in0=in0, scalar1=s1, scalar2=None, op0=o0), reads, writes)
        elif accum is None:
            op(eng, lambda e: e.tensor_scalar(out=out, in0=in0, scalar1=s1, scalar2=s2, op0=o0, op1=o1), reads, writes)
        else:
            op(eng, lambda e: e.tensor_scalar(out=out, in0=in0, scalar1=s1, scalar2=s2, op0=o0, op1=o1, accum_out=accum), reads, writes)

    def stt_(out, in0, sc, in1, o0, o1, reads, writes):
        op('dve', lambda e: e.scalar_tensor_tensor(out=out, in0=in0, scalar=sc, in1=in1, op0=o0, op1=o1), reads, writes)

    def cp(eng, out, in_, reads, writes):
        op(eng, lambda e: e.tensor_copy(out=out, in_=in_), reads, writes)

    ps = es.enter_context(nc.psum_tensor("ps", [128, 8, 512], F32))
    psb = ps[:, 5, :].bitcast(BF16)
    cst = sb(es, "cst", [128, NCST], F32)
    identb = sb(es, "identb", [128, 128], BF16)
    pmq = sb(es, "pmq", [128, 128], BF16)
    pmi = sb(es, "pmi", [128, 128], BF16)
    ident3 = sb(es, "ident3", [128, 3, 128], BF16)
    onesf = sb(es, "onesf", [128, 128], F32)
    onesb = sb(es, "onesb", [128, 128], BF16)
    lnp = sb(es, "lnp", [128, L * 6, 8], F32)
    cw = sb(es, "cw", [128, L * 3, 2], F32)

    dma('sp', cst[:], cst_in[:, :], 'cst', writes=['cst'])
    dma('sp', lnp[:], lnp_in[:, :, :], 'lnp', writes=['lnp'])
    dma('sp', cw[:], conv_w[:, :, :], 'cw', writes=['cw'])
    cp('dve', identb[:], cst[:, C_ID:C_ID + 128], ['cst'], ['identb'])
    cp('dve', pmq[:], cst[:, C_PMQ:C_PMQ + 128], ['cst'], ['pmq'])
    cp('dve', pmi[:], cst[:, C_PMI:C_PMI + 128], ['cst'], ['pmi'])
    for k in range(3):
        cp('dve', ident3[:, k, :], cst[:, C_ID:C_ID + 128], ['cst'], ['ident3'])
    op('pool', lambda e: e.memset(onesf[:], 1.0 / D), [], ['onesf'])
    op('pool', lambda e: e.memset(onesb[:], 1.0), [], ['onesb'])
    tr.barrier()

    def layer_norm(st_bufs, t, l, which, t0, final, tagres):
        sq, mean_sb, m2, rstd, xnb = st_bufs
        xn = t
        W = TT
        gi = l * 6 + which * 2
        act(sq[:], t[:], AF.Square, ['t'], ['sq'])
        for c in range(8):
            mm(ps[:, 2, 0:W], onesf[:], t[:, c, :], c == 0, c == 7, ['t', 'onesf'], ['b2'])
        for c in range(8):
            mm(ps[:, 3, 0:W], onesf[:], sq[:, c, :], c == 0, c == 7, ['sq', 'onesf'], ['b3'])
        act(mean_sb[:], ps[:, 2, 0:W], AF.Copy, ['b2'], ['mean'])
        tt_('dve', m2[:], mean_sb[:], mean_sb[:], ALU.mult, ['mean'], ['m2'])
        tt_('dve', m2[:], ps[:, 3, 0:W], m2[:], ALU.subtract, ['b3', 'm2'], ['m2'])
        ts_('dve', m2[:], m2[:], LN_EPS, None, ALU.add, None, ['m2'], ['m2'])
        act(m2[:], m2[:], AF.Sqrt, ['m2'], ['m2'])
        op('dve', lambda e: e.reciprocal(out=rstd[:], in_=m2[:]), ['m2'], ['rstd'])
        tt_('dve', xn[:], t[:], mean_sb[:].unsqueeze(1).broadcast_to([128, 8, W]), ALU.subtract, ['t', 'mean', 'sq'], ['t'])
        tt_('dve', xn[:], xn[:], rstd[:].unsqueeze(1).broadcast_to([128, 8, W]), ALU.mult, ['t', 'rstd'], ['t'])
        for c in range(8):
            ts_('pool', xn[:, c, :], xn[:, c, :], lnp[:, gi, c:c + 1], lnp[:, gi + 1, c:c + 1], ALU.mult, ALU.add,
                ['t', 'lnp'], ['t'])
        if final:
            dma('sp', outT.rearrange("(c p) t -> p c t", p=128)[:, :, t0:t0 + W], xn[:], 'xn', ['t'], ['@outT'])
        else:
            dma('sp', fm(xres, 0, 8, t0, t0 + W), xn[:], 'xn', ['t'], ['@xres'])
            act(xnb[:], xn[:], AF.Copy, ['t'], ['xnb'])
            dma('sp', fm(xb, 0, 8, t0, t0 + W), xnb[:], 'xnb', ['xnb'], ['@xb'])

    def ln_bufs(st):
        return (sb(st, "ln_sq", [128, 8, TT], F32), sb(st, "ln_mean", [128, TT], F32), sb(st, "ln_m2", [128, TT], F32),
                sb(st, "ln_rstd", [128, TT], F32), sb(st, "ln_xnb", [128, 8, TT], BF16))

    def load_w_bf16(wt, src2d, ncols, name, nkc=8, col0=0):
        for kc in range(nkc if 'w' not in os.environ.get('KSKIP', '') else 0):
            dma('pool', wt[:, kc, 0:ncols], src2d[kc * 128:(kc + 1) * 128, col0:col0 + ncols], name + str(kc % 4), [], ['@' + name])

    with ExitStack() as st:
        posi = sb(st, "posi", [128, TL], I32)
        posf = sb(st, "posf", [128, TL], F32)
        ang = sb(st, "ang", [128, TL], F32)
        tb = sb(st, "tb", [128, TL], F32)
        posf2 = sb(st, "posf2", [128, TL], F32)
        dma('sp', posi[:], pos_in[0:1, :].broadcast_to([128, TL]), 'posi', [], ['posi'])
        cp('dve', posf[:], posi[:], ['posi'], ['posf'])
        C1 = 6.28125
        C2 = 2 * np.pi - 6.28125
        for ty, col in ((0, C_INVQ), (1, C_INVI)):
            ts_('dve', ang[:], posf[:], cst[:, col:col + 1], None, ALU.mult, None, ['posf', 'cst'], ['ang'])
            ts_('dve', tb[:], ang[:], float(1.0 / (2 * np.pi)), None, ALU.mult, None, ['ang'], ['tb'])
            cp('dve', posi[:], tb[:], ['tb'], ['posi'])
            cp('dve', tb[:], posi[:], ['posi'], ['tb'])
            stt_(ang[:], tb[:], -C1, ang[:], ALU.mult, ALU.add, ['tb', 'ang'], ['ang'])
            stt_(ang[:], tb[:], float(-C2), ang[:], ALU.mult, ALU.add, ['tb', 'ang'], ['ang'])
            for cs, shift in ((0, 0.5 * PI), (1, 0.0)):
                ts_('dve', tb[:], ang[:], shift, None, ALU.add, None, ['ang'], ['tb'])
                for (cmp_, thr_, adj_) in ((ALU.is_gt, PI, -2 * PI), (ALU.is_lt, -PI, 2 * PI), (ALU.is_gt, PI, -2 * PI), (ALU.is_lt, -PI, 2 * PI)):
                    ts_('dve', posf2[:], tb[:], thr_, adj_, cmp_, ALU.mult, ['tb'], ['posf2'])
                    tt_('dve', tb[:], tb[:], posf2[:], ALU.add, ['tb', 'posf2'], ['tb'])
                ts_('dve', tb[:], tb[:], PI, -PI, ALU.min, ALU.max, ['tb'], ['tb'])
                act(tb[:], tb[:], AF.Sin, ['tb'], ['tb'])
                dma('sp', tabs[2 * ty + cs, :, :], tb[:], 'tb', ['tb'], ['@tabs'])
        xf = [sb(st, "xf%d" % i, [128, 8, TT], F32) for i in range(2)]
        xh = [sb(st, "xh%d" % i, [128, 8, TT], BF16) for i in range(2)]
        for t in range(NT):
            s = t % 2
            dma('sp', xf[s][:], xT_in.rearrange("(c p) t -> p c t", p=128)[:, :, t * TT:(t + 1) * TT], 'xf%d' % s, [], ['xf%d' % s])
            act(xh[s][:], xf[s][:], AF.Copy, ['xf%d' % s], ['xh%d' % s])
            dma('sp', fm(xb, 0, 8, t * TT, (t + 1) * TT), xh[s][:], 'xh%d' % s, ['xh%d' % s], ['@xb'])
        tr.barrier()
    stage_end('S0')

    for l in range(L):
        if l > 0:
            tr.new_epoch('L%d' % l)
        res_src = (lambda c0, c1, t0, t1: xT_in.rearrange("(c p) t -> p c t", p=128)[:, c0:c1, t0:t1]) if l == 0 else \
                  (lambda c0, c1, t0, t1: fm(xres, c0, c1, t0, t1))

        with ExitStack() as st:
            win = sb(st, "win", [128, 8, INW], BF16)
            load_w_bf16(win, w_in[l], INW, 'win')
            xbt = [sb(st, "xbt%d" % i, [128, 8, TT], BF16) for i in range(2)]
            tbt = [sb(st, "tbt%d" % i, [128, 4, TT], F32) for i in range(2)]
            cbu_t = sb(st, "cbu_t", [128, 4, TT], F32)
            tmpcc = sb(st, "tmpcc", [128, 2, TT], F32)
            ab = sb(st, "ab", [128, TT], BF16)
            t1 = sb(st, "t1", [128, TT], F32)
            t2 = sb(st, "t2", [128, TT], F32)
            qst = sb(st, "qst", [128, 3, TT], BF16)
            iqst = sb(st, "iqst", [128, 3, TT], BF16)
            op('pool', lambda e: e.memset(iqst[:], 0.0), [], ['iqst'])
            cqst = sb(st, "cqst", [128, 3, TT], BF16)
            ckst = sb(st, "ckst", [128, 3, TT], BF16)
            kst = sb(st, "kst", [64, TT], BF16)
            ikst = sb(st, "ikst", [32, TT], BF16)
            vst = sb(st, "vst", [128, NB4, 64], BF16)
            iwst = sb(st, "iwst", [128, NB4, 8], F32)
            cvst = sb(st, "cvst", [128, NB4, 384], BF16)

            def load_tile(t):
                s = t % 2
                dma('sp', xbt[s][:], fm(xb, 0, 8, t * TT, (t + 1) * TT), 'xbt%d' % s, ['xb'], ['xbt%d' % s])
                dma('sp', tbt[s][:], fm(tabs, 0, 4, t * TT, (t + 1) * TT), 'tbt%d' % s, ['tabs'], ['tbt%d' % s])
            load_tile(0)
            bank_ctr = [0]
            rb_ctr = [0]
            for t in range(NT):
                s = t % 2
                if t + 1 < NT:
                    load_tile(t + 1)
                X = xbt[s]
                XR = 'xbt%d' % s
                TB = tbt[s]
                TBR = 'tbt%d' % s
                c0t, c1t = t * TT, (t + 1) * TT

                def proj(col0, M):
                    b = bank_ctr[0] % 4
                    bank_ctr[0] += 1
                    for kc in range(8):
                        mm(ps[0:M, b, 0:TT], win[:, kc, col0:col0 + M], X[:, kc, :], kc == 0, kc == 7, ['@win', XR], ['b%d' % b])
                    return b

                def rope(b, M, pm, pmname, tC, tS, dst, dstres):
                    rb = 4 + rb_ctr[0] % 2
                    rb_ctr[0] += 1
                    act(ab[0:M, :], ps[0:M, b, 0:TT], AF.Copy, ['b%d' % b], ['ab'])
                    KR = os.environ.get('KR2', '')
                    if 'm' not in KR:
                        mm(ps[0:M, rb, 0:TT], pm[0:M, 0:M], ab[0:M, :], True, True, ['ab', pmname], ['b%d' % rb])
                    if 'd' in KR:
                        return
                    if os.environ.get('KR4') == '1':
                        act(t2[0:M, :], ps[0:M, b, 0:TT], AF.Copy, ['b%d' % b], ['t2'])
                        tt_('dve', t1[0:M, :], t2[0:M, :], TB[0:M, tC, :], ALU.mult, ['t2', TBR], ['t1'])
                    elif os.environ.get('KR4') == '2':
                        cp('dve', t1[0:M, :], ps[0:M, b, 0:TT], ['b%d' % b, 'ab'], ['t1'])
                    elif os.environ.get('KR4') == '3':
                        cp('dve', t1[0:M, :], TB[0:M, tC, :], [TBR], ['t1'])
                    else:
                        tt_('dve', t1[0:M, :], ps[0:M, b, 0:TT], TB[0:M, tC, :], ALU.mult, ['b%d' % b, TBR], ['t1'])
                    if os.environ.get('KR3') == '1':
                        cp('dve', dst, t1[0:M, :], ['t1'], [dstres])
                        return
                    tt_('dve', t2[0:M, :], ps[0:M, rb, 0:TT], TB[0:M, tS, :], ALU.mult, ['b%d' % rb, TBR], ['t2'])
                    if os.environ.get('KR3') == '2':
                        cp('dve', dst, t2[0:M, :], ['t2'], [dstres])
                        return
                    tt_('pool' if os.environ.get('KROPE') is None else 'dve', dst, t1[0:M, :], t2[0:M, :], ALU.add, ['t1', 't2'], [dstres])

                SK = os.environ.get('KSKIP', '')
                for k in range(2 if 'c' not in SK else 0):
                    b = proj(k * 128, 128)
                    act(cbu_t[:, k, :], ps[:, b, 0:TT], AF.Copy, ['b%d' % b], ['cbu_t'])
                for k in range(2 if 'c' not in SK else 0):
                    b = proj(256 + k * 128, 128)
                    act(tmpcc[:, k, :], ps[:, b, 0:TT], AF.Copy, ['b%d' % b], ['tmpcc'])
                for k in range(2 if 'c' not in SK else 0):
                    b = proj(512 + k * 128, 128)
                    tt_('dve', cbu_t[:, 2 + k, :], ps[:, b, 0:TT], tmpcc[:, k, :], ALU.mult, ['b%d' % b, 'tmpcc'], ['cbu_t'])
                dma('sp', fm(cbu_d, 0, 4, c0t, c1t), cbu_t[:], 'cbu_t', ['cbu_t'], ['@cbu_d'])
                for k in range(2 if 'h' not in SK else 0):
                    dma('sp', hal[k * 128:(k + 1) * 128, t * NB4 * 2:(t + 1) * NB4 * 2].rearrange("p (b e) -> p b e", e=2),
                        cbu_t[:, 2 + k, :].rearrange("p (b e) -> p b e", e=128)[:, :, 126:128], 'cbu_t', ['cbu_t'], ['@hal'])
                for k in range(3 if 'q' not in SK else 0):
                    b = proj(768 + k * 128, 128)
                    rope(b, 128, pmq, 'pmq', 0, 1, qst[:, k, :], 'qst')
                dma('sp', fm(qT_d, 0, 3, c0t, c1t), qst[:], 'qst', ['qst'], ['@qT_d'])
                if 'k' not in SK:
                    b = proj(1152, 64)
                    rope(b, 64, pmq, 'pmq', 0, 1, kst[:], 'kst')
                    dma('sp', pay[t, 0:64, :], kst[:], 'kst', ['kst'], ['@pay'])
                for k in range(3 if 'i' not in SK else 0):
                    Mk = 96 if k < 2 else 64
                    b = proj(1280 + k * 96, Mk)
                    rope(b, Mk, pmi, 'pmi', 2, 3, iqst[0:Mk, k, :], 'iqst')
                dma('sp', fm(iqT_d, 0, 3, c0t, c1t), iqst[:], 'iqst', ['iqst'], ['@iqT_d'])
                if 'j' not in SK:
                    b = proj(1536, 32)
                    rope(b, 32, pmi, 'pmi', 2, 3, ikst[:], 'ikst')
                    dma('sp', pay[t, 64:96, :], ikst[:], 'ikst', ['ikst'], ['@pay'])
                for k in range(3 if 'a' not in SK else 0):
                    b = proj(1576 + k * 128, 128)
                    act(cqst[:, k, :], ps[:, b, 0:TT], AF.Copy, ['b%d' % b], ['cqst'])
                dma('sp', fm(cqT_d, 0, 3, c0t, c1t), cqst[:], 'cqst', ['cqst'], ['@cqT_d'])
                for k in range(3 if 'b' not in SK else 0):
                    b = proj(1960 + k * 128, 128)
                    act(ckst[:, k, :], ps[:, b, 0:TT], AF.Copy, ['b%d' % b], ['ckst'])
                dma('sp', pay[t, 160:544, :].rearrange("(k p) t -> p k t", p=128), ckst[:], 'ckst', ['ckst'], ['@pay'])
                for bl in range(NB4 if 't' not in SK else 0):
                    for (o0, col0, n) in ((0, 1216, 64), (64, 1568, 8), (72, 2344, 384)):
                        for kc in range(8):
                            mm(ps[:, 6, o0:o0 + n], X[:, kc, bl * 128:(bl + 1) * 128], win[:, kc, col0:col0 + n],
                               kc == 0, kc == 7, ['@win', XR], ['b6'])
                    act(vst[:, bl, :], ps[:, 6, 0:64], AF.Copy, ['b6'], ['vst'])
                    ts_('dve', iwst[:, bl, :], ps[:, 6, 64:72], float(8 ** -0.5), None, ALU.mult, None, ['b6'], ['iwst'])
                    act(cvst[:, bl, :], ps[:, 6, 72:456], AF.Copy, ['b6'], ['cvst'])
                if 's' in SK:
                    continue
                vflat = pay[t, 96:160, :].rearrange("a t -> (a t)")
                dma('sp', vflat.rearrange("(b p d) -> p b d", p=128, d=64), vst[:], 'vst', ['vst'], ['@pay'])
                dma('sp', iw_d[c0t:c1t, :].rearrange("(b p) d -> p b d", p=128), iwst[:], 'iwst', ['iwst'], ['@iw_d'])
                cvflat = pay[t, 544:928, :].rearrange("a t -> (a t)")
                dma('sp', cvflat.rearrange("(b p d) -> p b d", p=128, d=384), cvst[:], 'cvst', ['cvst'], ['@pay'])
            tr.barrier()
        stage_end('S1')

        groups = [[0, 1, 2, 3], [4, 5, 6, 7]]
        for t in range(NT):
            op('pool', lambda e: e.collective_compute("AllGather", ALU.bypass, replica_groups=groups, ins=[pay[t].opt()], outs=[gath[t].opt()]),
               ['pay'], ['gath%d' % t])
        op('pool', lambda e: e.collective_compute("AllGather", ALU.bypass, replica_groups=groups, ins=[hal.opt()], outs=[hgath.opt()]),
           ['hal'], ['hgath'])
        tr.barrier()
        stage_end('G')

        with ExitStack() as st:
            kT2 = sb(st, "kT2", [128, 4, TL], BF16)
            ikT4 = sb(st, "ikT4", [128, 4, TL], BF16)
            vx = sb(st, "vx", [128, 4, NI, 65], BF16)
            score = sb(st, "score", [128, NI * 512], F32)
            Mq = sb(st, "Mq", [128, NI * 512], BF16)
            rl = [sb(st, "rl%d" % i, [128, 3, 512], BF16) for i in range(2)]
            pT = [sb(st, "pT%d" % i, [128, 2, 384], BF16) for i in range(3)]
            qt = [sb(st, "qt%d" % i, [128, 3, 128], BF16) for i in range(2)]
            iqt = [sb(st, "iqt%d" % i, [128, 3, 128], BF16) for i in range(2)]
            iwt = [sb(st, "iwt%d" % i, [128, 8], F32) for i in range(2)]
            dg = sb(st, "dg", [128, 8, 128], BF16)
            lastp = sb(st, "lastp", [128, 512], F32)
            sm = sb(st, "sm", [128, 16], F32)
            halves = sb(st, "halves", [128, 32], F32)
            rden = sb(st, "rden", [128, 8], F32)
            ytm = sb(st, "ytm", [128, 384], BF16)
            yst = [sb(st, "yst%d" % i, [128, 3, 128], BF16) for i in range(2)]
            for r in range(4):
                base = r * PAYR
                for t in range(NT):
                    tsl = slice(t * TT, (t + 1) * TT)
                    for hh in range(2):
                        dma('sp', kT2[hh * 64:(hh + 1) * 64, r, tsl], gath[t, base:base + 64, :], 'kT2_%d' % hh, ['gath'], ['@kT2'])
                    for g in range(3):
                        dma('sp', ikT4[g * 32:(g + 1) * 32, r, tsl], gath[t, base + 64:base + 96, :], 'ikT4_%d' % g, ['gath'], ['@ikT4'])
                    dma('sp', vx[:, r, t * NB4:(t + 1) * NB4, 0:64],
                        gath[t, base + 96:base + 160, :].rearrange("a t -> (a t)").rearrange("(i p d) -> p i d", p=128, d=64),
                        'vx', ['gath'], ['@vx'])
            op('pool', lambda e: e.memset(vx[:, :, :, 64:65], 1.0), [], ['vx1'])

            def load_q(i):
                s = i % 2
                dma('sp', qt[s][:], fm(qT_d, 0, 3, i * 128, (i + 1) * 128), 'qt%d' % s, ['qT_d'], ['qt%d' % s])
                dma('sp', iqt[s][:], fm(iqT_d, 0, 3, i * 128, (i + 1) * 128), 'iqt%d' % s, ['iqT_d'], ['iqt%d' % s])
                dma('sp', iwt[s][:], iw_d[i * 128:(i + 1) * 128, :], 'iwt%d' % s, ['iw_d'], ['iwt%d' % s])
            load_q(0)
            ucnt = 0
            ktc = 0
            stres = lambda s2_: ['b%d' % bb for bb in (2 * s2_, 2 * s2_ + 1)]
            for i in range(NI):
                s = i % 2
                if i + 1 < NI:
                    load_q(i + 1)
                N = (i + 1) * 512
                QT, IQ, IW = qt[s], iqt[s], iwt[s]
                for h in range(8):
                    ts_('pool', dg[:, h, :], cst[:, C_ID:C_ID + 128], IW[:, h:h + 1], None, ALU.mult, None, ['cst', 'iwt%d' % s], ['dg'])
                units = [(m, c) for m in range(i + 1) for c in range(3)]
                ncg = (3, 3, 2)

                def logits(u, uc):
                    m, c = u
                    bs = 3 * (uc % 2)
                    for g in range(ncg[c]):
                        mm(ps[:, bs + g, :].rearrange("p (a b) -> p a b", b=128), IQ[g * 32:(g + 1) * 32, c, :],
                           ikT4[g * 32:(g + 1) * 32, :, m * 128:(m + 1) * 128],
                           True, True, ['iqt%d' % s, '@ikT4'], ['b%d' % (bs + g)])

                def relu_diag(u, uc):
                    m, c = u
                    rs = uc % 2
                    bs = 3 * (uc % 2)
                    ng = ncg[c]
                    act(rl[rs][:, 0:ng, :], ps[:, bs:bs + ng, :], AF.Relu, ['b%d' % (bs + g_) for g_ in range(ng)], ['rl%d' % rs])
                    sbk = 6 + m % 2
                    for g in range(ng):
                        mm(ps[:, sbk, :], dg[:, 3 * c + g, :], rl[rs][:, g, :],
                           c == 0 and g == 0, c == 2 and g == 1, ['dg', 'rl%d' % rs], ['b%d' % sbk])
                    if c == 2:
                        if m < i:
                            act(score[:, m * 512:(m + 1) * 512], ps[:, sbk, :], AF.Copy, ['b%d' % sbk], ['score'])
                        else:
                            madm = cst[:, C_MADM:C_MADM + 512]
                            tt_('dve', score[:, m * 512:(m + 1) * 512], ps[:, sbk, :], madm, ALU.add, ['b%d' % sbk, 'cst'], ['score'])
                            tt_('dve', lastp[:], ps[:, sbk, :], madm, ALU.subtract, ['b%d' % sbk, 'cst'], ['lastp'])
                logits(units[0], ucnt)
                for ui, u in enumerate(units):
                    if ui + 1 < len(units):
                        logits(units[ui + 1], ucnt + 1)
                    relu_diag(u, ucnt)
                    ucnt += 1
                op('dve', lambda e: e.tensor_reduce(out=sm[:, 0:1], in_=score[:, 0:N], axis=AX.X, op=ALU.max), ['score'], ['sm'])
                op('dve', lambda e: e.tensor_reduce(out=sm[:, 1:2], in_=lastp[:], axis=AX.X, op=ALU.min), ['lastp'], ['sm'])
                if i > 0:
                    op('dve', lambda e: e.tensor_reduce(out=sm[:, 2:3], in_=score[:, 0:N - 512], axis=AX.X, op=ALU.min), ['score'], ['sm'])
                    tt_('dve', sm[:, 1:2], sm[:, 1:2], sm[:, 2:3], ALU.min, ['sm'], ['sm'])
                tt_('dve', sm[:, 3:4], sm[:, 0:1], sm[:, 1:2], ALU.subtract, ['sm'], ['sm'])
                ts_('dve', sm[:, 3:4], sm[:, 3:4], 1.0001, 1e-6, ALU.mult, ALU.add, ['sm'], ['sm'])
                ts_('dve', halves[:, 0:NBIS], cst[:, C_POW:C_POW + NBIS], sm[:, 3:4], None, ALU.mult, None, ['sm', 'cst'], ['halves'])
                cp('dve', sm[:, 4:5], sm[:, 1:2], ['sm'], ['sm'])
                for r in range(NBIS):
                    tt_('dve', sm[:, 5:6], sm[:, 4:5], halves[:, r:r + 1], ALU.add, ['sm', 'halves'], ['sm'])
                    ts_('dve', Mq[:, 0:N], score[:, 0:N], sm[:, 5:6], 0.0, ALU.is_ge, ALU.add, ['score', 'sm'], ['Mq', 'sm'], accum=sm[:, 6:7])
                    ts_('dve', sm[:, 7:8], sm[:, 6:7], cst[:, C_KTAB + i:C_KTAB + i + 1], halves[:, r:r + 1], ALU.is_ge, ALU.mult,
                        ['sm', 'cst', 'halves'], ['sm'])
                    tt_('dve', sm[:, 4:5], sm[:, 4:5], sm[:, 7:8], ALU.add, ['sm'], ['sm'])
                ts_('dve', Mq[:, 0:N], score[:, 0:N], sm[:, 4:5], NEG, ALU.is_lt, ALU.mult, ['score', 'sm'], ['Mq'])
                nkt = 4 * (i + 1)
                for kt in range(nkt):
                    m, r = kt // 4, kt % 4
                    s2 = ktc % 2
                    p3 = ktc % 3
                    ktc += 1
                    for par in range(2):
                        mm(ps[:, 2 * s2 + par, 0:384].rearrange("p (a b) -> p a b", b=128), kT2[par * 64:(par + 1) * 64, r, m * 128:(m + 1) * 128],
                           QT[par * 64:(par + 1) * 64, :, :], True, False, ['@kT2', 'qt%d' % s], stres(s2))
                    for par in range(2):
                        mm(ps[:, 2 * s2 + par, 0:384].rearrange("p (a b) -> p a b", b=128), Mq[:, kt * 128:(kt + 1) * 128], ident3[:], False, True,
                           ['Mq', 'ident3'], stres(s2))
                    act(pT[p3][:], ps[:, 2 * s2:2 * s2 + 2, 0:384], AF.Exp, stres(s2), ['pT%d' % p3], scale=0.125)
                    for hs in range(6):
                        par, c = hs // 3, hs % 3
                        mm(ps[:, 4, hs * 65:(hs + 1) * 65], pT[p3][:, par, c * 128:(c + 1) * 128], vx[:, r, m, :],
                           kt == 0 and hs == 0, kt == nkt - 1 and hs == 5, ['pT%d' % p3, '@vx', 'vx1'], ['b4'])
                pvv = ps[:, 4, 0:390].rearrange("p (h e) -> p h e", e=65)
                op('dve', lambda e: e.reciprocal(out=rden[:, 0:6], in_=pvv[:, :, 64]), ['b4'], ['rden'])
                for hs in range(6):
                    par, c = hs // 3, hs % 3
                    head = 2 * c + par
                    ts_('dve', ytm[:, head * 64:(head + 1) * 64], ps[:, 4, hs * 65:hs * 65 + 64], rden[:, hs:hs + 1], None, ALU.mult, None,
                        ['b4', 'rden'], ['ytm'])
                for k in range(3):
                    op('pe', lambda e, k=k: e.transpose(out=psb[:, k * 128:(k + 1) * 128], in_=ytm[:, k * 128:(k + 1) * 128], identity=identb[:]),
                       ['ytm', 'identb'], ['b5'])
                act(yst[s][:], psb[:, 0:384].rearrange("p (k t) -> p k t", t=128), AF.Copy, ['b5'], ['yst%d' % s])
                dma('sp', fm(yT_d, 2, 5, i * 128, (i + 1) * 128), yst[s][:], 'yst%d' % s, ['yst%d' % s], ['@yT_d'])
            tr.barrier()
        stage_end('S2a')

        with ExitStack() as st:
            bd = sb(st, "bd", [128, 5, 6, 128], F32)
            bslot = sb(st, "bslot", [128, 8, 6, 128], F32)
            hc = sb(st, "hc", [128, 2, 4, NI + 1, 2], F32)
            hsel = sb(st, "hsel", [128, 2, NI, 2], F32)
            cqt = [sb(st, "cqt%d" % i, [128, 3, 128], BF16) for i in range(2)]
            ckt = [sb(st, "ckt%d" % i, [128, 3, 2, 4, 128], BF16) for i in range(2)]
            cvt = [sb(st, "cvt%d" % i, [128, 2, 4, 384], BF16) for i in range(2)]
            cbt = [sb(st, "cbt%d" % i, [128, 2, 128], F32) for i in range(2)]
            ut = [sb(st, "ut%d" % i, [128, 2, 130], F32) for i in range(2)]
            sbs = [sb(st, "sbs%d" % i, [128, 2, 384], F32) for i in range(2)]
            pTb = [sb(st, "pTb%d" % i, [128, 2, 384], BF16) for i in range(2)]
            rdb = sb(st, "rdb", [128, 8], F32)
            ytb = sb(st, "ytb", [128, 384], BF16)
            ystb = [sb(st, "ystb%d" % i, [128, 3, 128], BF16) for i in range(2)]
            acc = sb(st, "acc", [128, 2, 128], F32)
            yc = [sb(st, "yc%d" % i, [128, 2, 128], BF16) for i in range(2)]
            for dl in range(5):
                for par in range(2):
                    dma('sp', bd[:, dl, par * 3:(par + 1) * 3, :],
                        rel_bias[l, :, (dl * 6 + par * 3) * 128:(dl * 6 + par * 3 + 3) * 128].rearrange("p (h q) -> p h q", q=128),
                        'bd', [], ['bd%d%d' % (dl, par)])
            bdall = ['bd%d%d' % (dl, par) for dl in range(5) for par in range(2)]
            op('pool', lambda e: e.memset(bd[64:128, 0, :, 0:64], NEG), bdall, bdall)
            op('pool', lambda e: e.memset(bd[0:64, 4, :, 64:128], NEG), bdall, bdall)
            for sl in range(8):
                ts_('dve', bslot[:, sl, :, :], bd[:, 0, :, :], cst[:, C_SEL + sl * 5:C_SEL + sl * 5 + 1], cst[:, C_NEGS + sl:C_NEGS + sl + 1],
                    ALU.mult, ALU.add, bdall + ['cst'], ['bslot'])
                for dl in range(1, 5):
                    stt_(bslot[:, sl, :, :], bd[:, dl, :, :], cst[:, C_SEL + sl * 5 + dl:C_SEL + sl * 5 + dl + 1], bslot[:, sl, :, :],
                         ALU.mult, ALU.add, bdall + ['cst', 'bslot'], ['bslot'])
            op('pool', lambda e: e.memset(hc[:], 0.0), [], ['hc'])
            for r in range(4):
                for k in range(2):
                    dma('sp', hc[:, k, r, 1:NI + 1, :], hgath[r * 256 + k * 128:r * 256 + (k + 1) * 128, :].rearrange("p (b e) -> p b e", e=2),
                        'hc', ['hgath', 'hc'], ['hc%d%d' % (r, k)])
            hcall = ['hc%d%d' % (r, k) for r in range(4) for k in range(2)]
            first = True
            for r in range(4):
                for (colb, off) in ((C_SELA, 1), (C_SELB, 0)):
                    src = hc[:, :, r, off:off + NI, :]
                    if first:
                        ts_('dve', hsel[:], src, cst[:, colb + r:colb + r + 1], None, ALU.mult, None, hcall + ['cst'], ['hsel'])
                        first = False
                    else:
                        stt_(hsel[:], src, cst[:, colb + r:colb + r + 1], hsel[:], ALU.mult, ALU.add, hcall + ['cst', 'hsel'], ['hsel'])

            def load_b(i):
                s = i % 2
                dma('sp', cqt[s][:], fm(cqT_d, 0, 3, i * 128, (i + 1) * 128), 'cqt%d' % s, ['cqT_d'], ['cqt%d' % s])
                for mi in range(2):
                    blk = i - 1 + mi
                    if blk < 0:
                        continue
                    t_, bi = blk // NB4, blk % NB4
                    for r in range(4):
                        base = r * PAYR
                        dma('sp', ckt[s][:, :, mi, r, :],
                            gath[t_, base + 160:base + 544, bi * 128:(bi + 1) * 128].rearrange("(k p) t -> p k t", p=128),
                            'ckt%d' % s, ['gath'], ['@ckt%d' % s])
                        cvflat_r = gath[t_, base + 544:base + 928, :].rearrange("a t -> (a t)")
                        dma('sp', cvt[s][:, mi, r, :], cvflat_r[bi * 128 * 384:(bi + 1) * 128 * 384].rearrange("(p e) -> p e", e=384),
                            'cvt%d' % s, ['gath'], ['@cvt%d' % s])
                dma('sp', cbt[s][:], fm(cbu_d, 0, 2, i * 128, (i + 1) * 128), 'cbt%d' % s, ['cbu_d'], ['cbt%d' % s])
                dma('sp', ut[s][:, :, 2:130], fm(cbu_d, 2, 4, i * 128, (i + 1) * 128), 'ut%d' % s, ['cbu_d'], ['ut%d' % s])
            load_b(0)
            slc = 0
            for i in range(NI):
                s = i % 2
                if i + 1 < NI:
                    load_b(i + 1)
                slots = [(mi, r) for mi in range(2) for r in range(4) if not (i == 0 and mi == 0)]
                for si, (mi, r) in enumerate(slots):
                    sl = mi * 4 + r
                    s2 = slc % 2
                    slc += 1
                    for k in range(3):
                        for par in range(2):
                            mm(ps[:, 2 * s2 + par, k * 128:(k + 1) * 128], ckt[s][par * 64:(par + 1) * 64, k, mi, r, :],
                               cqt[s][par * 64:(par + 1) * 64, k, :], True, True, ['@ckt%d' % s, 'cqt%d' % s], ['b%d' % (2 * s2 + par)])
                    stt_(sbs[s2][:], ps[:, 2 * s2:2 * s2 + 2, 0:384], 0.125,
                         bslot[:, sl, :, :].rearrange("p (a k) q -> p a (k q)", a=2), ALU.mult, ALU.add,
                         ['b%d' % (2 * s2), 'b%d' % (2 * s2 + 1), 'bslot'], ['sbs%d' % s2])
                    act(pTb[s2][:], sbs[s2][:], AF.Exp, ['sbs%d' % s2], ['pTb%d' % s2])
                    for hs in range(6):
                        par, k = hs // 3, hs % 3
                        head = 2 * k + par
                        mm(ps[:, 4, head * 64:(head + 1) * 64], pTb[s2][:, par, k * 128:(k + 1) * 128], cvt[s][:, mi, r, head * 64:(head + 1) * 64],
                           si == 0 and hs == 0, False, ['pTb%d' % s2, '@cvt%d' % s], ['b4'])
                        mm(ps[:, 4, 384 + head:385 + head], pTb[s2][:, par, k * 128:(k + 1) * 128], onesb[:, 0:1],
                           False, si == len(slots) - 1 and hs == 5, ['pTb%d' % s2, 'onesb'], ['b4'])
                op('dve', lambda e: e.reciprocal(out=rdb[:, 0:6], in_=ps[:, 4, 384:390]), ['b4'], ['rdb'])
                tt_('dve', ytb[:].rearrange("p (h e) -> p h e", e=64), ps[:, 4, 0:384].rearrange("p (h e) -> p h e", e=64),
                    rdb[:, 0:6].unsqueeze(2).broadcast_to([128, 6, 64]), ALU.mult, ['b4', 'rdb'], ['ytb'])
                for k in range(3):
                    op('pe', lambda e, k=k: e.transpose(out=psb[:, k * 128:(k + 1) * 128], in_=ytb[:, k * 128:(k + 1) * 128], identity=identb[:]),
                       ['ytb', 'identb'], ['b5'])
                act(ystb[s][:], psb[:, 0:384].rearrange("p (k t) -> p k t", t=128), AF.Copy, ['b5'], ['ystb%d' % s])
                dma('sp', fm(yT_d, 5, 8, i * 128, (i + 1) * 128), ystb[s][:], 'ystb%d' % s, ['ystb%d' % s], ['@yT_d'])
                cp('pool', ut[s][:, :, 0:2], hsel[:, :, i, :], ['hsel', 'ut%d' % s], ['ut%d' % s])
                for k in range(2):
                    w = lambda jj: cw[:, l * 3 + jj, k:k + 1]
                    ts_('dve', acc[:, k, :], ut[s][:, k, 0:128], w(0), None, ALU.mult, None, ['ut%d' % s, 'cw'], ['acc'])
                    stt_(acc[:, k, :], ut[s][:, k, 1:129], w(1), acc[:, k, :], ALU.mult, ALU.add, ['ut%d' % s, 'cw', 'acc'], ['acc'])
                    stt_(acc[:, k, :], ut[s][:, k, 2:130], w(2), acc[:, k, :], ALU.mult, ALU.add, ['ut%d' % s, 'cw', 'acc'], ['acc'])
                tt_('dve', yc[s][:], acc[:], cbt[s][:], ALU.mult, ['acc', 'cbt%d' % s], ['yc%d' % s])
                dma('sp', fm(yT_d, 0, 2, i * 128, (i + 1) * 128), yc[s][:], 'yc%d' % s, ['yc%d' % s], ['@yT_d'])
            tr.barrier()
        stage_end('S2b')

        with ExitStack() as st:
            wo = sb(st, "wo", [128, 8, D], BF16)
            load_w_bf16(wo, w_mix_out[l], D, 'wo')
            yt = [sb(st, "yt%d" % i, [128, 8, TT], BF16) for i in range(2)]
            xr = [sb(st, "xr%d" % i, [128, 8, TT], F32) for i in range(2)]
            tbuf = sb(st, "tbuf", [128, 8, TT], F32)
            lb = ln_bufs(st)

            def load3(t):
                s = t % 2
                dma('sp', yt[s][:], fm(yT_d, 0, 8, t * TT, (t + 1) * TT), 'yt%d' % s, ['yT_d'], ['yt%d' % s])
                dma('sp', xr[s][:], res_src(0, 8, t * TT, (t + 1) * TT), 'xr%d' % s, ['xres'], ['xr%d' % s])
            load3(0)
            for t in range(NT):
                s = t % 2
                if t + 1 < NT:
                    load3(t + 1)
                for oc in range(8):
                    b = oc % 2
                    for kc in range(8):
                        mm(ps[:, b, 0:TT], wo[:, kc, oc * 128:(oc + 1) * 128], yt[s][:, kc, :], kc == 0, kc == 7, ['@wo', 'yt%d' % s], ['b%d' % b])
                    stt_(tbuf[:, oc, :], xr[s][:, oc, :], ALPHA, ps[:, b, 0:TT], ALU.mult, ALU.add, ['xr%d' % s, 'b%d' % b], ['t'])
                layer_norm(lb, tbuf, l, 0, t * TT, False, None)
            tr.barrier()
        stage_end('S3')

        st4 = ExitStack()
        if True:
            st = st4
            kmT = sb(st, "kmT", [128, 8, 256], BF16)
            vm = sb(st, "vm", [128, 2, D], BF16)
            if True:
                st2 = st
                wkv = sb(st2, "wkv", [128, 8, 2 * D], BF16)
                memf = sb(st2, "memf", [128, 8, 256], F32)
                memb = sb(st2, "memb", [128, 8, 256], BF16)
                load_w_bf16(wkv, w_mkv[l], 2 * D, 'wkv')
                dma('sp', memf[:], memT_in.rearrange("(c p) m -> p c m", p=128), 'memf', [], ['memf'])
                cp('dve', memb[:], memf[:], ['memf'], ['memb'])
                for ec in range(8):
                    b = ec % 2
                    for kc in range(8):
                        mm(ps[:, b, 0:256], wkv[:, kc, ec * 128:(ec + 1) * 128], memb[:, kc, :], kc == 0, kc == 7, ['@wkv', 'memb'], ['b%d' % b])
                    act(kmT[:, ec, :], ps[:, b, 0:256], AF.Copy, ['b%d' % b], ['kmT'])
                for mt in range(2):
                    for eh in range(2):
                        b = 2 + eh
                        for kc in range(8):
                            mm(ps[:, b, :], memb[:, kc, mt * 128:(mt + 1) * 128], wkv[:, kc, D + eh * 512:D + (eh + 1) * 512],
                               kc == 0, kc == 7, ['@wkv', 'memb'], ['b%d' % b])
                        act(vm[:, mt, eh * 512:(eh + 1) * 512], ps[:, b, :], AF.Copy, ['b%d' % b], ['vm'])
                tr.barrier()
        if True:
            st = st4
            wq = sb(st, "wq", [128, 8, D], BF16)
            wmo = sb(st, "wmo", [128, 8, D], BF16)
            load_w_bf16(wq, w_mq[l], D, 'wq')
            load_w_bf16(wmo, w_mo[l], D, 'wmo')
            xbt4 = [sb(st, "xbt4_%d" % i, [128, 8, TT], BF16) for i in range(2)]
            xr = [sb(st, "xr4_%d" % i, [128, 8, TT], F32) for i in range(2)]
            qm = sb(st, "qm", [128, 8, TT], BF16)
            pTm = [sb(st, "pTm%d" % i, [128, 2, TT], BF16) for i in range(2)]
            rdm = sb(st, "rdm", [128, TT], F32)
            om = sb(st, "om", [128, 8, TT], BF16)
            tbuf = sb(st, "tbuf4", [128, 8, TT], F32)
            lb = ln_bufs(st)

            def load4(t):
                s = t % 2
                dma('sp', xbt4[s][:], fm(xb, 0, 8, t * TT, (t + 1) * TT), 'xbt4%d' % s, ['xb'], ['xbt4%d' % s])
                dma('sp', xr[s][:], fm(xres, 0, 8, t * TT, (t + 1) * TT), 'xr4%d' % s, ['xres'], ['xr4%d' % s])
            load4(0)
            hc4 = 0
            for t in range(NT):
                s = t % 2
                for ec in range(8):
                    b = ec % 2
                    for kc in range(8):
                        mm(ps[:, b, 0:TT], wq[:, kc, ec * 128:(ec + 1) * 128], xbt4[s][:, kc, :], kc == 0, kc == 7, ['@wq', 'xbt4%d' % s], ['b%d' % b])
                    act(qm[:, ec, :], ps[:, b, 0:TT], AF.Copy, ['b%d' % b], ['qm'])
                for h in range(4):
                    hs_ = hc4 % 2
                    hc4 += 1
                    for mt in range(2):
                        for dc in range(2):
                            mm(ps[:, 2 + mt, 0:TT], kmT[:, 2 * h + dc, mt * 128:(mt + 1) * 128], qm[:, 2 * h + dc, :], dc == 0, dc == 1,
                               ['kmT', 'qm'], ['b%d' % (2 + mt)])
                    act(pTm[hs_][:], ps[:, 2:4, 0:TT], AF.Exp, ['b2', 'b3'], ['pTm%d' % hs_], scale=1.0 / 16.0)
                    for mt in range(2):
                        mm(ps[:, 6, 0:TT], onesb[:], pTm[hs_][:, mt, :], mt == 0, mt == 1, ['onesb', 'pTm%d' % hs_], ['b6'])
                    op('dve', lambda e: e.reciprocal(out=rdm[:], in_=ps[:, 6, 0:TT]), ['b6'], ['rdm'])
                    for dc in range(2):
                        b = 4 + dc
                        for mt in range(2):
                            mm(ps[:, b, 0:TT], vm[:, mt, h * 256 + dc * 128:h * 256 + (dc + 1) * 128], pTm[hs_][:, mt, :], mt == 0, mt == 1,
                               ['vm', 'pTm%d' % hs_], ['b%d' % b])
                        tt_('dve', om[:, 2 * h + dc, :], ps[:, b, 0:TT], rdm[:], ALU.mult, ['b%d' % b, 'rdm'], ['om'])
                if t + 1 < NT:
                    load4(t + 1)
                for oc in range(8):
                    b = oc % 2
                    for kc in range(8):
                        mm(ps[:, b, 0:TT], wmo[:, kc, oc * 128:(oc + 1) * 128], om[:, kc, :], kc == 0, kc == 7, ['@wmo', 'om'], ['b%d' % b])
                    stt_(tbuf[:, oc, :], xr[s][:, oc, :], ALPHA, ps[:, b, 0:TT], ALU.mult, ALU.add, ['xr4%d' % s, 'b%d' % b], ['t'])
                layer_norm(lb, tbuf, l, 1, t * TT, False, None)
            tr.barrier()
            st4.close()
        stage_end('S4')

        with ExitStack() as st:
            xba = sb(st, "xba", [128, 8, TL], BF16)
            dma('sp', xba[:], fm(xb, 0, 8, 0, TL), 'xba', ['xb'], ['xba'])
            w1 = [sb(st, "w1_%d" % i, [128, 8, 128], BF16) for i in range(4)]
            rr = [sb(st, "rr%d" % i, [128, TT], F32) for i in range(2)]
            hst = [sb(st, "hst%d" % i, [128, TT], BF16) for i in range(4)]

            def loadw1(f):
                s = f % 4
                dma('pool', w1[s][:], w_ff1[l][:, f * 128:(f + 1) * 128].rearrange("(kc p) n -> p kc n", p=128), 'w1_%d' % s, [], ['w1_%d' % s])
            for f in range(3):
                loadw1(f)
            ct = 0
            for f in range(32):
                if f + 3 < 32:
                    loadw1(f + 3)
                s = f % 4
                for t in range(NT):
                    b = ct % 4
                    r2 = ct % 2
                    h4 = ct % 4
                    ct += 1
                    for kc in range(8):
                        mm(ps[:, b, 0:TT], w1[s][:, kc, :], xba[:, kc, t * TT:(t + 1) * TT], kc == 0, kc == 7, ['w1_%d' % s, 'xba'], ['b%d' % b])
                    act(rr[r2][:], ps[:, b, 0:TT], AF.Relu, ['b%d' % b], ['rr%d' % r2])
                    tt_('pool', hst[h4][:], rr[r2][:], rr[r2][:], ALU.mult, ['rr%d' % r2], ['hst%d' % h4])
                    dma('sp', hT_d[f, :, t * TT:(t + 1) * TT], hst[h4][:], 'hst%d' % h4, ['hst%d' % h4], ['@hT_d'])
            tr.barrier()
        stage_end('S5a')

        with ExitStack() as st:
            w2 = sb(st, "w2", [128, 32, D], BF16)
            load_w_bf16(w2, w_ff2[l], D, 'w2', nkc=32)
            ht = [sb(st, "ht%d" % i, [128, 32, TT], BF16) for i in range(2)]
            xr = [sb(st, "xr5_%d" % i, [128, 8, TT], F32) for i in range(1)]
            xr = [xr[0], xr[0]]
            tbuf = sb(st, "tbuf5", [128, 8, TT], F32)
            lb = ln_bufs(st)

            def load5(t):
                s = t % 2
                dma('sp', ht[s][:], fm(hT_d, 0, 32, t * TT, (t + 1) * TT), 'ht%d' % s, ['hT_d'], ['ht%d' % s])

            def load5x(t):
                dma('sp', xr[0][:], fm(xres, 0, 8, t * TT, (t + 1) * TT), 'xr50', ['xres'], ['xr50'])
            load5(0)
            load5x(0)
            for t in range(NT):
                s = t % 2
                if t + 1 < NT:
                    load5(t + 1)
                for oc in range(8):
                    b = oc % 2
                    for kc in range(32):
                        mm(ps[:, b, 0:TT], w2[:, kc, oc * 128:(oc + 1) * 128], ht[s][:, kc, :], kc == 0, kc == 31, ['@w2', 'ht%d' % s], ['b%d' % b])
                    stt_(tbuf[:, oc, :], xr[s][:, oc, :], ALPHA, ps[:, b, 0:TT], ALU.mult, ALU.add, ['xr50', 'b%d' % b], ['t'])
                if t + 1 < NT:
                    load5x(t + 1)
                layer_norm(lb, tbuf, l, 2, t * TT, l == L - 1, None)
            tr.barrier()

    tr.barrier()
    es.close()
    return nc


def make_consts(S, j):
    TL = S // 4
    NI = TL // 128
    NCST = C_KTAB + NI
    c = np.zeros((128, NCST), np.float32)
    c[:, C_ID:C_ID + 128] = np.eye(128, dtype=np.float32)
    pmq = np.zeros((128, 128), np.float32)
    for hb in (0, 64):
        for d in range(8):
            pmq[hb + d + 8, hb + d] = -1.0
            pmq[hb + d, hb + d + 8] = 1.0
    pmi = np.zeros((128, 128), np.float32)
    for hb in (0, 32, 64, 96):
        for d in range(4):
            pmi[hb + d + 4, hb + d] = -1.0
            pmi[hb + d, hb + d + 4] = 1.0
    c[:, C_PMQ:C_PMQ + 128] = pmq
    c[:, C_PMI:C_PMI + 128] = pmi
    fq = np.power(np.float32(500000.0), -np.arange(8, dtype=np.float32) / np.float32(8))
    fi = np.power(np.float32(500000.0), -np.arange(4, dtype=np.float32) / np.float32(4))
    for p in range(128):
        d = p % 64
        c[p, C_INVQ] = fq[d % 8] if d < 16 else 0.0
        d = p % 32
        c[p, C_INVI] = fi[d % 4] if d < 8 else 0.0
    for mi in range(2):
        for r in range(4):
            sl = mi * 4 + r
            dl = (j - r) if mi == 1 else (4 + j - r)
            if 0 <= dl <= 4:
                c[:, C_SEL + sl * 5 + dl] = 1.0
            else:
                c[:, C_NEGS + sl] = NEG
    if j > 0:
        c[:, C_SELA + j - 1] = 1.0
    else:
        c[:, C_SELB + 3] = 1.0
    for r in range(32):
        c[:, C_POW + r] = 2.0 ** -(r + 1)
    madm = np.zeros((128, 4, 128), np.float32)
    for r in range(4):
        if r > j:
            madm[:, r, :] = NEG
        elif r == j:
            madm[0:64, r, 64:128] = NEG
    c[:, C_MADM:C_MADM + 512] = madm.reshape(128, 512)
    for i in range(NI):
        g = 4 * i + j
        c[0:64, C_KTAB + i] = min(256, (2 * g + 1) * 64)
        c[64:128, C_KTAB + i] = min(256, (2 * g + 2) * 64)
    return c


def run_module(inputs, S, L):
    x = np.asarray(inputs['x'], np.float32)
    B = x.shape[0]
    TL = S // 4
    NBLK = S // 128
    in_maps = []
    lnp = np.stack([np.asarray(inputs[k], np.float32)[:L] for k in ('ln1_g', 'ln1_b', 'ln2_g', 'ln2_b', 'ln3_g', 'ln3_b')], axis=1)
    for c in range(8):
        b, j = c // 4, c % 4
        xs = x[b].reshape(NBLK, 128, D)[j::4].reshape(TL, D)
        ps_ = np.asarray(inputs['positions'])[b].reshape(NBLK, 128)[j::4].reshape(1, TL).astype(np.int32)
        m = {
            'xT': np.ascontiguousarray(xs.T),
            'pos': np.ascontiguousarray(ps_),
            'cst': make_consts(S, j),
            'memT': np.ascontiguousarray(np.asarray(inputs['mem'], np.float32)[b].T),
            'lnp': np.ascontiguousarray(lnp.reshape(L * 6, 8, 128).transpose(2, 0, 1)),
            'conv_w': np.ascontiguousarray(np.asarray(inputs['conv_w'], np.float32)[:L].reshape(L * 3, 2, 128).transpose(2, 0, 1)),
        }
        for k in ('w_in', 'w_mix_out', 'w_mq', 'w_mkv', 'w_mo', 'w_ff1', 'w_ff2'):
            m[k] = np.ascontiguousarray(np.asarray(inputs[k], np.float32)[:L])
        in_maps.append(m)
    rb = np.asarray(inputs['rel_bias'], np.float32)[:L]
    e = np.arange(767)
    ext = rb[:, :, np.clip(e - 127, -63, 128) + 63]
    s_ = np.arange(128)[:, None, None, None]
    dl_ = np.arange(5)[None, :, None, None]
    hs_ = np.arange(6)[None, None, :, None]
    q_ = np.arange(128)[None, None, None, :]
    head = 2 * (hs_ % 3) + hs_ // 3
    idx = 128 * dl_ + 127 + q_ - s_
    relb = np.ascontiguousarray(ext[:, head, idx].reshape(L, 128, 5 * 6 * 128))
    for m in in_maps:
        m['relb'] = relb
    nc = build_program(S, L)
    res = run_bass_kernel_spmd(nc, in_maps, core_ids=list(range(8)))
    out = np.zeros((B, NBLK, 128, D), np.float32)
    for c in range(8):
        b, j = c // 4, c % 4
        o = np.asarray(res.results[c]['outT'], np.float32).T.reshape(TL // 128, 128, D)
        out[b, j::4] = o
    return out.reshape(B, S, D)


def kernel(**inputs):
    return run_module(inputs, 16384, 4)
```

```python
import numpy as np
from contextlib import ExitStack
import concourse.bass as bass
import concourse.mybir as mybir
from concourse.bass_utils import run_bass_kernel_spmd

F32 = mybir.dt.float32
BF16 = mybir.dt.bfloat16
I32 = mybir.dt.int32
AF = mybir.ActivationFunctionType
ALU = mybir.AluOpType
AX = mybir.AxisListType

D = 1024
INW = 2728
DFF = 4096
NEG = -30000.0
ALPHA_L = lambda L: (2.0 * L) ** 0.25
LN_EPS = 1e-5
PI = float(np.pi)
NBIS = 20
SAME_ENG_SYNC = True
PAYR = 928

C_ID, C_PMQ, C_PMI, C_INVQ, C_INVI, C_SEL, C_NEGS, C_SELA, C_SELB, C_POW, C_MADM = 0, 128, 256, 384, 385, 386, 426, 434, 438, 442, 474
C_KTAB = C_MADM + 512


class Tr:
    def __init__(self, nc, es):
        self.nc = nc
        self.es = es
        self.E = {'pe': nc.tensor, 'dve': nc.vector, 'act': nc.scalar, 'pool': nc.gpsimd, 'sp': nc.sync}
        self.sem = {k: es.enter_context(nc.semaphore('s_' + k)) for k in self.E}
        self.cnt = {k: 0 for k in self.E}
        self.waited = {k: {} for k in self.E}
        self.W = {}
        self.R = {}
        self.dpool = []
        self.dmap = {}
        self.B1 = es.enter_context(nc.semaphore('bar1'))
        self.B2 = es.enter_context(nc.semaphore('bar2'))
        self.bk = 0
        self.nins = 0
        self.groups = {}
        self.gctr = 0

    def _dkey(self, key):
        if key not in self.dmap:
            idx = len(self.dmap)
            if idx >= len(self.dpool):
                self.dpool.append([self.es.enter_context(self.nc.semaphore('d_%d' % idx)), 0, idx])
            self.dmap[key] = idx
        return self.dpool[self.dmap[key]]

    def op(self, eng, fn, reads=(), writes=(), dma=None):
        deps = {}

        def merge(d):
            for k, sv in d.items():
                if k not in deps or deps[k][1] < sv[1]:
                    deps[k] = sv
        rl_ = []
        for r in reads:
            if r.startswith('@'):
                rl_.extend(self.groups.get(r, []))
            else:
                rl_.append(r)
        wl_ = []
        for w in writes:
            if w.startswith('@'):
                g = self.groups.setdefault(w, [])
                had_readers = False
                for sname in g:
                    if self.R.get(sname):
                        had_readers = True
                        merge(self.R[sname])
                if had_readers:
                    for sname in g:
                        merge(self.W.get(sname, {}))
                        self.W.pop(sname, None)
                        self.R.pop(sname, None)
                    del g[:]
                self.gctr += 1
                sname = '%s#%d' % (w, self.gctr)
                g.append(sname)
                wl_.append(sname)
            else:
                wl_.append(w)
        bk = [x for x in rl_ + wl_ if len(x) == 2 and x[0] == 'b' and x[1].isdigit()]
        rl_ = [x for x in rl_ if x not in bk]
        wl_ = [x for x in wl_ if x not in bk] + sorted(set(bk))
        reads, writes = rl_, wl_
        for r in reads:
            merge(self.W.get(r, {}))
        for w in writes:
            merge(self.W.get(w, {}))
            merge(self.R.get(w, {}))
        e = self.E[eng]
        wd = self.waited[eng]
        for k, (s, v) in deps.items():
            if k == eng and (eng == 'pe' or not SAME_ENG_SYNC):
                continue
            if wd.get(k, 0) >= v:
                continue
            e.wait_ge(s, v)
            wd[k] = v
        ins = fn(e)
        self.nins += 1
        if dma is not None:
            d = self._dkey(dma)
            d[1] += 16
            ins.then_inc(d[0], 16)
            ev = ('D%d' % d[2], (d[0], d[1]))
        else:
            self.cnt[eng] += 1
            ins.then_inc(self.sem[eng], 1)
            ev = (eng, (self.sem[eng], self.cnt[eng]))
        for w in writes:
            self.W[w] = {ev[0]: ev[1]}
            self.R[w] = {}
        for r in reads:
            if r not in writes:
                self.R.setdefault(r, {})[ev[0]] = ev[1]

    def new_epoch(self, tag):
        for k in self.E:
            self.sem[k] = self.es.enter_context(self.nc.semaphore('s_%s_%s' % (k, tag)))
            self.cnt[k] = 0
            for wd in self.waited.values():
                wd.pop(k, None)

    def barrier(self):
        for eng, e in self.E.items():
            wd = self.waited[eng]
            for k in self.E:
                if k == eng:
                    continue
                v = self.cnt[k]
                if v > wd.get(k, 0):
                    e.wait_ge(self.sem[k], v)
                    wd[k] = v
            for (s, v, idx) in self.dpool:
                k = 'D%d' % idx
                if v > wd.get(k, 0):
                    e.wait_ge(s, v)
                    wd[k] = v
        self.W.clear()
        self.R.clear()
        self.groups.clear()
        self.dmap.clear()


import os


class _Stop(Exception):
    pass


def build_program(S, L, debug=()):
    try:
        return _build_program(S, L, debug)
    except _Stop as e:
        return e.args[0]


def _build_program(S, L, debug=()):
    TL = S // 4
    NI = TL // 128
    TT = 512 if TL >= 512 else TL
    NT = TL // TT
    NB4 = TT // 128
    ALPHA = float(8.0 ** 0.25)
    NCST = C_KTAB + NI

    nc = bass.Bass("TRN2", target_bir_lowering=False)
    es = ExitStack()
    tr = Tr(nc, es)

    def din(name, shape, dt=F32):
        return nc.dram_tensor(name, list(shape), dt, kind="ExternalInput").ap()

    def dscr(name, shape, dt):
        return nc.dram_tensor(name, list(shape), dt).ap()

    xT_in = din("xT", [D, TL])
    pos_in = din("pos", [1, TL], I32)
    cst_in = din("cst", [128, NCST])
    memT_in = din("memT", [D, 256])
    w_in = din("w_in", [L, D, INW])
    conv_w = din("conv_w", [128, L * 3, 2])
    rel_bias = din("relb", [L, 128, 5 * 6 * 128])
    w_mix_out = din("w_mix_out", [L, D, D])
    lnp_in = din("lnp", [128, L * 6, 8])
    w_mq = din("w_mq", [L, D, D])
    w_mkv = din("w_mkv", [L, D, 2 * D])
    w_mo = din("w_mo", [L, D, D])
    w_ff1 = din("w_ff1", [L, D, DFF])
    w_ff2 = din("w_ff2", [L, DFF, D])
    outT = nc.dram_tensor("outT", [D, TL], F32, kind="ExternalOutput").ap()

    xres = dscr("xres", [8, 128, TL], F32)
    xb = dscr("xb", [8, 128, TL], BF16)
    tabs = dscr("tabs", [4, 128, TL], F32)
    qT_d = dscr("qT_d", [3, 128, TL], BF16)
    iqT_d = dscr("iqT_d", [3, 128, TL], BF16)
    iw_d = dscr("iw_d", [TL, 8], F32)
    cqT_d = dscr("cqT_d", [3, 128, TL], BF16)
    cbu_d = dscr("cbu_d", [4, 128, TL], F32)
    pay = dscr("pay", [NT, PAYR, TT], BF16)
    gath = dscr("gath", [NT, 4 * PAYR, TT], BF16)
    hal = dscr("hal", [256, NI * 2], F32)
    hgath = dscr("hgath", [4 * 256, NI * 2], F32)
    yT_d = dscr("yT_d", [8, 128, TL], BF16)
    hT_d = dscr("hT_d", [32, 128, TL], BF16)
    ext_d = dscr("ext_d", [6, 768], F32)
    dbg_out = {}
    for nm in debug:
        pass

    stage_ctr = [0]

    def stage_end(name):
        stage_ctr[0] += 1
        if os.environ.get('KSTOP') == name:
            tr.barrier()
            es.close()
            raise _Stop(nc)

    def fm(ap3, c0, c1, t0, t1):
        return ap3[c0:c1, :, t0:t1].rearrange("c p t -> p c t")

    uid = [0]

    def sb(st, name, shape, dt):
        uid[0] += 1
        return st.enter_context(nc.sbuf_tensor("s%d_%s" % (uid[0], name), list(shape), dt))

    op = tr.op

    def dma(q, out, in_, key, reads=(), writes=()):
        op(q, lambda e: e.dma_start(out=out, in_=in_), reads, writes, dma=key)

    def mm(out, lhsT, rhs, start, stop, reads, writes):
        op('pe', lambda e: e.matmul(out, lhsT, rhs, start=start, stop=stop, skip_group_check=True), reads, writes)

    def act(out, in_, func, reads, writes, scale=None):
        if scale is None:
            op('act', lambda e: e.activation(out=out, in_=in_, func=func), reads, writes)
        else:
            op('act', lambda e: e.activation(out=out, in_=in_, func=func, scale=scale), reads, writes)

    def tt_(eng, out, in0, in1, o, reads, writes):
        op(eng, lambda e: e.tensor_tensor(out=out, in0=in0, in1=in1, op=o), reads, writes)

    def ts_(eng, out, in0, s1, s2, o0, o1, reads, writes, accum=None):
        if o1 is None:
            op(eng, lambda e: e.tensor_scalar(out=out, in0=in0, scalar1=s1, scalar2=None, op0=o0), reads, writes)
        elif accum is None:
            op(eng, lambda e: e.tensor_scalar(out=out, in0=in0, scalar1=s1, scalar2=s2, op0=o0, op1=o1), reads, writes)
        else:
            op(eng, lambda e: e.tensor_scalar(out=out, in0=in0, scalar1=s1, scalar2=s2, op0=o0, op1=o1, accum_out=accum), reads, writes)

    def stt_(out, in0, sc, in1, o0, o1, reads, writes):
        op('dve', lambda e: e.scalar_tensor_tensor(out=out, in0=in0, scalar=sc, in1=in1, op0=o0, op1=o1), reads, writes)

    def cp(eng, out, in_, reads, writes):
        op(eng, lambda e: e.tensor_copy(out=out, in_=in_), reads, writes)

    ps = es.enter_context(nc.psum_tensor("ps", [128, 8, 512], F32))
    psb = ps[:, 5, :].bitcast(BF16)
    cst = sb(es, "cst", [128, NCST], F32)
    identb = sb(es, "identb", [128, 128], BF16)
    pmq = sb(es, "pmq", [128, 128], BF16)
    pmi = sb(es, "pmi", [128, 128], BF16)
    ident3 = sb(es, "ident3", [128, 3, 128], BF16)
    onesf = sb(es, "onesf", [128, 128], F32)
    onesb = sb(es, "onesb", [128, 128], BF16)
    lnp = sb(es, "lnp", [128, L * 6, 8], F32)
    cw = sb(es, "cw", [128, L * 3, 2], F32)

    dma('sp', cst[:], cst_in[:, :], 'cst', writes=['cst'])
    dma('sp', lnp[:], lnp_in[:, :, :], 'lnp', writes=['lnp'])
    dma('sp', cw[:], conv_w[:, :, :], 'cw', writes=['cw'])
    cp('dve', identb[:], cst[:, C_ID:C_ID + 128], ['cst'], ['identb'])
    cp('dve', pmq[:], cst[:, C_PMQ:C_PMQ + 128], ['cst'], ['pmq'])
    cp('dve', pmi[:], cst[:, C_PMI:C_PMI + 128], ['cst'], ['pmi'])
    for k in range(3):
        cp('dve', ident3[:, k, :], cst[:, C_ID:C_ID + 128], ['cst'], ['ident3'])
    op('pool', lambda e: e.memset(onesf[:], 1.0 / D), [], ['onesf'])
    op('pool', lambda e: e.memset(onesb[:], 1.0), [], ['onesb'])
    tr.barrier()

    def layer_norm(st_bufs, t, l, which, t0, final, tagres):
        sq, mean_sb, m2, rstd, xnb = st_bufs
        xn = t
        W = TT
        gi = l * 6 + which * 2
        act(sq[:], t[:], AF.Square, ['t'], ['sq'])
        for c in range(8):
            mm(ps[:, 2, 0:W], onesf[:], t[:, c, :], c == 0, c == 7, ['t', 'onesf'], ['b2'])
        for c in range(8):
            mm(ps[:, 3, 0:W], onesf[:], sq[:, c, :], c == 0, c == 7, ['sq', 'onesf'], ['b3'])
        act(mean_sb[:], ps[:, 2, 0:W], AF.Copy, ['b2'], ['mean'])
        tt_('dve', m2[:], mean_sb[:], mean_sb[:], ALU.mult, ['mean'], ['m2'])
        tt_('dve', m2[:], ps[:, 3, 0:W], m2[:], ALU.subtract, ['b3', 'm2'], ['m2'])
        ts_('dve', m2[:], m2[:], LN_EPS, None, ALU.add, None, ['m2'], ['m2'])
        act(m2[:], m2[:], AF.Sqrt, ['m2'], ['m2'])
        op('dve', lambda e: e.reciprocal(out=rstd[:], in_=m2[:]), ['m2'], ['rstd'])
        tt_('dve', xn[:], t[:], mean_sb[:].unsqueeze(1).broadcast_to([128, 8, W]), ALU.subtract, ['t', 'mean', 'sq'], ['t'])
        tt_('dve', xn[:], xn[:], rstd[:].unsqueeze(1).broadcast_to([128, 8, W]), ALU.mult, ['t', 'rstd'], ['t'])
        for c in range(8):
            ts_('pool', xn[:, c, :], xn[:, c, :], lnp[:, gi, c:c + 1], lnp[:, gi + 1, c:c + 1], ALU.mult, ALU.add,
                ['t', 'lnp'], ['t'])
        if final:
            dma('sp', outT.rearrange("(c p) t -> p c t", p=128)[:, :, t0:t0 + W], xn[:], 'xn', ['t'], ['@outT'])
        else:
            dma('sp', fm(xres, 0, 8, t0, t0 + W), xn[:], 'xn', ['t'], ['@xres'])
            act(xnb[:], xn[:], AF.Copy, ['t'], ['xnb'])
            dma('sp', fm(xb, 0, 8, t0, t0 + W), xnb[:], 'xnb', ['xnb'], ['@xb'])

    def ln_bufs(st):
        return (sb(st, "ln_sq", [128, 8, TT], F32), sb(st, "ln_mean", [128, TT], F32), sb(st, "ln_m2", [128, TT], F32),
                sb(st, "ln_rstd", [128, TT], F32), sb(st, "ln_xnb", [128, 8, TT], BF16))

    def load_w_bf16(wt, src2d, ncols, name, nkc=8, col0=0):
        for kc in range(nkc if 'w' not in os.environ.get('KSKIP', '') else 0):
            dma('pool', wt[:, kc, 0:ncols], src2d[kc * 128:(kc + 1) * 128, col0:col0 + ncols], name + str(kc % 4), [], ['@' + name])

    with ExitStack() as st:
        posi = sb(st, "posi", [128, TL], I32)
        posf = sb(st, "posf", [128, TL], F32)
        ang = sb(st, "ang", [128, TL], F32)
        tb = sb(st, "tb", [128, TL], F32)
        posf2 = sb(st, "posf2", [128, TL], F32)
        dma('sp', posi[:], pos_in[0:1, :].broadcast_to([128, TL]), 'posi', [], ['posi'])
        cp('dve', posf[:], posi[:], ['posi'], ['posf'])
        C1 = 6.28125
        C2 = 2 * np.pi - 6.28125
        for ty, col in ((0, C_INVQ), (1, C_INVI)):
            ts_('dve', ang[:], posf[:], cst[:, col:col + 1], None, ALU.mult, None, ['posf', 'cst'], ['ang'])
            ts_('dve', tb[:], ang[:], float(1.0 / (2 * np.pi)), None, ALU.mult, None, ['ang'], ['tb'])
            cp('dve', posi[:], tb[:], ['tb'], ['posi'])
            cp('dve', tb[:], posi[:], ['posi'], ['tb'])
            stt_(ang[:], tb[:], -C1, ang[:], ALU.mult, ALU.add, ['tb', 'ang'], ['ang'])
            stt_(ang[:], tb[:], float(-C2), ang[:], ALU.mult, ALU.add, ['tb', 'ang'], ['ang'])
            for cs, shift in ((0, 0.5 * PI), (1, 0.0)):
                ts_('dve', tb[:], ang[:], shift, None, ALU.add, None, ['ang'], ['tb'])
                for (cmp_, thr_, adj_) in ((ALU.is_gt, PI, -2 * PI), (ALU.is_lt, -PI, 2 * PI), (ALU.is_gt, PI, -2 * PI), (ALU.is_lt, -PI, 2 * PI)):
                    ts_('dve', posf2[:], tb[:], thr_, adj_, cmp_, ALU.mult, ['tb'], ['posf2'])
                    tt_('dve', tb[:], tb[:], posf2[:], ALU.add, ['tb', 'posf2'], ['tb'])
                ts_('dve', tb[:], tb[:], PI, -PI, ALU.min, ALU.max, ['tb'], ['tb'])
                act(tb[:], tb[:], AF.Sin, ['tb'], ['tb'])
                dma('sp', tabs[2 * ty + cs, :, :], tb[:], 'tb', ['tb'], ['@tabs'])
        xf = [sb(st, "xf%d" % i, [128, 8, TT], F32) for i in range(2)]
        xh = [sb(st, "xh%d" % i, [128, 8, TT], BF16) for i in range(2)]
        for t in range(NT):
            s = t % 2
            dma('sp', xf[s][:], xT_in.rearrange("(c p) t -> p c t", p=128)[:, :, t * TT:(t + 1) * TT], 'xf%d' % s, [], ['xf%d' % s])
            act(xh[s][:], xf[s][:], AF.Copy, ['xf%d' % s], ['xh%d' % s])
            dma('sp', fm(xb, 0, 8, t * TT, (t + 1) * TT), xh[s][:], 'xh%d' % s, ['xh%d' % s], ['@xb'])
        tr.barrier()
    stage_end('S0')

    for l in range(L):
        if l > 0:
            tr.new_epoch('L%d' % l)
        res_src = (lambda c0, c1, t0, t1: xT_in.rearrange("(c p) t -> p c t", p=128)[:, c0:c1, t0:t1]) if l == 0 else \
                  (lambda c0, c1, t0, t1: fm(xres, c0, c1, t0, t1))

        with ExitStack() as st:
            win = sb(st, "win", [128, 8, INW], BF16)
            load_w_bf16(win, w_in[l], INW, 'win')
            xbt = [sb(st, "xbt%d" % i, [128, 8, TT], BF16) for i in range(2)]
            tbt = [sb(st, "tbt%d" % i, [128, 4, TT], F32) for i in range(2)]
            cbu_t = sb(st, "cbu_t", [128, 4, TT], F32)
            tmpcc = sb(st, "tmpcc", [128, 2, TT], F32)
            ab = sb(st, "ab", [128, TT], BF16)
            t1 = sb(st, "t1", [128, TT], F32)
            t2 = sb(st, "t2", [128, TT], F32)
            qst = sb(st, "qst", [128, 3, TT], BF16)
            iqst = sb(st, "iqst", [128, 3, TT], BF16)
            op('pool', lambda e: e.memset(iqst[:], 0.0), [], ['iqst'])
            cqst = sb(st, "cqst", [128, 3, TT], BF16)
            ckst = sb(st, "ckst", [128, 3, TT], BF16)
            kst = sb(st, "kst", [64, TT], BF16)
            ikst = sb(st, "ikst", [32, TT], BF16)
            vst = sb(st, "vst", [128, NB4, 64], BF16)
            iwst = sb(st, "iwst", [128, NB4, 8], F32)
            cvst = sb(st, "cvst", [128, NB4, 384], BF16)

            def load_tile(t):
                s = t % 2
                dma('sp', xbt[s][:], fm(xb, 0, 8, t * TT, (t + 1) * TT), 'xbt%d' % s, ['xb'], ['xbt%d' % s])
                dma('sp', tbt[s][:], fm(tabs, 0, 4, t * TT, (t + 1) * TT), 'tbt%d' % s, ['tabs'], ['tbt%d' % s])
            load_tile(0)
            bank_ctr = [0]
            rb_ctr = [0]
            for t in range(NT):
                s = t % 2
                if t + 1 < NT:
                    load_tile(t + 1)
                X = xbt[s]
                XR = 'xbt%d' % s
                TB = tbt[s]
                TBR = 'tbt%d' % s
                c0t, c1t = t * TT, (t + 1) * TT

                def proj(col0, M):
                    b = bank_ctr[0] % 4
                    bank_ctr[0] += 1
                    for kc in range(8):
                        mm(ps[0:M, b, 0:TT], win[:, kc, col0:col0 + M], X[:, kc, :], kc == 0, kc == 7, ['@win', XR], ['b%d' % b])
                    return b

                def rope(b, M, pm, pmname, tC, tS, dst, dstres):
                    rb = 4 + rb_ctr[0] % 2
                    rb_ctr[0] += 1
                    act(ab[0:M, :], ps[0:M, b, 0:TT], AF.Copy, ['b%d' % b], ['ab'])
                    KR = os.environ.get('KR2', '')
                    if 'm' not in KR:
                        mm(ps[0:M, rb, 0:TT], pm[0:M, 0:M], ab[0:M, :], True, True, ['ab', pmname], ['b%d' % rb])
                    if 'd' in KR:
                        return
                    if os.environ.get('KR4') == '1':
                        act(t2[0:M, :], ps[0:M, b, 0:TT], AF.Copy, ['b%d' % b], ['t2'])
                        tt_('dve', t1[0:M, :], t2[0:M, :], TB[0:M, tC, :], ALU.mult, ['t2', TBR], ['t1'])
                    elif os.environ.get('KR4') == '2':
                        cp('dve', t1[0:M, :], ps[0:M, b, 0:TT], ['b%d' % b, 'ab'], ['t1'])
                    elif os.environ.get('KR4') == '3':
                        cp('dve', t1[0:M, :], TB[0:M, tC, :], [TBR], ['t1'])
                    else:
                        tt_('dve', t1[0:M, :], ps[0:M, b, 0:TT], TB[0:M, tC, :], ALU.mult, ['b%d' % b, TBR], ['t1'])
                    if os.environ.get('KR3') == '1':
                        cp('dve', dst, t1[0:M, :], ['t1'], [dstres])
                        return
                    tt_('dve', t2[0:M, :], ps[0:M, rb, 0:TT], TB[0:M, tS, :], ALU.mult, ['b%d' % rb, TBR], ['t2'])
                    if os.environ.get('KR3') == '2':
                        cp('dve', dst, t2[0:M, :], ['t2'], [dstres])
                        return
                    tt_('pool' if os.environ.get('KROPE') is None else 'dve', dst, t1[0:M, :], t2[0:M, :], ALU.add, ['t1', 't2'], [dstres])

                SK = os.environ.get('KSKIP', '')
                for k in range(2 if 'c' not in SK else 0):
                    b = proj(k * 128, 128)
                    act(cbu_t[:, k, :], ps[:, b, 0:TT], AF.Copy, ['b%d' % b], ['cbu_t'])
                for k in range(2 if 'c' not in SK else 0):
                    b = proj(256 + k * 128, 128)
                    act(tmpcc[:, k, :], ps[:, b, 0:TT], AF.Copy, ['b%d' % b], ['tmpcc'])
                for k in range(2 if 'c' not in SK else 0):
                    b = proj(512 + k * 128, 128)
                    tt_('dve', cbu_t[:, 2 + k, :], ps[:, b, 0:TT], tmpcc[:, k, :], ALU.mult, ['b%d' % b, 'tmpcc'], ['cbu_t'])
                dma('sp', fm(cbu_d, 0, 4, c0t, c1t), cbu_t[:], 'cbu_t', ['cbu_t'], ['@cbu_d'])
                for k in range(2 if 'h' not in SK else 0):
                    dma('sp', hal[k * 128:(k + 1) * 128, t * NB4 * 2:(t + 1) * NB4 * 2].rearrange("p (b e) -> p b e", e=2),
                        cbu_t[:, 2 + k, :].rearrange("p (b e) -> p b e", e=128)[:, :, 126:128], 'cbu_t', ['cbu_t'], ['@hal'])
                for k in range(3 if 'q' not in SK else 0):
                    b = proj(768 + k * 128, 128)
                    rope(b, 128, pmq, 'pmq', 0, 1, qst[:, k, :], 'qst')
                dma('sp', fm(qT_d, 0, 3, c0t, c1t), qst[:], 'qst', ['qst'], ['@qT_d'])
                if 'k' not in SK:
                    b = proj(1152, 64)
                    rope(b, 64, pmq, 'pmq', 0, 1, kst[:], 'kst')
                    dma('sp', pay[t, 0:64, :], kst[:], 'kst', ['kst'], ['@pay'])
                for k in range(3 if 'i' not in SK else 0):
                    Mk = 96 if k < 2 else 64
                    b = proj(1280 + k * 96, Mk)
                    rope(b, Mk, pmi, 'pmi', 2, 3, iqst[0:Mk, k, :], 'iqst')
                dma('sp', fm(iqT_d, 0, 3, c0t, c1t), iqst[:], 'iqst', ['iqst'], ['@iqT_d'])
                if 'j' not in SK:
                    b = proj(1536, 32)
                    rope(b, 32, pmi, 'pmi', 2, 3, ikst[:], 'ikst')
                    dma('sp', pay[t, 64:96, :], ikst[:], 'ikst', ['ikst'], ['@pay'])
                for k in range(3 if 'a' not in SK else 0):
                    b = proj(1576 + k * 128, 128)
                    act(cqst[:, k, :], ps[:, b, 0:TT], AF.Copy, ['b%d' % b], ['cqst'])
                dma('sp', fm(cqT_d, 0, 3, c0t, c1t), cqst[:], 'cqst', ['cqst'], ['@cqT_d'])
                for k in range(3 if 'b' not in SK else 0):
                    b = proj(1960 + k * 128, 128)
                    act(ckst[:, k, :], ps[:, b, 0:TT], AF.Copy, ['b%d' % b], ['ckst'])
                dma('sp', pay[t, 160:544, :].rearrange("(k p) t -> p k t", p=128), ckst[:], 'ckst', ['ckst'], ['@pay'])
                for bl in range(NB4 if 't' not in SK else 0):
                    for (o0, col0, n) in ((0, 1216, 64), (64, 1568, 8), (72, 2344, 384)):
                        for kc in range(8):
                            mm(ps[:, 6, o0:o0 + n], X[:, kc, bl * 128:(bl + 1) * 128], win[:, kc, col0:col0 + n],
                               kc == 0, kc == 7, ['@win', XR], ['b6'])
                    act(vst[:, bl, :], ps[:, 6, 0:64], AF.Copy, ['b6'], ['vst'])
                    ts_('dve', iwst[:, bl, :], ps[:, 6, 64:72], float(8 ** -0.5), None, ALU.mult, None, ['b6'], ['iwst'])
                    act(cvst[:, bl, :], ps[:, 6, 72:456], AF.Copy, ['b6'], ['cvst'])
                if 's' in SK:
                    continue
                vflat = pay[t, 96:160, :].rearrange("a t -> (a t)")
                dma('sp', vflat.rearrange("(b p d) -> p b d", p=128, d=64), vst[:], 'vst', ['vst'], ['@pay'])
                dma('sp', iw_d[c0t:c1t, :].rearrange("(b p) d -> p b d", p=128), iwst[:], 'iwst', ['iwst'], ['@iw_d'])
                cvflat = pay[t, 544:928, :].rearrange("a t -> (a t)")
                dma('sp', cvflat.rearrange("(b p d) -> p b d", p=128, d=384), cvst[:], 'cvst', ['cvst'], ['@pay'])
            tr.barrier()
        stage_end('S1')

        groups = [[0, 1, 2, 3], [4, 5, 6, 7]]
        for t in range(NT):
            op('pool', lambda e: e.collective_compute("AllGather", ALU.bypass, replica_groups=groups, ins=[pay[t].opt()], outs=[gath[t].opt()]),
               ['pay'], ['gath%d' % t])
        op('pool', lambda e: e.collective_compute("AllGather", ALU.bypass, replica_groups=groups, ins=[hal.opt()], outs=[hgath.opt()]),
           ['hal'], ['hgath'])
        tr.barrier()
        stage_end('G')

        with ExitStack() as st:
            kT2 = sb(st, "kT2", [128, 4, TL], BF16)
            ikT4 = sb(st, "ikT4", [128, 4, TL], BF16)
            vx = sb(st, "vx", [128, 4, NI, 65], BF16)
            score = sb(st, "score", [128, NI * 512], F32)
            Mq = sb(st, "Mq", [128, NI * 512], BF16)
            rl = [sb(st, "rl%d" % i, [128, 3, 512], BF16) for i in range(2)]
            pT = [sb(st, "pT%d" % i, [128, 2, 384], BF16) for i in range(3)]
            qt = [sb(st, "qt%d" % i, [128, 3, 128], BF16) for i in range(2)]
            iqt = [sb(st, "iqt%d" % i, [128, 3, 128], BF16) for i in range(2)]
            iwt = [sb(st, "iwt%d" % i, [128, 8], F32) for i in range(2)]
            dg = sb(st, "dg", [128, 8, 128], BF16)
            lastp = sb(st, "lastp", [128, 512], F32)
            sm = sb(st, "sm", [128, 16], F32)
            halves = sb(st, "halves", [128, 32], F32)
            rden = sb(st, "rden", [128, 8], F32)
            ytm = sb(st, "ytm", [128, 384], BF16)
            yst = [sb(st, "yst%d" % i, [128, 3, 128], BF16) for i in range(2)]
            for r in range(4):
                base = r * PAYR
                for t in range(NT):
                    tsl = slice(t * TT, (t + 1) * TT)
                    for hh in range(2):
                        dma('sp', kT2[hh * 64:(hh + 1) * 64, r, tsl], gath[t, base:base + 64, :], 'kT2_%d' % hh, ['gath'], ['@kT2'])
                    for g in range(3):
                        dma('sp', ikT4[g * 32:(g + 1) * 32, r, tsl], gath[t, base + 64:base + 96, :], 'ikT4_%d' % g, ['gath'], ['@ikT4'])
                    dma('sp', vx[:, r, t * NB4:(t + 1) * NB4, 0:64],
                        gath[t, base + 96:base + 160, :].rearrange("a t -> (a t)").rearrange("(i p d) -> p i d", p=128, d=64),
                        'vx', ['gath'], ['@vx'])
            op('pool', lambda e: e.memset(vx[:, :, :, 64:65], 1.0), [], ['vx1'])

            def load_q(i):
                s = i % 2
                dma('sp', qt[s][:], fm(qT_d, 0, 3, i * 128, (i + 1) * 128), 'qt%d' % s, ['qT_d'], ['qt%d' % s])
                dma('sp', iqt[s][:], fm(iqT_d, 0, 3, i * 128, (i + 1) * 128), 'iqt%d' % s, ['iqT_d'], ['iqt%d' % s])
                dma('sp', iwt[s][:], iw_d[i * 128:(i + 1) * 128, :], 'iwt%d' % s, ['iw_d'], ['iwt%d' % s])
            load_q(0)
            ucnt = 0
            ktc = 0
            stres = lambda s2_: ['b%d' % bb for bb in (2 * s2_, 2 * s2_ + 1)]
            for i in range(NI):
                s = i % 2
                if i + 1 < NI:
                    load_q(i + 1)
                N = (i + 1) * 512
                QT, IQ, IW = qt[s], iqt[s], iwt[s]
                for h in range(8):
                    ts_('pool', dg[:, h, :], cst[:, C_ID:C_ID + 128], IW[:, h:h + 1], None, ALU.mult, None, ['cst', 'iwt%d' % s], ['dg'])
                units = [(m, c) for m in range(i + 1) for c in range(3)]
                ncg = (3, 3, 2)

                def logits(u, uc):
                    m, c = u
                    bs = 3 * (uc % 2)
                    for g in range(ncg[c]):
                        mm(ps[:, bs + g, :].rearrange("p (a b) -> p a b", b=128), IQ[g * 32:(g + 1) * 32, c, :],
                           ikT4[g * 32:(g + 1) * 32, :, m * 128:(m + 1) * 128],
                           True, True, ['iqt%d' % s, '@ikT4'], ['b%d' % (bs + g)])

                def relu_diag(u, uc):
                    m, c = u
                    rs = uc % 2
                    bs = 3 * (uc % 2)
                    ng = ncg[c]
                    act(rl[rs][:, 0:ng, :], ps[:, bs:bs + ng, :], AF.Relu, ['b%d' % (bs + g_) for g_ in range(ng)], ['rl%d' % rs])
                    sbk = 6 + m % 2
                    for g in range(ng):
                        mm(ps[:, sbk, :], dg[:, 3 * c + g, :], rl[rs][:, g, :],
                           c == 0 and g == 0, c == 2 and g == 1, ['dg', 'rl%d' % rs], ['b%d' % sbk])
                    if c == 2:
                        if m < i:
                            act(score[:, m * 512:(m + 1) * 512], ps[:, sbk, :], AF.Copy, ['b%d' % sbk], ['score'])
                        else:
                            madm = cst[:, C_MADM:C_MADM + 512]
                            tt_('dve', score[:, m * 512:(m + 1) * 512], ps[:, sbk, :], madm, ALU.add, ['b%d' % sbk, 'cst'], ['score'])
                            if i == 0:
                                tt_('dve', lastp[:], ps[:, sbk, :], madm, ALU.subtract, ['b%d' % sbk, 'cst'], ['lastp'])
                logits(units[0], ucnt)
                for ui, u in enumerate(units):
                    if ui + 1 < len(units):
                        logits(units[ui + 1], ucnt + 1)
                    relu_diag(u, ucnt)
                    ucnt += 1
                op('dve', lambda e: e.tensor_reduce(out=sm[:, 0:1], in_=score[:, 0:N], axis=AX.X, op=ALU.max), ['score'], ['sm'])
                if i == 0:
                    op('dve', lambda e: e.tensor_reduce(out=sm[:, 1:2], in_=lastp[:], axis=AX.X, op=ALU.min), ['lastp'], ['sm'])
                else:
                    op('dve', lambda e: e.tensor_reduce(out=sm[:, 1:2], in_=score[:, 0:512], axis=AX.X, op=ALU.min), ['score'], ['sm'])
                tt_('dve', sm[:, 3:4], sm[:, 0:1], sm[:, 1:2], ALU.subtract, ['sm'], ['sm'])
                ts_('dve', sm[:, 3:4], sm[:, 3:4], 1.0001, 1e-6, ALU.mult, ALU.add, ['sm'], ['sm'])
                ts_('dve', halves[:, 0:NBIS], cst[:, C_POW:C_POW + NBIS], sm[:, 3:4], None, ALU.mult, None, ['sm', 'cst'], ['halves'])
                cp('dve', sm[:, 4:5], sm[:, 1:2], ['sm'], ['sm'])
                N1 = max(128, (int(N * 0.40) // 128) * 128)
                N2 = N - N1
                ts_('dve', sm[:, 8:9], cst[:, C_KTAB + i:C_KTAB + i + 1], 2.0, float(-N2), ALU.mult, ALU.add, ['cst'], ['sm_k2'])
                for r in range(NBIS):
                    tt_('dve', sm[:, 5:6], sm[:, 4:5], halves[:, r:r + 1], ALU.add, ['sm', 'halves'], ['sm_mid'])
                    ts_('dve', Mq[:, 0:N1], score[:, 0:N1], sm[:, 5:6], 0.0, ALU.is_ge, ALU.add, ['score', 'sm_mid'], ['MqA', 'sm_cnt'],
                        accum=sm[:, 6:7])
                    op('act', lambda e: e.activation(out=Mq[:, N1:N], in_=score[:, N1:N], func=AF.Sign, bias=sm[:, 5:6], scale=-1.0,
                                                     accum_out=sm[:, 10:11]), ['score', 'sm_mid'], ['MqB', 'sm_s'])
                    stt_(sm[:, 7:8], sm[:, 6:7], 2.0, sm[:, 10:11], ALU.mult, ALU.subtract, ['sm_cnt', 'sm_s'], ['sm_tmp'])
                    ts_('dve', sm[:, 7:8], sm[:, 7:8], sm[:, 8:9], halves[:, r:r + 1], ALU.is_ge, ALU.mult,
                        ['sm_tmp', 'sm_k2', 'halves'], ['sm_tmp'])
                    tt_('dve', sm[:, 4:5], sm[:, 4:5], sm[:, 7:8], ALU.add, ['sm', 'sm_tmp'], ['sm'])
                ts_('dve', Mq[:, 0:N], score[:, 0:N], sm[:, 4:5], NEG, ALU.is_lt, ALU.mult, ['score', 'sm', 'MqA', 'MqB'], ['Mq', 'MqA', 'MqB'])
                nkt = 4 * (i + 1)
                kbase = ktc

                def st_mask(kt):
                    m, r = kt // 4, kt % 4
                    s2 = (kbase + kt) % 2
                    for par in range(2):
                        mm(ps[:, 2 * s2 + par, 0:384].rearrange("p (a b) -> p a b", b=128), kT2[par * 64:(par + 1) * 64, r, m * 128:(m + 1) * 128],
                           QT[par * 64:(par + 1) * 64, :, :], True, False, ['@kT2', 'qt%d' % s], stres(s2))
                    for par in range(2):
                        mm(ps[:, 2 * s2 + par, 0:384].rearrange("p (a b) -> p a b", b=128), Mq[:, kt * 128:(kt + 1) * 128], ident3[:], False, True,
                           ['Mq', 'ident3'], stres(s2))
                st_mask(0)
                for kt in range(nkt):
                    m, r = kt // 4, kt % 4
                    s2 = (kbase + kt) % 2
                    p3 = (kbase + kt) % 3
                    if kt + 1 < nkt:
                        st_mask(kt + 1)
                    act(pT[p3][:], ps[:, 2 * s2:2 * s2 + 2, 0:384], AF.Exp, stres(s2), ['pT%d' % p3], scale=0.125)
                    for hs in range(6):
                        par, c = hs // 3, hs % 3
                        mm(ps[:, 4, hs * 65:(hs + 1) * 65], pT[p3][:, par, c * 128:(c + 1) * 128], vx[:, r, m, :],
                           kt == 0 and hs == 0, kt == nkt - 1 and hs == 5, ['pT%d' % p3, '@vx', 'vx1'], ['b4'])
                ktc += nkt
                pvv = ps[:, 4, 0:390].rearrange("p (h e) -> p h e", e=65)
                op('dve', lambda e: e.reciprocal(out=rden[:, 0:6], in_=pvv[:, :, 64]), ['b4'], ['rden'])
                for hs in range(6):
                    par, c = hs // 3, hs % 3
                    head = 2 * c + par
                    ts_('dve', ytm[:, head * 64:(head + 1) * 64], ps[:, 4, hs * 65:hs * 65 + 64], rden[:, hs:hs + 1], None, ALU.mult, None,
                        ['b4', 'rden'], ['ytm'])
                for k in range(3):
                    op('pe', lambda e, k=k: e.transpose(out=psb[:, k * 128:(k + 1) * 128], in_=ytm[:, k * 128:(k + 1) * 128], identity=identb[:]),
                       ['ytm', 'identb'], ['b5'])
                act(yst[s][:], psb[:, 0:384].rearrange("p (k t) -> p k t", t=128), AF.Copy, ['b5'], ['yst%d' % s])
                dma('sp', fm(yT_d, 2, 5, i * 128, (i + 1) * 128), yst[s][:], 'yst%d' % s, ['yst%d' % s], ['@yT_d'])
            tr.barrier()
        stage_end('S2a')

        with ExitStack() as st:
            bd = sb(st, "bd", [128, 5, 6, 128], F32)
            bslot = sb(st, "bslot", [128, 8, 6, 128], F32)
            hc = sb(st, "hc", [128, 2, 4, NI + 1, 2], F32)
            hsel = sb(st, "hsel", [128, 2, NI, 2], F32)
            cqt = [sb(st, "cqt%d" % i, [128, 3, 128], BF16) for i in range(2)]
            ckt = [sb(st, "ckt%d" % i, [128, 3, 2, 4, 128], BF16) for i in range(2)]
            cvt = [sb(st, "cvt%d" % i, [128, 2, 4, 384], BF16) for i in range(2)]
            cbt = [sb(st, "cbt%d" % i, [128, 2, 128], F32) for i in range(2)]
            ut = [sb(st, "ut%d" % i, [128, 2, 130], F32) for i in range(2)]
            sbs = [sb(st, "sbs%d" % i, [128, 2, 384], F32) for i in range(2)]
            pTb = [sb(st, "pTb%d" % i, [128, 2, 384], BF16) for i in range(2)]
            rdb = sb(st, "rdb", [128, 8], F32)
            ytb = sb(st, "ytb", [128, 384], BF16)
            ystb = [sb(st, "ystb%d" % i, [128, 3, 128], BF16) for i in range(2)]
            acc = sb(st, "acc", [128, 2, 128], F32)
            yc = [sb(st, "yc%d" % i, [128, 2, 128], BF16) for i in range(2)]
            for dl in range(5):
                for par in range(2):
                    dma('sp', bd[:, dl, par * 3:(par + 1) * 3, :],
                        rel_bias[l, :, (dl * 6 + par * 3) * 128:(dl * 6 + par * 3 + 3) * 128].rearrange("p (h q) -> p h q", q=128),
                        'bd', [], ['bd%d%d' % (dl, par)])
            bdall = ['bd%d%d' % (dl, par) for dl in range(5) for par in range(2)]
            op('pool', lambda e: e.memset(bd[64:128, 0, :, 0:64], NEG), bdall, bdall)
            op('pool', lambda e: e.memset(bd[0:64, 4, :, 64:128], NEG), bdall, bdall)
            for sl in range(8):
                ts_('dve', bslot[:, sl, :, :], bd[:, 0, :, :], cst[:, C_SEL + sl * 5:C_SEL + sl * 5 + 1], cst[:, C_NEGS + sl:C_NEGS + sl + 1],
                    ALU.mult, ALU.add, bdall + ['cst'], ['bslot'])
                for dl in range(1, 5):
                    stt_(bslot[:, sl, :, :], bd[:, dl, :, :], cst[:, C_SEL + sl * 5 + dl:C_SEL + sl * 5 + dl + 1], bslot[:, sl, :, :],
                         ALU.mult, ALU.add, bdall + ['cst', 'bslot'], ['bslot'])
            op('pool', lambda e: e.memset(hc[:], 0.0), [], ['hc'])
            for r in range(4):
                for k in range(2):
                    dma('sp', hc[:, k, r, 1:NI + 1, :], hgath[r * 256 + k * 128:r * 256 + (k + 1) * 128, :].rearrange("p (b e) -> p b e", e=2),
                        'hc', ['hgath', 'hc'], ['hc%d%d' % (r, k)])
            hcall = ['hc%d%d' % (r, k) for r in range(4) for k in range(2)]
            first = True
            for r in range(4):
                for (colb, off) in ((C_SELA, 1), (C_SELB, 0)):
                    src = hc[:, :, r, off:off + NI, :]
                    if first:
                        ts_('dve', hsel[:], src, cst[:, colb + r:colb + r + 1], None, ALU.mult, None, hcall + ['cst'], ['hsel'])
                        first = False
                    else:
                        stt_(hsel[:], src, cst[:, colb + r:colb + r + 1], hsel[:], ALU.mult, ALU.add, hcall + ['cst', 'hsel'], ['hsel'])

            def load_b(i):
                s = i % 2
                dma('sp', cqt[s][:], fm(cqT_d, 0, 3, i * 128, (i + 1) * 128), 'cqt%d' % s, ['cqT_d'], ['cqt%d' % s])
                for mi in range(2):
                    blk = i - 1 + mi
                    if blk < 0:
                        continue
                    t_, bi = blk // NB4, blk % NB4
                    for r in range(4):
                        base = r * PAYR
                        dma('sp', ckt[s][:, :, mi, r, :],
                            gath[t_, base + 160:base + 544, bi * 128:(bi + 1) * 128].rearrange("(k p) t -> p k t", p=128),
                            'ckt%d' % s, ['gath'], ['@ckt%d' % s])
                        cvflat_r = gath[t_, base + 544:base + 928, :].rearrange("a t -> (a t)")
                        dma('sp', cvt[s][:, mi, r, :], cvflat_r[bi * 128 * 384:(bi + 1) * 128 * 384].rearrange("(p e) -> p e", e=384),
                            'cvt%d' % s, ['gath'], ['@cvt%d' % s])
                dma('sp', cbt[s][:], fm(cbu_d, 0, 2, i * 128, (i + 1) * 128), 'cbt%d' % s, ['cbu_d'], ['cbt%d' % s])
                dma('sp', ut[s][:, :, 2:130], fm(cbu_d, 2, 4, i * 128, (i + 1) * 128), 'ut%d' % s, ['cbu_d'], ['ut%d' % s])
            load_b(0)
            slc = 0
            for i in range(NI):
                s = i % 2
                if i + 1 < NI:
                    load_b(i + 1)
                slots = [(mi, r) for mi in range(2) for r in range(4) if not (i == 0 and mi == 0)]
                for si, (mi, r) in enumerate(slots):
                    sl = mi * 4 + r
                    s2 = slc % 2
                    slc += 1
                    for k in range(3):
                        for par in range(2):
                            mm(ps[:, 2 * s2 + par, k * 128:(k + 1) * 128], ckt[s][par * 64:(par + 1) * 64, k, mi, r, :],
                               cqt[s][par * 64:(par + 1) * 64, k, :], True, True, ['@ckt%d' % s, 'cqt%d' % s], ['b%d' % (2 * s2 + par)])
                    stt_(sbs[s2][:], ps[:, 2 * s2:2 * s2 + 2, 0:384], 0.125,
                         bslot[:, sl, :, :].rearrange("p (a k) q -> p a (k q)", a=2), ALU.mult, ALU.add,
                         ['b%d' % (2 * s2), 'b%d' % (2 * s2 + 1), 'bslot'], ['sbs%d' % s2])
                    act(pTb[s2][:], sbs[s2][:], AF.Exp, ['sbs%d' % s2], ['pTb%d' % s2])
                    for hs in range(6):
                        par, k = hs // 3, hs % 3
                        head = 2 * k + par
                        mm(ps[:, 4, head * 64:(head + 1) * 64], pTb[s2][:, par, k * 128:(k + 1) * 128], cvt[s][:, mi, r, head * 64:(head + 1) * 64],
                           si == 0 and hs == 0, False, ['pTb%d' % s2, '@cvt%d' % s], ['b4'])
                        mm(ps[:, 4, 384 + head:385 + head], pTb[s2][:, par, k * 128:(k + 1) * 128], onesb[:, 0:1],
                           False, si == len(slots) - 1 and hs == 5, ['pTb%d' % s2, 'onesb'], ['b4'])
                op('dve', lambda e: e.reciprocal(out=rdb[:, 0:6], in_=ps[:, 4, 384:390]), ['b4'], ['rdb'])
                tt_('dve', ytb[:].rearrange("p (h e) -> p h e", e=64), ps[:, 4, 0:384].rearrange("p (h e) -> p h e", e=64),
                    rdb[:, 0:6].unsqueeze(2).broadcast_to([128, 6, 64]), ALU.mult, ['b4', 'rdb'], ['ytb'])
                for k in range(3):
                    op('pe', lambda e, k=k: e.transpose(out=psb[:, k * 128:(k + 1) * 128], in_=ytb[:, k * 128:(k + 1) * 128], identity=identb[:]),
                       ['ytb', 'identb'], ['b5'])
                act(ystb[s][:], psb[:, 0:384].rearrange("p (k t) -> p k t", t=128), AF.Copy, ['b5'], ['ystb%d' % s])
                dma('sp', fm(yT_d, 5, 8, i * 128, (i + 1) * 128), ystb[s][:], 'ystb%d' % s, ['ystb%d' % s], ['@yT_d'])
                cp('pool', ut[s][:, :, 0:2], hsel[:, :, i, :], ['hsel', 'ut%d' % s], ['ut%d' % s])
                for k in range(2):
                    w = lambda jj: cw[:, l * 3 + jj, k:k + 1]
                    ts_('dve', acc[:, k, :], ut[s][:, k, 0:128], w(0), None, ALU.mult, None, ['ut%d' % s, 'cw'], ['acc'])
                    stt_(acc[:, k, :], ut[s][:, k, 1:129], w(1), acc[:, k, :], ALU.mult, ALU.add, ['ut%d' % s, 'cw', 'acc'], ['acc'])
                    stt_(acc[:, k, :], ut[s][:, k, 2:130], w(2), acc[:, k, :], ALU.mult, ALU.add, ['ut%d' % s, 'cw', 'acc'], ['acc'])
                tt_('dve', yc[s][:], acc[:], cbt[s][:], ALU.mult, ['acc', 'cbt%d' % s], ['yc%d' % s])
                dma('sp', fm(yT_d, 0, 2, i * 128, (i + 1) * 128), yc[s][:], 'yc%d' % s, ['yc%d' % s], ['@yT_d'])
            tr.barrier()
        stage_end('S2b')

        with ExitStack() as st:
            wo = sb(st, "wo", [128, 8, D], BF16)
            load_w_bf16(wo, w_mix_out[l], D, 'wo')
            yt = [sb(st, "yt%d" % i, [128, 8, TT], BF16) for i in range(2)]
            xr = [sb(st, "xr%d" % i, [128, 8, TT], F32) for i in range(2)]
            tbuf = sb(st, "tbuf", [128, 8, TT], F32)
            lb = ln_bufs(st)

            def load3(t):
                s = t % 2
                dma('sp', yt[s][:], fm(yT_d, 0, 8, t * TT, (t + 1) * TT), 'yt%d' % s, ['yT_d'], ['yt%d' % s])
                dma('sp', xr[s][:], res_src(0, 8, t * TT, (t + 1) * TT), 'xr%d' % s, ['xres'], ['xr%d' % s])
            load3(0)
            for t in range(NT):
                s = t % 2
                if t + 1 < NT:
                    load3(t + 1)
                for oc in range(8):
                    b = oc % 2
                    for kc in range(8):
                        mm(ps[:, b, 0:TT], wo[:, kc, oc * 128:(oc + 1) * 128], yt[s][:, kc, :], kc == 0, kc == 7, ['@wo', 'yt%d' % s], ['b%d' % b])
                    stt_(tbuf[:, oc, :], xr[s][:, oc, :], ALPHA, ps[:, b, 0:TT], ALU.mult, ALU.add, ['xr%d' % s, 'b%d' % b], ['t'])
                layer_norm(lb, tbuf, l, 0, t * TT, False, None)
            tr.barrier()
        stage_end('S3')

        st4 = ExitStack()
        if True:
            st = st4
            kmT = sb(st, "kmT", [128, 8, 256], BF16)
            vm = sb(st, "vm", [128, 2, D], BF16)
            if True:
                st2 = st
                wkv = sb(st2, "wkv", [128, 8, 2 * D], BF16)
                memf = sb(st2, "memf", [128, 8, 256], F32)
                memb = sb(st2, "memb", [128, 8, 256], BF16)
                load_w_bf16(wkv, w_mkv[l], 2 * D, 'wkv')
                dma('sp', memf[:], memT_in.rearrange("(c p) m -> p c m", p=128), 'memf', [], ['memf'])
                cp('dve', memb[:], memf[:], ['memf'], ['memb'])
                for ec in range(8):
                    b = ec % 2
                    for kc in range(8):
                        mm(ps[:, b, 0:256], wkv[:, kc, ec * 128:(ec + 1) * 128], memb[:, kc, :], kc == 0, kc == 7, ['@wkv', 'memb'], ['b%d' % b])
                    act(kmT[:, ec, :], ps[:, b, 0:256], AF.Copy, ['b%d' % b], ['kmT'])
                for mt in range(2):
                    for eh in range(2):
                        b = 2 + eh
                        for kc in range(8):
                            mm(ps[:, b, :], memb[:, kc, mt * 128:(mt + 1) * 128], wkv[:, kc, D + eh * 512:D + (eh + 1) * 512],
                               kc == 0, kc == 7, ['@wkv', 'memb'], ['b%d' % b])
                        act(vm[:, mt, eh * 512:(eh + 1) * 512], ps[:, b, :], AF.Copy, ['b%d' % b], ['vm'])
                tr.barrier()
        if True:
            st = st4
            wq = sb(st, "wq", [128, 8, D], BF16)
            wmo = sb(st, "wmo", [128, 8, D], BF16)
            load_w_bf16(wq, w_mq[l], D, 'wq')
            load_w_bf16(wmo, w_mo[l], D, 'wmo')
            xbt4 = [sb(st, "xbt4_%d" % i, [128, 8, TT], BF16) for i in range(2)]
            xr = [sb(st, "xr4_%d" % i, [128, 8, TT], F32) for i in range(2)]
            qm = sb(st, "qm", [128, 8, TT], BF16)
            pTm = [sb(st, "pTm%d" % i, [128, 2, TT], BF16) for i in range(2)]
            rdm = sb(st, "rdm", [128, TT], F32)
            om = sb(st, "om", [128, 8, TT], BF16)
            tbuf = sb(st, "tbuf4", [128, 8, TT], F32)
            lb = ln_bufs(st)

            def load4(t):
                s = t % 2
                dma('sp', xbt4[s][:], fm(xb, 0, 8, t * TT, (t + 1) * TT), 'xbt4%d' % s, ['xb'], ['xbt4%d' % s])
                dma('sp', xr[s][:], fm(xres, 0, 8, t * TT, (t + 1) * TT), 'xr4%d' % s, ['xres'], ['xr4%d' % s])
            load4(0)
            hc4 = 0
            for t in range(NT):
                s = t % 2
                for ec in range(8):
                    b = ec % 2
                    for kc in range(8):
                        mm(ps[:, b, 0:TT], wq[:, kc, ec * 128:(ec + 1) * 128], xbt4[s][:, kc, :], kc == 0, kc == 7, ['@wq', 'xbt4%d' % s], ['b%d' % b])
                    act(qm[:, ec, :], ps[:, b, 0:TT], AF.Copy, ['b%d' % b], ['qm'])
                for h in range(4):
                    hs_ = hc4 % 2
                    hc4 += 1
                    for mt in range(2):
                        for dc in range(2):
                            mm(ps[:, 2 + mt, 0:TT], kmT[:, 2 * h + dc, mt * 128:(mt + 1) * 128], qm[:, 2 * h + dc, :], dc == 0, dc == 1,
                               ['kmT', 'qm'], ['b%d' % (2 + mt)])
                    act(pTm[hs_][:], ps[:, 2:4, 0:TT], AF.Exp, ['b2', 'b3'], ['pTm%d' % hs_], scale=1.0 / 16.0)
                    for mt in range(2):
                        mm(ps[:, 6, 0:TT], onesb[:], pTm[hs_][:, mt, :], mt == 0, mt == 1, ['onesb', 'pTm%d' % hs_], ['b6'])
                    op('dve', lambda e: e.reciprocal(out=rdm[:], in_=ps[:, 6, 0:TT]), ['b6'], ['rdm'])
                    for dc in range(2):
                        b = 4 + dc
                        for mt in range(2):
                            mm(ps[:, b, 0:TT], vm[:, mt, h * 256 + dc * 128:h * 256 + (dc + 1) * 128], pTm[hs_][:, mt, :], mt == 0, mt == 1,
                               ['vm', 'pTm%d' % hs_], ['b%d' % b])
                        tt_('dve', om[:, 2 * h + dc, :], ps[:, b, 0:TT], rdm[:], ALU.mult, ['b%d' % b, 'rdm'], ['om'])
                if t + 1 < NT:
                    load4(t + 1)
                for oc in range(8):
                    b = oc % 2
                    for kc in range(8):
                        mm(ps[:, b, 0:TT], wmo[:, kc, oc * 128:(oc + 1) * 128], om[:, kc, :], kc == 0, kc == 7, ['@wmo', 'om'], ['b%d' % b])
                    stt_(tbuf[:, oc, :], xr[s][:, oc, :], ALPHA, ps[:, b, 0:TT], ALU.mult, ALU.add, ['xr4%d' % s, 'b%d' % b], ['t'])
                layer_norm(lb, tbuf, l, 1, t * TT, False, None)
            tr.barrier()
            st4.close()
        stage_end('S4')

        with ExitStack() as st:
            xba = sb(st, "xba", [128, 8, TL], BF16)
            dma('sp', xba[:], fm(xb, 0, 8, 0, TL), 'xba', ['xb'], ['xba'])
            w1 = [sb(st, "w1_%d" % i, [128, 8, 128], BF16) for i in range(4)]
            rr = [sb(st, "rr%d" % i, [128, TT], F32) for i in range(2)]
            hst = [sb(st, "hst%d" % i, [128, TT], BF16) for i in range(4)]

            def loadw1(f):
                s = f % 4
                dma('pool', w1[s][:], w_ff1[l][:, f * 128:(f + 1) * 128].rearrange("(kc p) n -> p kc n", p=128), 'w1_%d' % s, [], ['w1_%d' % s])
            for f in range(3):
                loadw1(f)
            ct = 0
            for f in range(32):
                if f + 3 < 32:
                    loadw1(f + 3)
                s = f % 4
                for t in range(NT):
                    b = ct % 4
                    r2 = ct % 2
                    h4 = ct % 4
                    ct += 1
                    for kc in range(8):
                        mm(ps[:, b, 0:TT], w1[s][:, kc, :], xba[:, kc, t * TT:(t + 1) * TT], kc == 0, kc == 7, ['w1_%d' % s, 'xba'], ['b%d' % b])
                    act(rr[r2][:], ps[:, b, 0:TT], AF.Relu, ['b%d' % b], ['rr%d' % r2])
                    tt_('pool', hst[h4][:], rr[r2][:], rr[r2][:], ALU.mult, ['rr%d' % r2], ['hst%d' % h4])
                    dma('sp', hT_d[f, :, t * TT:(t + 1) * TT], hst[h4][:], 'hst%d' % h4, ['hst%d' % h4], ['@hT_d'])
            tr.barrier()
        stage_end('S5a')

        with ExitStack() as st:
            w2 = sb(st, "w2", [128, 32, D], BF16)
            load_w_bf16(w2, w_ff2[l], D, 'w2', nkc=32)
            ht = [sb(st, "ht%d" % i, [128, 32, TT], BF16) for i in range(2)]
            xr = [sb(st, "xr5_%d" % i, [128, 8, TT], F32) for i in range(1)]
            xr = [xr[0], xr[0]]
            tbuf = sb(st, "tbuf5", [128, 8, TT], F32)
            lb = ln_bufs(st)

            def load5(t):
                s = t % 2
                dma('sp', ht[s][:], fm(hT_d, 0, 32, t * TT, (t + 1) * TT), 'ht%d' % s, ['hT_d'], ['ht%d' % s])

            def load5x(t):
                dma('sp', xr[0][:], fm(xres, 0, 8, t * TT, (t + 1) * TT), 'xr50', ['xres'], ['xr50'])
            load5(0)
            load5x(0)
            for t in range(NT):
                s = t % 2
                if t + 1 < NT:
                    load5(t + 1)
                for oc in range(8):
                    b = oc % 2
                    for kc in range(32):
                        mm(ps[:, b, 0:TT], w2[:, kc, oc * 128:(oc + 1) * 128], ht[s][:, kc, :], kc == 0, kc == 31, ['@w2', 'ht%d' % s], ['b%d' % b])
                    stt_(tbuf[:, oc, :], xr[s][:, oc, :], ALPHA, ps[:, b, 0:TT], ALU.mult, ALU.add, ['xr50', 'b%d' % b], ['t'])
                if t + 1 < NT:
                    load5x(t + 1)
                layer_norm(lb, tbuf, l, 2, t * TT, l == L - 1, None)
            tr.barrier()

    tr.barrier()
    es.close()
    return nc


def make_consts(S, j):
    TL = S // 4
    NI = TL // 128
    NCST = C_KTAB + NI
    c = np.zeros((128, NCST), np.float32)
    c[:, C_ID:C_ID + 128] = np.eye(128, dtype=np.float32)
    pmq = np.zeros((128, 128), np.float32)
    for hb in (0, 64):
        for d in range(8):
            pmq[hb + d + 8, hb + d] = -1.0
            pmq[hb + d, hb + d + 8] = 1.0
    pmi = np.zeros((128, 128), np.float32)
    for hb in (0, 32, 64, 96):
        for d in range(4):
            pmi[hb + d + 4, hb + d] = -1.0
            pmi[hb + d, hb + d + 4] = 1.0
    c[:, C_PMQ:C_PMQ + 128] = pmq
    c[:, C_PMI:C_PMI + 128] = pmi
    fq = np.power(np.float32(500000.0), -np.arange(8, dtype=np.float32) / np.float32(8))
    fi = np.power(np.float32(500000.0), -np.arange(4, dtype=np.float32) / np.float32(4))
    for p in range(128):
        d = p % 64
        c[p, C_INVQ] = fq[d % 8] if d < 16 else 0.0
        d = p % 32
        c[p, C_INVI] = fi[d % 4] if d < 8 else 0.0
    for mi in range(2):
        for r in range(4):
            sl = mi * 4 + r
            dl = (j - r) if mi == 1 else (4 + j - r)
            if 0 <= dl <= 4:
                c[:, C_SEL + sl * 5 + dl] = 1.0
            else:
                c[:, C_NEGS + sl] = NEG
    if j > 0:
        c[:, C_SELA + j - 1] = 1.0
    else:
        c[:, C_SELB + 3] = 1.0
    for r in range(32):
        c[:, C_POW + r] = 2.0 ** -(r + 1)
    madm = np.zeros((128, 4, 128), np.float32)
    for r in range(4):
        if r > j:
            madm[:, r, :] = NEG
        elif r == j:
            madm[0:64, r, 64:128] = NEG
    c[:, C_MADM:C_MADM + 512] = madm.reshape(128, 512)
    for i in range(NI):
        g = 4 * i + j
        c[0:64, C_KTAB + i] = min(256, (2 * g + 1) * 64)
        c[64:128, C_KTAB + i] = min(256, (2 * g + 2) * 64)
    return c


def run_module(inputs, S, L):
    x = np.asarray(inputs['x'], np.float32)
    B = x.shape[0]
    TL = S // 4
    NBLK = S // 128
    in_maps = []
    lnp = np.stack([np.asarray(inputs[k], np.float32)[:L] for k in ('ln1_g', 'ln1_b', 'ln2_g', 'ln2_b', 'ln3_g', 'ln3_b')], axis=1)
    for c in range(8):
        b, j = c // 4, c % 4
        xs = x[b].reshape(NBLK, 128, D)[j::4].reshape(TL, D)
        ps_ = np.asarray(inputs['positions'])[b].reshape(NBLK, 128)[j::4].reshape(1, TL).astype(np.int32)
        m = {
            'xT': np.ascontiguousarray(xs.T),
            'pos': np.ascontiguousarray(ps_),
            'cst': make_consts(S, j),
            'memT': np.ascontiguousarray(np.asarray(inputs['mem'], np.float32)[b].T),
            'lnp': np.ascontiguousarray(lnp.reshape(L * 6, 8, 128).transpose(2, 0, 1)),
            'conv_w': np.ascontiguousarray(np.asarray(inputs['conv_w'], np.float32)[:L].reshape(L * 3, 2, 128).transpose(2, 0, 1)),
        }
        for k in ('w_in', 'w_mix_out', 'w_mq', 'w_mkv', 'w_mo', 'w_ff1', 'w_ff2'):
            m[k] = np.ascontiguousarray(np.asarray(inputs[k], np.float32)[:L])
        in_maps.append(m)
    rb = np.asarray(inputs['rel_bias'], np.float32)[:L]
    e = np.arange(767)
    ext = rb[:, :, np.clip(e - 127, -63, 128) + 63]
    s_ = np.arange(128)[:, None, None, None]
    dl_ = np.arange(5)[None, :, None, None]
    hs_ = np.arange(6)[None, None, :, None]
    q_ = np.arange(128)[None, None, None, :]
    head = 2 * (hs_ % 3) + hs_ // 3
    idx = 128 * dl_ + 127 + q_ - s_
    relb = np.ascontiguousarray(ext[:, head, idx].reshape(L, 128, 5 * 6 * 128))
    for m in in_maps:
        m['relb'] = relb
    nc = build_program(S, L)
    res = run_bass_kernel_spmd(nc, in_maps, core_ids=list(range(8)))
    out = np.zeros((B, NBLK, 128, D), np.float32)
    for c in range(8):
        b, j = c // 4, c % 4
        o = np.asarray(res.results[c]['outT'], np.float32).T.reshape(TL // 128, 128, D)
        out[b, j::4] = o
    return out.reshape(B, S, D)


def kernel(**inputs):
    return run_module(inputs, 16384, 4)
```

```python
import numpy as np
from contextlib import ExitStack
import concourse.bass as bass
import concourse.mybir as mybir
from concourse.bass_utils import run_bass_kernel_spmd

F32 = mybir.dt.float32
BF16 = mybir.dt.bfloat16
I32 = mybir.dt.int32
AF = mybir.ActivationFunctionType
ALU = mybir.AluOpType
AX = mybir.AxisListType

D = 1024
INW = 2728
DFF = 4096
NEG = -30000.0
ALPHA_L = lambda L: (2.0 * L) ** 0.25
LN_EPS = 1e-5
PI = float(np.pi)
NBIS = 20
SAME_ENG_SYNC = True
PAYR = 928

C_ID, C_PMQ, C_PMI, C_INVQ, C_INVI, C_SEL, C_NEGS, C_SELA, C_SELB, C_POW, C_MADM = 0, 128, 256, 384, 385, 386, 426, 434, 438, 442, 474
C_KTAB = C_MADM + 512


class Tr:
    def __init__(self, nc, es):
        self.nc = nc
        self.es = es
        self.E = {'pe': nc.tensor, 'dve': nc.vector, 'act': nc.scalar, 'pool': nc.gpsimd, 'sp': nc.sync}
        self.sem = {k: es.enter_context(nc.semaphore('s_' + k)) for k in self.E}
        self.cnt = {k: 0 for k in self.E}
        self.waited = {k: {} for k in self.E}
        self.W = {}
        self.R = {}
        self.dpool = []
        self.dmap = {}
        self.B1 = es.enter_context(nc.semaphore('bar1'))
        self.B2 = es.enter_context(nc.semaphore('bar2'))
        self.bk = 0
        self.nins = 0
        self.groups = {}
        self.gctr = 0

    def _dkey(self, key):
        if key not in self.dmap:
            idx = len(self.dmap)
            if idx >= len(self.dpool):
                self.dpool.append([self.es.enter_context(self.nc.semaphore('d_%d' % idx)), 0, idx])
            self.dmap[key] = idx
        return self.dpool[self.dmap[key]]

    def op(self, eng, fn, reads=(), writes=(), dma=None):
        deps = {}

        def merge(d):
            for k, sv in d.items():
                if k not in deps or deps[k][1] < sv[1]:
                    deps[k] = sv
        rl_ = []
        for r in reads:
            if r.startswith('@'):
                rl_.extend(self.groups.get(r, []))
            else:
                rl_.append(r)
        wl_ = []
        for w in writes:
            if w.startswith('@'):
                g = self.groups.setdefault(w, [])
                had_readers = False
                for sname in g:
                    if self.R.get(sname):
                        had_readers = True
                        merge(self.R[sname])
                if had_readers:
                    for sname in g:
                        merge(self.W.get(sname, {}))
                        self.W.pop(sname, None)
                        self.R.pop(sname, None)
                    del g[:]
                self.gctr += 1
                sname = '%s#%d' % (w, self.gctr)
                g.append(sname)
                wl_.append(sname)
            else:
                wl_.append(w)
        bk = [x for x in rl_ + wl_ if len(x) == 2 and x[0] == 'b' and x[1].isdigit()]
        rl_ = [x for x in rl_ if x not in bk]
        wl_ = [x for x in wl_ if x not in bk] + sorted(set(bk))
        reads, writes = rl_, wl_
        for r in reads:
            merge(self.W.get(r, {}))
        for w in writes:
            merge(self.W.get(w, {}))
            merge(self.R.get(w, {}))
        e = self.E[eng]
        wd = self.waited[eng]
        for k, (s, v) in deps.items():
            if k == eng and (eng == 'pe' or not SAME_ENG_SYNC):
                continue
            if wd.get(k, 0) >= v:
                continue
            e.wait_ge(s, v)
            wd[k] = v
        ins = fn(e)
        self.nins += 1
        if dma is not None:
            d = self._dkey(dma)
            d[1] += 16
            ins.then_inc(d[0], 16)
            ev = ('D%d' % d[2], (d[0], d[1]))
        else:
            self.cnt[eng] += 1
            ins.then_inc(self.sem[eng], 1)
            ev = (eng, (self.sem[eng], self.cnt[eng]))
        for w in writes:
            self.W[w] = {ev[0]: ev[1]}
            self.R[w] = {}
        for r in reads:
            if r not in writes:
                self.R.setdefault(r, {})[ev[0]] = ev[1]

    def new_epoch(self, tag):
        for k in self.E:
            self.sem[k] = self.es.enter_context(self.nc.semaphore('s_%s_%s' % (k, tag)))
            self.cnt[k] = 0
            for wd in self.waited.values():
                wd.pop(k, None)

    def barrier(self):
        for eng, e in self.E.items():
            wd = self.waited[eng]
            for k in self.E:
                if k == eng:
                    continue
                v = self.cnt[k]
                if v > wd.get(k, 0):
                    e.wait_ge(self.sem[k], v)
                    wd[k] = v
            for (s, v, idx) in self.dpool:
                k = 'D%d' % idx
                if v > wd.get(k, 0):
                    e.wait_ge(s, v)
                    wd[k] = v
        self.W.clear()
        self.R.clear()
        self.groups.clear()
        self.dmap.clear()


import os


class _Stop(Exception):
    pass


def build_program(S, L, debug=()):
    try:
        return _build_program(S, L, debug)
    except _Stop as e:
        return e.args[0]


def _build_program(S, L, debug=()):
    TL = S // 4
    NI = TL // 128
    TT = 512 if TL >= 512 else TL
    NT = TL // TT
    NB4 = TT // 128
    ALPHA = float(8.0 ** 0.25)
    NCST = C_KTAB + NI

    nc = bass.Bass("TRN2", target_bir_lowering=False)
    es = ExitStack()
    tr = Tr(nc, es)

    def din(name, shape, dt=F32):
        return nc.dram_tensor(name, list(shape), dt, kind="ExternalInput").ap()

    def dscr(name, shape, dt):
        return nc.dram_tensor(name, list(shape), dt).ap()

    xT_in = din("xT", [D, TL])
    pos_in = din("pos", [1, TL], I32)
    cst_in = din("cst", [128, NCST])
    memT_in = din("memT", [D, 256])
    w_in = din("w_in", [L, D, INW])
    conv_w = din("conv_w", [128, L * 3, 2])
    rel_bias = din("relb", [L, 128, 5 * 6 * 128])
    w_mix_out = din("w_mix_out", [L, D, D])
    lnp_in = din("lnp", [128, L * 6, 8])
    w_mq = din("w_mq", [L, D, D])
    w_mkv = din("w_mkv", [L, D, 2 * D])
    w_mo = din("w_mo", [L, D, D])
    w_ff1 = din("w_ff1", [L, D, DFF])
    w_ff2 = din("w_ff2", [L, DFF, D])
    outT = nc.dram_tensor("outT", [D, TL], F32, kind="ExternalOutput").ap()

    xres = dscr("xres", [8, 128, TL], F32)
    xb = dscr("xb", [8, 128, TL], BF16)
    tabs = dscr("tabs", [4, 128, TL], F32)
    qT_d = dscr("qT_d", [3, 128, TL], BF16)
    iqT_d = dscr("iqT_d", [3, 128, TL], BF16)
    iw_d = dscr("iw_d", [TL, 8], F32)
    cqT_d = dscr("cqT_d", [3, 128, TL], BF16)
    cbu_d = dscr("cbu_d", [4, 128, TL], F32)
    pay = dscr("pay", [NT, PAYR, TT], BF16)
    gath = dscr("gath", [NT, 4 * PAYR, TT], BF16)
    hal = dscr("hal", [256, NI * 2], F32)
    hgath = dscr("hgath", [4 * 256, NI * 2], F32)
    yT_d = dscr("yT_d", [8, 128, TL], BF16)
    hT_d = dscr("hT_d", [32, 128, TL], BF16)
    ext_d = dscr("ext_d", [6, 768], F32)
    dbg_out = {}
    for nm in debug:
        pass

    stage_ctr = [0]

    def stage_end(name):
        stage_ctr[0] += 1
        if os.environ.get('KSTOP') == name:
            tr.barrier()
            es.close()
            raise _Stop(nc)

    def fm(ap3, c0, c1, t0, t1):
        return ap3[c0:c1, :, t0:t1].rearrange("c p t -> p c t")

    uid = [0]

    def sb(st, name, shape, dt):
        uid[0] += 1
        return st.enter_context(nc.sbuf_tensor("s%d_%s" % (uid[0], name), list(shape), dt))

    op = tr.op

    def dma(q, out, in_, key, reads=(), writes=()):
        op(q, lambda e: e.dma_start(out=out, in_=in_), reads, writes, dma=key)

    def mm(out, lhsT, rhs, start, stop, reads, writes):
        op('pe', lambda e: e.matmul(out, lhsT, rhs, start=start, stop=stop, skip_group_check=True), reads, writes)

    def act(out, in_, func, reads, writes, scale=None):
        if scale is None:
            op('act', lambda e: e.activation(out=out, in_=in_, func=func), reads, writes)
        else:
            op('act', lambda e: e.activation(out=out, in_=in_, func=func, scale=scale), reads, writes)

    def tt_(eng, out, in0, in1, o, reads, writes):
        op(eng, lambda e: e.tensor_tensor(out=out, in0=in0, in1=in1, op=o), reads, writes)

    def ts_(eng, out, in0, s1, s2, o0, o1, reads, writes, accum=None):
        if o1 is None:
            op(eng, lambda e: e.tensor_scalar(out=out, in0=in0, scalar1=s1, scalar2=None, op0=o0), reads, writes)
        elif accum is None:
            op(eng, lambda e: e.tensor_scalar(out=out, in0=in0, scalar1=s1, scalar2=s2, op0=o0, op1=o1), reads, writes)
        else:
            op(eng, lambda e: e.tensor_scalar(out=out, in0=in0, scalar1=s1, scalar2=s2, op0=o0, op1=o1, accum_out=accum), reads, writes)

    def stt_(out, in0, sc, in1, o0, o1, reads, writes):
        op('dve', lambda e: e.scalar_tensor_tensor(out=out, in0=in0, scalar=sc, in1=in1, op0=o0, op1=o1), reads, writes)

    def cp(eng, out, in_, reads, writes):
        op(eng, lambda e: e.tensor_copy(out=out, in_=in_), reads, writes)

    ps = es.enter_context(nc.psum_tensor("ps", [128, 8, 512], F32))
    psb = ps[:, 5, :].bitcast(BF16)
    cst = sb(es, "cst", [128, NCST], F32)
    identb = sb(es, "identb", [128, 128], BF16)
    pmq = sb(es, "pmq", [128, 128], BF16)
    pmi = sb(es, "pmi", [128, 128], BF16)
    ident3 = sb(es, "ident3", [128, 3, 128], BF16)
    onesf = sb(es, "onesf", [128, 128], F32)
    onesb = sb(es, "onesb", [128, 128], BF16)
    lnp = sb(es, "lnp", [128, L * 6, 8], F32)
    cw = sb(es, "cw", [128, L * 3, 2], F32)

    dma('sp', cst[:], cst_in[:, :], 'cst', writes=['cst'])
    dma('sp', lnp[:], lnp_in[:, :, :], 'lnp', writes=['lnp'])
    dma('sp', cw[:], conv_w[:, :, :], 'cw', writes=['cw'])
    cp('dve', identb[:], cst[:, C_ID:C_ID + 128], ['cst'], ['identb'])
    cp('dve', pmq[:], cst[:, C_PMQ:C_PMQ + 128], ['cst'], ['pmq'])
    cp('dve', pmi[:], cst[:, C_PMI:C_PMI + 128], ['cst'], ['pmi'])
    for k in range(3):
        cp('dve', ident3[:, k, :], cst[:, C_ID:C_ID + 128], ['cst'], ['ident3'])
    op('pool', lambda e: e.memset(onesf[:], 1.0 / D), [], ['onesf'])
    op('pool', lambda e: e.memset(onesb[:], 1.0), [], ['onesb'])
    tr.barrier()

    def layer_norm(st_bufs, t, l, which, t0, final, tagres):
        sq, mean_sb, m2, rstd, xnb = st_bufs
        xn = t
        W = TT
        gi = l * 6 + which * 2
        act(sq[:], t[:], AF.Square, ['t'], ['sq'])
        for c in range(8):
            mm(ps[:, 2, 0:W], onesf[:], t[:, c, :], c == 0, c == 7, ['t', 'onesf'], ['b2'])
        for c in range(8):
            mm(ps[:, 3, 0:W], onesf[:], sq[:, c, :], c == 0, c == 7, ['sq', 'onesf'], ['b3'])
        act(mean_sb[:], ps[:, 2, 0:W], AF.Copy, ['b2'], ['mean'])
        tt_('dve', m2[:], mean_sb[:], mean_sb[:], ALU.mult, ['mean'], ['m2'])
        tt_('dve', m2[:], ps[:, 3, 0:W], m2[:], ALU.subtract, ['b3', 'm2'], ['m2'])
        ts_('dve', m2[:], m2[:], LN_EPS, None, ALU.add, None, ['m2'], ['m2'])
        act(m2[:], m2[:], AF.Sqrt, ['m2'], ['m2'])
        op('dve', lambda e: e.reciprocal(out=rstd[:], in_=m2[:]), ['m2'], ['rstd'])
        tt_('dve', xn[:], t[:], mean_sb[:].unsqueeze(1).broadcast_to([128, 8, W]), ALU.subtract, ['t', 'mean', 'sq'], ['t'])
        tt_('dve', xn[:], xn[:], rstd[:].unsqueeze(1).broadcast_to([128, 8, W]), ALU.mult, ['t', 'rstd'], ['t'])
        for c in range(8):
            ts_('pool', xn[:, c, :], xn[:, c, :], lnp[:, gi, c:c + 1], lnp[:, gi + 1, c:c + 1], ALU.mult, ALU.add,
                ['t', 'lnp'], ['t'])
        if final:
            dma('sp', outT.rearrange("(c p) t -> p c t", p=128)[:, :, t0:t0 + W], xn[:], 'xn', ['t'], ['@outT'])
        else:
            dma('sp', fm(xres, 0, 8, t0, t0 + W), xn[:], 'xn', ['t'], ['@xres'])
            act(xnb[:], xn[:], AF.Copy, ['t'], ['xnb'])
            dma('sp', fm(xb, 0, 8, t0, t0 + W), xnb[:], 'xnb', ['xnb'], ['@xb'])

    def ln_bufs(st):
        return (sb(st, "ln_sq", [128, 8, TT], F32), sb(st, "ln_mean", [128, TT], F32), sb(st, "ln_m2", [128, TT], F32),
                sb(st, "ln_rstd", [128, TT], F32), sb(st, "ln_xnb", [128, 8, TT], BF16))

    def load_w_bf16(wt, src2d, ncols, name, nkc=8, col0=0):
        for kc in range(nkc if 'w' not in os.environ.get('KSKIP', '') else 0):
            dma('pool', wt[:, kc, 0:ncols], src2d[kc * 128:(kc + 1) * 128, col0:col0 + ncols], name + str(kc % 4), [], ['@' + name])

    with ExitStack() as st:
        posi = sb(st, "posi", [128, TL], I32)
        posf = sb(st, "posf", [128, TL], F32)
        ang = sb(st, "ang", [128, TL], F32)
        tb = sb(st, "tb", [128, TL], F32)
        posf2 = sb(st, "posf2", [128, TL], F32)
        dma('sp', posi[:], pos_in[0:1, :].broadcast_to([128, TL]), 'posi', [], ['posi'])
        cp('dve', posf[:], posi[:], ['posi'], ['posf'])
        C1 = 6.28125
        C2 = 2 * np.pi - 6.28125
        for ty, col in ((0, C_INVQ), (1, C_INVI)):
            ts_('dve', ang[:], posf[:], cst[:, col:col + 1], None, ALU.mult, None, ['posf', 'cst'], ['ang'])
            ts_('dve', tb[:], ang[:], float(1.0 / (2 * np.pi)), None, ALU.mult, None, ['ang'], ['tb'])
            cp('dve', posi[:], tb[:], ['tb'], ['posi'])
            cp('dve', tb[:], posi[:], ['posi'], ['tb'])
            stt_(ang[:], tb[:], -C1, ang[:], ALU.mult, ALU.add, ['tb', 'ang'], ['ang'])
            stt_(ang[:], tb[:], float(-C2), ang[:], ALU.mult, ALU.add, ['tb', 'ang'], ['ang'])
            for cs, shift in ((0, 0.5 * PI), (1, 0.0)):
                ts_('dve', tb[:], ang[:], shift, None, ALU.add, None, ['ang'], ['tb'])
                for (cmp_, thr_, adj_) in ((ALU.is_gt, PI, -2 * PI), (ALU.is_lt, -PI, 2 * PI), (ALU.is_gt, PI, -2 * PI), (ALU.is_lt, -PI, 2 * PI)):
                    ts_('dve', posf2[:], tb[:], thr_, adj_, cmp_, ALU.mult, ['tb'], ['posf2'])
                    tt_('dve', tb[:], tb[:], posf2[:], ALU.add, ['tb', 'posf2'], ['tb'])
                ts_('dve', tb[:], tb[:], PI, -PI, ALU.min, ALU.max, ['tb'], ['tb'])
                act(tb[:], tb[:], AF.Sin, ['tb'], ['tb'])
                dma('sp', tabs[2 * ty + cs, :, :], tb[:], 'tb', ['tb'], ['@tabs'])
        xf = [sb(st, "xf%d" % i, [128, 8, TT], F32) for i in range(2)]
        xh = [sb(st, "xh%d" % i, [128, 8, TT], BF16) for i in range(2)]
        for t in range(NT):
            s = t % 2
            dma('sp', xf[s][:], xT_in.rearrange("(c p) t -> p c t", p=128)[:, :, t * TT:(t + 1) * TT], 'xf%d' % s, [], ['xf%d' % s])
            act(xh[s][:], xf[s][:], AF.Copy, ['xf%d' % s], ['xh%d' % s])
            dma('sp', fm(xb, 0, 8, t * TT, (t + 1) * TT), xh[s][:], 'xh%d' % s, ['xh%d' % s], ['@xb'])
        tr.barrier()
    stage_end('S0')

    for l in range(L):
        if l > 0:
            tr.new_epoch('L%d' % l)
        res_src = (lambda c0, c1, t0, t1: xT_in.rearrange("(c p) t -> p c t", p=128)[:, c0:c1, t0:t1]) if l == 0 else \
                  (lambda c0, c1, t0, t1: fm(xres, c0, c1, t0, t1))

        with ExitStack() as st:
            win = sb(st, "win", [128, 8, INW], BF16)
            load_w_bf16(win, w_in[l], INW, 'win')
            xbt = [sb(st, "xbt%d" % i, [128, 8, TT], BF16) for i in range(2)]
            tbt = [sb(st, "tbt%d" % i, [128, 4, TT], F32) for i in range(2)]
            cbu_t = sb(st, "cbu_t", [128, 4, TT], F32)
            tmpcc = sb(st, "tmpcc", [128, 2, TT], F32)
            ab = sb(st, "ab", [128, TT], BF16)
            t1 = sb(st, "t1", [128, TT], F32)
            t2 = sb(st, "t2", [128, TT], F32)
            qst = sb(st, "qst", [128, 3, TT], BF16)
            iqst = sb(st, "iqst", [128, 3, TT], BF16)
            op('pool', lambda e: e.memset(iqst[:], 0.0), [], ['iqst'])
            cqst = sb(st, "cqst", [128, 3, TT], BF16)
            ckst = sb(st, "ckst", [128, 3, TT], BF16)
            kst = sb(st, "kst", [64, TT], BF16)
            ikst = sb(st, "ikst", [32, TT], BF16)
            vst = sb(st, "vst", [128, NB4, 64], BF16)
            iwst = sb(st, "iwst", [128, NB4, 8], F32)
            cvst = sb(st, "cvst", [128, NB4, 384], BF16)

            def load_tile(t):
                s = t % 2
                dma('sp', xbt[s][:], fm(xb, 0, 8, t * TT, (t + 1) * TT), 'xbt%d' % s, ['xb'], ['xbt%d' % s])
                dma('sp', tbt[s][:], fm(tabs, 0, 4, t * TT, (t + 1) * TT), 'tbt%d' % s, ['tabs'], ['tbt%d' % s])
            load_tile(0)
            bank_ctr = [0]
            rb_ctr = [0]
            for t in range(NT):
                s = t % 2
                if t + 1 < NT:
                    load_tile(t + 1)
                X = xbt[s]
                XR = 'xbt%d' % s
                TB = tbt[s]
                TBR = 'tbt%d' % s
                c0t, c1t = t * TT, (t + 1) * TT

                def proj(col0, M):
                    b = bank_ctr[0] % 4
                    bank_ctr[0] += 1
                    for kc in range(8):
                        mm(ps[0:M, b, 0:TT], win[:, kc, col0:col0 + M], X[:, kc, :], kc == 0, kc == 7, ['@win', XR], ['b%d' % b])
                    return b

                def rope(b, M, pm, pmname, tC, tS, dst, dstres):
                    rb = 4 + rb_ctr[0] % 2
                    rb_ctr[0] += 1
                    act(ab[0:M, :], ps[0:M, b, 0:TT], AF.Copy, ['b%d' % b], ['ab'])
                    KR = os.environ.get('KR2', '')
                    if 'm' not in KR:
                        mm(ps[0:M, rb, 0:TT], pm[0:M, 0:M], ab[0:M, :], True, True, ['ab', pmname], ['b%d' % rb])
                    if 'd' in KR:
                        return
                    if os.environ.get('KR4') == '1':
                        act(t2[0:M, :], ps[0:M, b, 0:TT], AF.Copy, ['b%d' % b], ['t2'])
                        tt_('dve', t1[0:M, :], t2[0:M, :], TB[0:M, tC, :], ALU.mult, ['t2', TBR], ['t1'])
                    elif os.environ.get('KR4') == '2':
                        cp('dve', t1[0:M, :], ps[0:M, b, 0:TT], ['b%d' % b, 'ab'], ['t1'])
                    elif os.environ.get('KR4') == '3':
                        cp('dve', t1[0:M, :], TB[0:M, tC, :], [TBR], ['t1'])
                    else:
                        tt_('dve', t1[0:M, :], ps[0:M, b, 0:TT], TB[0:M, tC, :], ALU.mult, ['b%d' % b, TBR], ['t1'])
                    if os.environ.get('KR3') == '1':
                        cp('dve', dst, t1[0:M, :], ['t1'], [dstres])
                        return
                    tt_('dve', t2[0:M, :], ps[0:M, rb, 0:TT], TB[0:M, tS, :], ALU.mult, ['b%d' % rb, TBR], ['t2'])
                    if os.environ.get('KR3') == '2':
                        cp('dve', dst, t2[0:M, :], ['t2'], [dstres])
                        return
                    tt_('pool' if os.environ.get('KROPE') is None else 'dve', dst, t1[0:M, :], t2[0:M, :], ALU.add, ['t1', 't2'], [dstres])

                SK = os.environ.get('KSKIP', '')
                for k in range(2 if 'c' not in SK else 0):
                    b = proj(k * 128, 128)
                    act(cbu_t[:, k, :], ps[:, b, 0:TT], AF.Copy, ['b%d' % b], ['cbu_t'])
                for k in range(2 if 'c' not in SK else 0):
                    b = proj(256 + k * 128, 128)
                    act(tmpcc[:, k, :], ps[:, b, 0:TT], AF.Copy, ['b%d' % b], ['tmpcc'])
                for k in range(2 if 'c' not in SK else 0):
                    b = proj(512 + k * 128, 128)
                    tt_('dve', cbu_t[:, 2 + k, :], ps[:, b, 0:TT], tmpcc[:, k, :], ALU.mult, ['b%d' % b, 'tmpcc'], ['cbu_t'])
                dma('sp', fm(cbu_d, 0, 4, c0t, c1t), cbu_t[:], 'cbu_t', ['cbu_t'], ['@cbu_d'])
                for k in range(2 if 'h' not in SK else 0):
                    dma('sp', hal[k * 128:(k + 1) * 128, t * NB4 * 2:(t + 1) * NB4 * 2].rearrange("p (b e) -> p b e", e=2),
                        cbu_t[:, 2 + k, :].rearrange("p (b e) -> p b e", e=128)[:, :, 126:128], 'cbu_t', ['cbu_t'], ['@hal'])
                for k in range(3 if 'q' not in SK else 0):
                    b = proj(768 + k * 128, 128)
                    rope(b, 128, pmq, 'pmq', 0, 1, qst[:, k, :], 'qst')
                dma('sp', fm(qT_d, 0, 3, c0t, c1t), qst[:], 'qst', ['qst'], ['@qT_d'])
                if 'k' not in SK:
                    b = proj(1152, 64)
                    rope(b, 64, pmq, 'pmq', 0, 1, kst[:], 'kst')
                    dma('sp', pay[t, 0:64, :], kst[:], 'kst', ['kst'], ['@pay'])
                for k in range(3 if 'i' not in SK else 0):
                    Mk = 96 if k < 2 else 64
                    b = proj(1280 + k * 96, Mk)
                    rope(b, Mk, pmi, 'pmi', 2, 3, iqst[0:Mk, k, :], 'iqst')
                dma('sp', fm(iqT_d, 0, 3, c0t, c1t), iqst[:], 'iqst', ['iqst'], ['@iqT_d'])
                if 'j' not in SK:
                    b = proj(1536, 32)
                    rope(b, 32, pmi, 'pmi', 2, 3, ikst[:], 'ikst')
                    dma('sp', pay[t, 64:96, :], ikst[:], 'ikst', ['ikst'], ['@pay'])
                for k in range(3 if 'a' not in SK else 0):
                    b = proj(1576 + k * 128, 128)
                    act(cqst[:, k, :], ps[:, b, 0:TT], AF.Copy, ['b%d' % b], ['cqst'])
                dma('sp', fm(cqT_d, 0, 3, c0t, c1t), cqst[:], 'cqst', ['cqst'], ['@cqT_d'])
                for k in range(3 if 'b' not in SK else 0):
                    b = proj(1960 + k * 128, 128)
                    act(ckst[:, k, :], ps[:, b, 0:TT], AF.Copy, ['b%d' % b], ['ckst'])
                dma('sp', pay[t, 160:544, :].rearrange("(k p) t -> p k t", p=128), ckst[:], 'ckst', ['ckst'], ['@pay'])
                for bl in range(NB4 if 't' not in SK else 0):
                    for (o0, col0, n) in ((0, 1216, 64), (64, 1568, 8), (72, 2344, 384)):
                        for kc in range(8):
                            mm(ps[:, 6, o0:o0 + n], X[:, kc, bl * 128:(bl + 1) * 128], win[:, kc, col0:col0 + n],
                               kc == 0, kc == 7, ['@win', XR], ['b6'])
                    act(vst[:, bl, :], ps[:, 6, 0:64], AF.Copy, ['b6'], ['vst'])
                    ts_('dve', iwst[:, bl, :], ps[:, 6, 64:72], float(8 ** -0.5), None, ALU.mult, None, ['b6'], ['iwst'])
                    act(cvst[:, bl, :], ps[:, 6, 72:456], AF.Copy, ['b6'], ['cvst'])
                if 's' in SK:
                    continue
                vflat = pay[t, 96:160, :].rearrange("a t -> (a t)")
                dma('sp', vflat.rearrange("(b p d) -> p b d", p=128, d=64), vst[:], 'vst', ['vst'], ['@pay'])
                dma('sp', iw_d[c0t:c1t, :].rearrange("(b p) d -> p b d", p=128), iwst[:], 'iwst', ['iwst'], ['@iw_d'])
                cvflat = pay[t, 544:928, :].rearrange("a t -> (a t)")
                dma('sp', cvflat.rearrange("(b p d) -> p b d", p=128, d=384), cvst[:], 'cvst', ['cvst'], ['@pay'])
            tr.barrier()
        stage_end('S1')

        groups = [[0, 1, 2, 3], [4, 5, 6, 7]]
        for t in range(NT):
            op('pool', lambda e: e.collective_compute("AllGather", ALU.bypass, replica_groups=groups, ins=[pay[t].opt()], outs=[gath[t].opt()]),
               ['pay'], ['gath%d' % t])
        op('pool', lambda e: e.collective_compute("AllGather", ALU.bypass, replica_groups=groups, ins=[hal.opt()], outs=[hgath.opt()]),
           ['hal'], ['hgath'])
        tr.barrier()
        stage_end('G')

        with ExitStack() as st:
            kT2 = sb(st, "kT2", [128, 4, TL], BF16)
            ikT4 = sb(st, "ikT4", [128, 4, TL], BF16)
            vx = sb(st, "vx", [128, 4, NI, 65], BF16)
            score = sb(st, "score", [128, NI * 512], F32)
            Mq = sb(st, "Mq", [128, NI * 512], BF16)
            rl = [sb(st, "rl%d" % i, [128, 3, 512], BF16) for i in range(2)]
            pT = [sb(st, "pT%d" % i, [128, 2, 384], BF16) for i in range(3)]
            qt = [sb(st, "qt%d" % i, [128, 3, 128], BF16) for i in range(2)]
            iqt = [sb(st, "iqt%d" % i, [128, 3, 128], BF16) for i in range(2)]
            iwt = [sb(st, "iwt%d" % i, [128, 8], F32) for i in range(2)]
            dg = sb(st, "dg", [128, 8, 128], BF16)
            lastp = sb(st, "lastp", [128, 512], F32)
            sm = sb(st, "sm", [128, 16], F32)
            halves = sb(st, "halves", [128, 32], F32)
            gmx = sb(st, "gmx", [128, 32], F32)
            rden = sb(st, "rden", [128, 8], F32)
            ytm = sb(st, "ytm", [128, 384], BF16)
            yst = [sb(st, "yst%d" % i, [128, 3, 128], BF16) for i in range(2)]
            for r in range(4):
                base = r * PAYR
                for t in range(NT):
                    tsl = slice(t * TT, (t + 1) * TT)
                    for hh in range(2):
                        dma('sp', kT2[hh * 64:(hh + 1) * 64, r, tsl], gath[t, base:base + 64, :], 'kT2_%d' % hh, ['gath'], ['@kT2'])
                    for g in range(3):
                        dma('sp', ikT4[g * 32:(g + 1) * 32, r, tsl], gath[t, base + 64:base + 96, :], 'ikT4_%d' % g, ['gath'], ['@ikT4'])
                    dma('sp', vx[:, r, t * NB4:(t + 1) * NB4, 0:64],
                        gath[t, base + 96:base + 160, :].rearrange("a t -> (a t)").rearrange("(i p d) -> p i d", p=128, d=64),
                        'vx', ['gath'], ['@vx'])
            op('pool', lambda e: e.memset(vx[:, :, :, 64:65], 1.0), [], ['vx1'])

            def load_q(i):
                s = i % 2
                dma('sp', qt[s][:], fm(qT_d, 0, 3, i * 128, (i + 1) * 128), 'qt%d' % s, ['qT_d'], ['qt%d' % s])
                dma('sp', iqt[s][:], fm(iqT_d, 0, 3, i * 128, (i + 1) * 128), 'iqt%d' % s, ['iqT_d'], ['iqt%d' % s])
                dma('sp', iwt[s][:], iw_d[i * 128:(i + 1) * 128, :], 'iwt%d' % s, ['iw_d'], ['iwt%d' % s])
            load_q(0)
            ucnt = 0
            ktc = 0
            stres = lambda s2_: ['b%d' % bb for bb in (2 * s2_, 2 * s2_ + 1)]
            for i in range(NI):
                s = i % 2
                if i + 1 < NI:
                    load_q(i + 1)
                N = (i + 1) * 512
                QT, IQ, IW = qt[s], iqt[s], iwt[s]
                for h in range(8):
                    ts_('pool', dg[:, h, :], cst[:, C_ID:C_ID + 128], IW[:, h:h + 1], None, ALU.mult, None, ['cst', 'iwt%d' % s], ['dg'])
                units = [(m, c) for m in range(i + 1) for c in range(3)]
                ncg = (3, 3, 2)

                def logits(u, uc):
                    m, c = u
                    bs = 3 * (uc % 2)
                    for g in range(ncg[c]):
                        mm(ps[:, bs + g, :].rearrange("p (a b) -> p a b", b=128), IQ[g * 32:(g + 1) * 32, c, :],
                           ikT4[g * 32:(g + 1) * 32, :, m * 128:(m + 1) * 128],
                           True, True, ['iqt%d' % s, '@ikT4'], ['b%d' % (bs + g)])

                def relu_diag(u, uc):
                    m, c = u
                    rs = uc % 2
                    bs = 3 * (uc % 2)
                    ng = ncg[c]
                    act(rl[rs][:, 0:ng, :], ps[:, bs:bs + ng, :], AF.Relu, ['b%d' % (bs + g_) for g_ in range(ng)], ['rl%d' % rs])
                    sbk = 6 + m % 2
                    for g in range(ng):
                        mm(ps[:, sbk, :], dg[:, 3 * c + g, :], rl[rs][:, g, :],
                           c == 0 and g == 0, c == 2 and g == 1, ['dg', 'rl%d' % rs], ['b%d' % sbk])
                    if c == 2:
                        if m < i:
                            act(score[:, m * 512:(m + 1) * 512], ps[:, sbk, :], AF.Copy, ['b%d' % sbk], ['score'])
                        else:
                            madm = cst[:, C_MADM:C_MADM + 512]
                            tt_('dve', score[:, m * 512:(m + 1) * 512], ps[:, sbk, :], madm, ALU.add, ['b%d' % sbk, 'cst'], ['score'])
                            if i == 0:
                                tt_('dve', lastp[:], ps[:, sbk, :], madm, ALU.subtract, ['b%d' % sbk, 'cst'], ['lastp'])
                        op('dve', lambda e: e.tensor_reduce(out=gmx[:, m:m + 1], in_=score[:, m * 512:(m + 1) * 512], axis=AX.X, op=ALU.max),
                           ['score'], ['gmx'])
                logits(units[0], ucnt)
                for ui, u in enumerate(units):
                    if ui + 1 < len(units):
                        logits(units[ui + 1], ucnt + 1)
                    relu_diag(u, ucnt)
                    ucnt += 1
                op('dve', lambda e: e.tensor_reduce(out=sm[:, 0:1], in_=gmx[:, 0:i + 1], axis=AX.X, op=ALU.max), ['gmx'], ['sm'])
                if i == 0:
                    op('dve', lambda e: e.tensor_reduce(out=sm[:, 1:2], in_=lastp[:], axis=AX.X, op=ALU.min), ['lastp'], ['sm'])
                else:
                    op('dve', lambda e: e.tensor_reduce(out=sm[:, 1:2], in_=score[:, 0:512], axis=AX.X, op=ALU.min), ['score'], ['sm'])
                tt_('dve', sm[:, 3:4], sm[:, 0:1], sm[:, 1:2], ALU.subtract, ['sm'], ['sm'])
                ts_('dve', sm[:, 3:4], sm[:, 3:4], 1.0001, 1e-6, ALU.mult, ALU.add, ['sm'], ['sm'])
                ts_('dve', halves[:, 0:NBIS], cst[:, C_POW:C_POW + NBIS], sm[:, 3:4], None, ALU.mult, None, ['sm', 'cst'], ['halves'])
                cp('dve', sm[:, 4:5], sm[:, 1:2], ['sm'], ['sm'])
                N1 = max(128, (int(N * 0.40) // 128) * 128)
                N2 = N - N1
                ts_('dve', sm[:, 8:9], cst[:, C_KTAB + i:C_KTAB + i + 1], 2.0, float(-N2), ALU.mult, ALU.add, ['cst'], ['sm_k2'])
                for r in range(NBIS):
                    tt_('dve', sm[:, 5:6], sm[:, 4:5], halves[:, r:r + 1], ALU.add, ['sm', 'halves'], ['sm_mid'])
                    ts_('dve', Mq[:, 0:N1], score[:, 0:N1], sm[:, 5:6], 0.0, ALU.is_ge, ALU.add, ['score', 'sm_mid'], ['MqA', 'sm_cnt'],
                        accum=sm[:, 6:7])
                    op('act', lambda e: e.activation(out=Mq[:, N1:N], in_=score[:, N1:N], func=AF.Sign, bias=sm[:, 5:6], scale=-1.0,
                                                     accum_out=sm[:, 10:11]), ['score', 'sm_mid'], ['MqB', 'sm_s'])
                    stt_(sm[:, 7:8], sm[:, 6:7], 2.0, sm[:, 10:11], ALU.mult, ALU.subtract, ['sm_cnt', 'sm_s'], ['sm_tmp'])
                    ts_('dve', sm[:, 7:8], sm[:, 7:8], sm[:, 8:9], halves[:, r:r + 1], ALU.is_ge, ALU.mult,
                        ['sm_tmp', 'sm_k2', 'halves'], ['sm_tmp'])
                    tt_('dve', sm[:, 4:5], sm[:, 4:5], sm[:, 7:8], ALU.add, ['sm', 'sm_tmp'], ['sm'])
                ts_('dve', Mq[:, 0:N], score[:, 0:N], sm[:, 4:5], NEG, ALU.is_lt, ALU.mult, ['score', 'sm', 'MqA', 'MqB'], ['Mq', 'MqA', 'MqB'])
                nkt = 4 * (i + 1)
                kbase = ktc

                def st_mask(kt):
                    m, r = kt // 4, kt % 4
                    s2 = (kbase + kt) % 2
                    for par in range(2):
                        mm(ps[:, 2 * s2 + par, 0:384].rearrange("p (a b) -> p a b", b=128), kT2[par * 64:(par + 1) * 64, r, m * 128:(m + 1) * 128],
                           QT[par * 64:(par + 1) * 64, :, :], True, False, ['@kT2', 'qt%d' % s], stres(s2))
                    for par in range(2):
                        mm(ps[:, 2 * s2 + par, 0:384].rearrange("p (a b) -> p a b", b=128), Mq[:, kt * 128:(kt + 1) * 128], ident3[:], False, True,
                           ['Mq', 'ident3'], stres(s2))
                st_mask(0)
                for kt in range(nkt):
                    m, r = kt // 4, kt % 4
                    s2 = (kbase + kt) % 2
                    p3 = (kbase + kt) % 3
                    if kt + 1 < nkt:
                        st_mask(kt + 1)
                    act(pT[p3][:], ps[:, 2 * s2:2 * s2 + 2, 0:384], AF.Exp, stres(s2), ['pT%d' % p3], scale=0.125)
                    for hs in range(6):
                        par, c = hs // 3, hs % 3
                        mm(ps[:, 4, hs * 65:(hs + 1) * 65], pT[p3][:, par, c * 128:(c + 1) * 128], vx[:, r, m, :],
                           kt == 0 and hs == 0, kt == nkt - 1 and hs == 5, ['pT%d' % p3, '@vx', 'vx1'], ['b4'])
                ktc += nkt
                pvv = ps[:, 4, 0:390].rearrange("p (h e) -> p h e", e=65)
                op('dve', lambda e: e.reciprocal(out=rden[:, 0:6], in_=pvv[:, :, 64]), ['b4'], ['rden'])
                for hs in range(6):
                    par, c = hs // 3, hs % 3
                    head = 2 * c + par
                    ts_('dve', ytm[:, head * 64:(head + 1) * 64], ps[:, 4, hs * 65:hs * 65 + 64], rden[:, hs:hs + 1], None, ALU.mult, None,
                        ['b4', 'rden'], ['ytm'])
                for k in range(3):
                    op('pe', lambda e, k=k: e.transpose(out=psb[:, k * 128:(k + 1) * 128], in_=ytm[:, k * 128:(k + 1) * 128], identity=identb[:]),
                       ['ytm', 'identb'], ['b5'])
                act(yst[s][:], psb[:, 0:384].rearrange("p (k t) -> p k t", t=128), AF.Copy, ['b5'], ['yst%d' % s])
                dma('sp', fm(yT_d, 2, 5, i * 128, (i + 1) * 128), yst[s][:], 'yst%d' % s, ['yst%d' % s], ['@yT_d'])
            tr.barrier()
        stage_end('S2a')

        with ExitStack() as st:
            bd = sb(st, "bd", [128, 5, 6, 128], F32)
            bslot = sb(st, "bslot", [128, 8, 6, 128], F32)
            hc = sb(st, "hc", [128, 2, 4, NI + 1, 2], F32)
            hsel = sb(st, "hsel", [128, 2, NI, 2], F32)
            cqt = [sb(st, "cqt%d" % i, [128, 3, 128], BF16) for i in range(2)]
            ckt = [sb(st, "ckt%d" % i, [128, 3, 2, 4, 128], BF16) for i in range(2)]
            cvt = [sb(st, "cvt%d" % i, [128, 2, 4, 384], BF16) for i in range(2)]
            cbt = [sb(st, "cbt%d" % i, [128, 2, 128], F32) for i in range(2)]
            ut = [sb(st, "ut%d" % i, [128, 2, 130], F32) for i in range(2)]
            sbs = [sb(st, "sbs%d" % i, [128, 2, 384], F32) for i in range(2)]
            pTb = [sb(st, "pTb%d" % i, [128, 2, 384], BF16) for i in range(2)]
            rdb = sb(st, "rdb", [128, 8], F32)
            ytb = sb(st, "ytb", [128, 384], BF16)
            ystb = [sb(st, "ystb%d" % i, [128, 3, 128], BF16) for i in range(2)]
            acc = sb(st, "acc", [128, 2, 128], F32)
            yc = [sb(st, "yc%d" % i, [128, 2, 128], BF16) for i in range(2)]
            for dl in range(5):
                for par in range(2):
                    dma('sp', bd[:, dl, par * 3:(par + 1) * 3, :],
                        rel_bias[l, :, (dl * 6 + par * 3) * 128:(dl * 6 + par * 3 + 3) * 128].rearrange("p (h q) -> p h q", q=128),
                        'bd', [], ['bd%d%d' % (dl, par)])
            bdall = ['bd%d%d' % (dl, par) for dl in range(5) for par in range(2)]
            op('pool', lambda e: e.memset(bd[64:128, 0, :, 0:64], NEG), bdall, bdall)
            op('pool', lambda e: e.memset(bd[0:64, 4, :, 64:128], NEG), bdall, bdall)
            for sl in range(8):
                ts_('dve', bslot[:, sl, :, :], bd[:, 0, :, :], cst[:, C_SEL + sl * 5:C_SEL + sl * 5 + 1], cst[:, C_NEGS + sl:C_NEGS + sl + 1],
                    ALU.mult, ALU.add, bdall + ['cst'], ['bslot'])
                for dl in range(1, 5):
                    stt_(bslot[:, sl, :, :], bd[:, dl, :, :], cst[:, C_SEL + sl * 5 + dl:C_SEL + sl * 5 + dl + 1], bslot[:, sl, :, :],
                         ALU.mult, ALU.add, bdall + ['cst', 'bslot'], ['bslot'])
            op('pool', lambda e: e.memset(hc[:], 0.0), [], ['hc'])
            for r in range(4):
                for k in range(2):
                    dma('sp', hc[:, k, r, 1:NI + 1, :], hgath[r * 256 + k * 128:r * 256 + (k + 1) * 128, :].rearrange("p (b e) -> p b e", e=2),
                        'hc', ['hgath', 'hc'], ['hc%d%d' % (r, k)])
            hcall = ['hc%d%d' % (r, k) for r in range(4) for k in range(2)]
            first = True
            for r in range(4):
                for (colb, off) in ((C_SELA, 1), (C_SELB, 0)):
                    src = hc[:, :, r, off:off + NI, :]
                    if first:
                        ts_('dve', hsel[:], src, cst[:, colb + r:colb + r + 1], None, ALU.mult, None, hcall + ['cst'], ['hsel'])
                        first = False
                    else:
                        stt_(hsel[:], src, cst[:, colb + r:colb + r + 1], hsel[:], ALU.mult, ALU.add, hcall + ['cst', 'hsel'], ['hsel'])

            def load_b(i):
                s = i % 2
                dma('sp', cqt[s][:], fm(cqT_d, 0, 3, i * 128, (i + 1) * 128), 'cqt%d' % s, ['cqT_d'], ['cqt%d' % s])
                for mi in range(2):
                    blk = i - 1 + mi
                    if blk < 0:
                        continue
                    t_, bi = blk // NB4, blk % NB4
                    for r in range(4):
                        base = r * PAYR
                        dma('sp', ckt[s][:, :, mi, r, :],
                            gath[t_, base + 160:base + 544, bi * 128:(bi + 1) * 128].rearrange("(k p) t -> p k t", p=128),
                            'ckt%d' % s, ['gath'], ['@ckt%d' % s])
                        cvflat_r = gath[t_, base + 544:base + 928, :].rearrange("a t -> (a t)")
                        dma('sp', cvt[s][:, mi, r, :], cvflat_r[bi * 128 * 384:(bi + 1) * 128 * 384].rearrange("(p e) -> p e", e=384),
                            'cvt%d' % s, ['gath'], ['@cvt%d' % s])
                dma('sp', cbt[s][:], fm(cbu_d, 0, 2, i * 128, (i + 1) * 128), 'cbt%d' % s, ['cbu_d'], ['cbt%d' % s])
                dma('sp', ut[s][:, :, 2:130], fm(cbu_d, 2, 4, i * 128, (i + 1) * 128), 'ut%d' % s, ['cbu_d'], ['ut%d' % s])
            load_b(0)
            slc = 0
            for i in range(NI):
                s = i % 2
                if i + 1 < NI:
                    load_b(i + 1)
                slots = [(mi, r) for mi in range(2) for r in range(4) if not (i == 0 and mi == 0)]
                for si, (mi, r) in enumerate(slots):
                    sl = mi * 4 + r
                    s2 = slc % 2
                    slc += 1
                    for k in range(3):
                        for par in range(2):
                            mm(ps[:, 2 * s2 + par, k * 128:(k + 1) * 128], ckt[s][par * 64:(par + 1) * 64, k, mi, r, :],
                               cqt[s][par * 64:(par + 1) * 64, k, :], True, True, ['@ckt%d' % s, 'cqt%d' % s], ['b%d' % (2 * s2 + par)])
                    stt_(sbs[s2][:], ps[:, 2 * s2:2 * s2 + 2, 0:384], 0.125,
                         bslot[:, sl, :, :].rearrange("p (a k) q -> p a (k q)", a=2), ALU.mult, ALU.add,
                         ['b%d' % (2 * s2), 'b%d' % (2 * s2 + 1), 'bslot'], ['sbs%d' % s2])
                    act(pTb[s2][:], sbs[s2][:], AF.Exp, ['sbs%d' % s2], ['pTb%d' % s2])
                    for hs in range(6):
                        par, k = hs // 3, hs % 3
                        head = 2 * k + par
                        mm(ps[:, 4, head * 64:(head + 1) * 64], pTb[s2][:, par, k * 128:(k + 1) * 128], cvt[s][:, mi, r, head * 64:(head + 1) * 64],
                           si == 0 and hs == 0, False, ['pTb%d' % s2, '@cvt%d' % s], ['b4'])
                        mm(ps[:, 4, 384 + head:385 + head], pTb[s2][:, par, k * 128:(k + 1) * 128], onesb[:, 0:1],
                           False, si == len(slots) - 1 and hs == 5, ['pTb%d' % s2, 'onesb'], ['b4'])
                op('dve', lambda e: e.reciprocal(out=rdb[:, 0:6], in_=ps[:, 4, 384:390]), ['b4'], ['rdb'])
                tt_('dve', ytb[:].rearrange("p (h e) -> p h e", e=64), ps[:, 4, 0:384].rearrange("p (h e) -> p h e", e=64),
                    rdb[:, 0:6].unsqueeze(2).broadcast_to([128, 6, 64]), ALU.mult, ['b4', 'rdb'], ['ytb'])
                for k in range(3):
                    op('pe', lambda e, k=k: e.transpose(out=psb[:, k * 128:(k + 1) * 128], in_=ytb[:, k * 128:(k + 1) * 128], identity=identb[:]),
                       ['ytb', 'identb'], ['b5'])
                act(ystb[s][:], psb[:, 0:384].rearrange("p (k t) -> p k t", t=128), AF.Copy, ['b5'], ['ystb%d' % s])
                dma('sp', fm(yT_d, 5, 8, i * 128, (i + 1) * 128), ystb[s][:], 'ystb%d' % s, ['ystb%d' % s], ['@yT_d'])
                cp('pool', ut[s][:, :, 0:2], hsel[:, :, i, :], ['hsel', 'ut%d' % s], ['ut%d' % s])
                for k in range(2):
                    w = lambda jj: cw[:, l * 3 + jj, k:k + 1]
                    ts_('dve', acc[:, k, :], ut[s][:, k, 0:128], w(0), None, ALU.mult, None, ['ut%d' % s, 'cw'], ['acc'])
                    stt_(acc[:, k, :], ut[s][:, k, 1:129], w(1), acc[:, k, :], ALU.mult, ALU.add, ['ut%d' % s, 'cw', 'acc'], ['acc'])
                    stt_(acc[:, k, :], ut[s][:, k, 2:130], w(2), acc[:, k, :], ALU.mult, ALU.add, ['ut%d' % s, 'cw', 'acc'], ['acc'])
                tt_('dve', yc[s][:], acc[:], cbt[s][:], ALU.mult, ['acc', 'cbt%d' % s], ['yc%d' % s])
                dma('sp', fm(yT_d, 0, 2, i * 128, (i + 1) * 128), yc[s][:], 'yc%d' % s, ['yc%d' % s], ['@yT_d'])
            tr.barrier()
        stage_end('S2b')

        with ExitStack() as st:
            wo = sb(st, "wo", [128, 8, D], BF16)
            load_w_bf16(wo, w_mix_out[l], D, 'wo')
            yt = [sb(st, "yt%d" % i, [128, 8, TT], BF16) for i in range(2)]
            xr = [sb(st, "xr%d" % i, [128, 8, TT], F32) for i in range(2)]
            tbuf = sb(st, "tbuf", [128, 8, TT], F32)
            lb = ln_bufs(st)

            def load3(t):
                s = t % 2
                dma('sp', yt[s][:], fm(yT_d, 0, 8, t * TT, (t + 1) * TT), 'yt%d' % s, ['yT_d'], ['yt%d' % s])
                dma('sp', xr[s][:], res_src(0, 8, t * TT, (t + 1) * TT), 'xr%d' % s, ['xres'], ['xr%d' % s])
            load3(0)
            for t in range(NT):
                s = t % 2
                if t + 1 < NT:
                    load3(t + 1)
                for oc in range(8):
                    b = oc % 2
                    for kc in range(8):
                        mm(ps[:, b, 0:TT], wo[:, kc, oc * 128:(oc + 1) * 128], yt[s][:, kc, :], kc == 0, kc == 7, ['@wo', 'yt%d' % s], ['b%d' % b])
                    stt_(tbuf[:, oc, :], xr[s][:, oc, :], ALPHA, ps[:, b, 0:TT], ALU.mult, ALU.add, ['xr%d' % s, 'b%d' % b], ['t'])
                layer_norm(lb, tbuf, l, 0, t * TT, False, None)
            tr.barrier()
        stage_end('S3')

        st4 = ExitStack()
        if True:
            st = st4
            kmT = sb(st, "kmT", [128, 8, 256], BF16)
            vm = sb(st, "vm", [128, 2, D], BF16)
            if True:
                st2 = st
                wkv = sb(st2, "wkv", [128, 8, 2 * D], BF16)
                memf = sb(st2, "memf", [128, 8, 256], F32)
                memb = sb(st2, "memb", [128, 8, 256], BF16)
                load_w_bf16(wkv, w_mkv[l], 2 * D, 'wkv')
                dma('sp', memf[:], memT_in.rearrange("(c p) m -> p c m", p=128), 'memf', [], ['memf'])
                cp('dve', memb[:], memf[:], ['memf'], ['memb'])
                for ec in range(8):
                    b = ec % 2
                    for kc in range(8):
                        mm(ps[:, b, 0:256], wkv[:, kc, ec * 128:(ec + 1) * 128], memb[:, kc, :], kc == 0, kc == 7, ['@wkv', 'memb'], ['b%d' % b])
                    act(kmT[:, ec, :], ps[:, b, 0:256], AF.Copy, ['b%d' % b], ['kmT'])
                for mt in range(2):
                    for eh in range(2):
                        b = 2 + eh
                        for kc in range(8):
                            mm(ps[:, b, :], memb[:, kc, mt * 128:(mt + 1) * 128], wkv[:, kc, D + eh * 512:D + (eh + 1) * 512],
                               kc == 0, kc == 7, ['@wkv', 'memb'], ['b%d' % b])
                        act(vm[:, mt, eh * 512:(eh + 1) * 512], ps[:, b, :], AF.Copy, ['b%d' % b], ['vm'])
                tr.barrier()
        if True:
            st = st4
            wq = sb(st, "wq", [128, 8, D], BF16)
            wmo = sb(st, "wmo", [128, 8, D], BF16)
            load_w_bf16(wq, w_mq[l], D, 'wq')
            load_w_bf16(wmo, w_mo[l], D, 'wmo')
            xbt4 = [sb(st, "xbt4_%d" % i, [128, 8, TT], BF16) for i in range(2)]
            xr = [sb(st, "xr4_%d" % i, [128, 8, TT], F32) for i in range(2)]
            qm = sb(st, "qm", [128, 8, TT], BF16)
            pTm = [sb(st, "pTm%d" % i, [128, 2, TT], BF16) for i in range(2)]
            rdm = sb(st, "rdm", [128, TT], F32)
            om = sb(st, "om", [128, 8, TT], BF16)
            tbuf = sb(st, "tbuf4", [128, 8, TT], F32)
            lb = ln_bufs(st)

            def load4(t):
                s = t % 2
                dma('sp', xbt4[s][:], fm(xb, 0, 8, t * TT, (t + 1) * TT), 'xbt4%d' % s, ['xb'], ['xbt4%d' % s])
                dma('sp', xr[s][:], fm(xres, 0, 8, t * TT, (t + 1) * TT), 'xr4%d' % s, ['xres'], ['xr4%d' % s])
            load4(0)
            hc4 = 0
            for t in range(NT):
                s = t % 2
                for ec in range(8):
                    b = ec % 2
                    for kc in range(8):
                        mm(ps[:, b, 0:TT], wq[:, kc, ec * 128:(ec + 1) * 128], xbt4[s][:, kc, :], kc == 0, kc == 7, ['@wq', 'xbt4%d' % s], ['b%d' % b])
                    act(qm[:, ec, :], ps[:, b, 0:TT], AF.Copy, ['b%d' % b], ['qm'])
                for h in range(4):
                    hs_ = hc4 % 2
                    hc4 += 1
                    for mt in range(2):
                        for dc in range(2):
                            mm(ps[:, 2 + mt, 0:TT], kmT[:, 2 * h + dc, mt * 128:(mt + 1) * 128], qm[:, 2 * h + dc, :], dc == 0, dc == 1,
                               ['kmT', 'qm'], ['b%d' % (2 + mt)])
                    act(pTm[hs_][:], ps[:, 2:4, 0:TT], AF.Exp, ['b2', 'b3'], ['pTm%d' % hs_], scale=1.0 / 16.0)
                    for mt in range(2):
                        mm(ps[:, 6, 0:TT], onesb[:], pTm[hs_][:, mt, :], mt == 0, mt == 1, ['onesb', 'pTm%d' % hs_], ['b6'])
                    op('dve', lambda e: e.reciprocal(out=rdm[:], in_=ps[:, 6, 0:TT]), ['b6'], ['rdm'])
                    for dc in range(2):
                        b = 4 + dc
                        for mt in range(2):
                            mm(ps[:, b, 0:TT], vm[:, mt, h * 256 + dc * 128:h * 256 + (dc + 1) * 128], pTm[hs_][:, mt, :], mt == 0, mt == 1,
                               ['vm', 'pTm%d' % hs_], ['b%d' % b])
                        tt_('dve', om[:, 2 * h + dc, :], ps[:, b, 0:TT], rdm[:], ALU.mult, ['b%d' % b, 'rdm'], ['om'])
                if t + 1 < NT:
                    load4(t + 1)
                for oc in range(8):
                    b = oc % 2
                    for kc in range(8):
                        mm(ps[:, b, 0:TT], wmo[:, kc, oc * 128:(oc + 1) * 128], om[:, kc, :], kc == 0, kc == 7, ['@wmo', 'om'], ['b%d' % b])
                    stt_(tbuf[:, oc, :], xr[s][:, oc, :], ALPHA, ps[:, b, 0:TT], ALU.mult, ALU.add, ['xr4%d' % s, 'b%d' % b], ['t'])
                layer_norm(lb, tbuf, l, 1, t * TT, False, None)
            tr.barrier()
            st4.close()
        stage_end('S4')

        with ExitStack() as st:
            xba = sb(st, "xba", [128, 8, TL], BF16)
            dma('sp', xba[:], fm(xb, 0, 8, 0, TL), 'xba', ['xb'], ['xba'])
            w1 = [sb(st, "w1_%d" % i, [128, 8, 128], BF16) for i in range(4)]
            rr = [sb(st, "rr%d" % i, [128, TT], F32) for i in range(2)]
            hst = [sb(st, "hst%d" % i, [128, TT], BF16) for i in range(4)]

            def loadw1(f):
                s = f % 4
                dma('pool', w1[s][:], w_ff1[l][:, f * 128:(f + 1) * 128].rearrange("(kc p) n -> p kc n", p=128), 'w1_%d' % s, [], ['w1_%d' % s])
            for f in range(3):
                loadw1(f)
            ct = 0
            for f in range(32):
                if f + 3 < 32:
                    loadw1(f + 3)
                s = f % 4
                for t in range(NT):
                    b = ct % 4
                    r2 = ct % 2
                    h4 = ct % 4
                    ct += 1
                    for kc in range(8):
                        mm(ps[:, b, 0:TT], w1[s][:, kc, :], xba[:, kc, t * TT:(t + 1) * TT], kc == 0, kc == 7, ['w1_%d' % s, 'xba'], ['b%d' % b])
                    act(rr[r2][:], ps[:, b, 0:TT], AF.Relu, ['b%d' % b], ['rr%d' % r2])
                    tt_('pool', hst[h4][:], rr[r2][:], rr[r2][:], ALU.mult, ['rr%d' % r2], ['hst%d' % h4])
                    dma('sp', hT_d[f, :, t * TT:(t + 1) * TT], hst[h4][:], 'hst%d' % h4, ['hst%d' % h4], ['@hT_d'])
            tr.barrier()
        stage_end('S5a')

        with ExitStack() as st:
            w2 = sb(st, "w2", [128, 32, D], BF16)
            load_w_bf16(w2, w_ff2[l], D, 'w2', nkc=32)
            ht = [sb(st, "ht%d" % i, [128, 32, TT], BF16) for i in range(2)]
            xr = [sb(st, "xr5_%d" % i, [128, 8, TT], F32) for i in range(1)]
            xr = [xr[0], xr[0]]
            tbuf = sb(st, "tbuf5", [128, 8, TT], F32)
            lb = ln_bufs(st)

            def load5(t):
                s = t % 2
                dma('sp', ht[s][:], fm(hT_d, 0, 32, t * TT, (t + 1) * TT), 'ht%d' % s, ['hT_d'], ['ht%d' % s])

            def load5x(t):
                dma('sp', xr[0][:], fm(xres, 0, 8, t * TT, (t + 1) * TT), 'xr50', ['xres'], ['xr50'])
            load5(0)
            load5x(0)
            for t in range(NT):
                s = t % 2
                if t + 1 < NT:
                    load5(t + 1)
                for oc in range(8):
                    b = oc % 2
                    for kc in range(32):
                        mm(ps[:, b, 0:TT], w2[:, kc, oc * 128:(oc + 1) * 128], ht[s][:, kc, :], kc == 0, kc == 31, ['@w2', 'ht%d' % s], ['b%d' % b])
                    stt_(tbuf[:, oc, :], xr[s][:, oc, :], ALPHA, ps[:, b, 0:TT], ALU.mult, ALU.add, ['xr50', 'b%d' % b], ['t'])
                if t + 1 < NT:
                    load5x(t + 1)
                layer_norm(lb, tbuf, l, 2, t * TT, l == L - 1, None)
            tr.barrier()

    tr.barrier()
    es.close()
    return nc


def make_consts(S, j):
    TL = S // 4
    NI = TL // 128
    NCST = C_KTAB + NI
    c = np.zeros((128, NCST), np.float32)
    c[:, C_ID:C_ID + 128] = np.eye(128, dtype=np.float32)
    pmq = np.zeros((128, 128), np.float32)
    for hb in (0, 64):
        for d in range(8):
            pmq[hb + d + 8, hb + d] = -1.0
            pmq[hb + d, hb + d + 8] = 1.0
    pmi = np.zeros((128, 128), np.float32)
    for hb in (0, 32, 64, 96):
        for d in range(4):
            pmi[hb + d + 4, hb + d] = -1.0
            pmi[hb + d, hb + d + 4] = 1.0
    c[:, C_PMQ:C_PMQ + 128] = pmq
    c[:, C_PMI:C_PMI + 128] = pmi
    fq = np.power(np.float32(500000.0), -np.arange(8, dtype=np.float32) / np.float32(8))
    fi = np.power(np.float32(500000.0), -np.arange(4, dtype=np.float32) / np.float32(4))
    for p in range(128):
        d = p % 64
        c[p, C_INVQ] = fq[d % 8] if d < 16 else 0.0
        d = p % 32
        c[p, C_INVI] = fi[d % 4] if d < 8 else 0.0
    for mi in range(2):
        for r in range(4):
            sl = mi * 4 + r
            dl = (j - r) if mi == 1 else (4 + j - r)
            if 0 <= dl <= 4:
                c[:, C_SEL + sl * 5 + dl] = 1.0
            else:
                c[:, C_NEGS + sl] = NEG
    if j > 0:
        c[:, C_SELA + j - 1] = 1.0
    else:
        c[:, C_SELB + 3] = 1.0
    for r in range(32):
        c[:, C_POW + r] = 2.0 ** -(r + 1)
    madm = np.zeros((128, 4, 128), np.float32)
    for r in range(4):
        if r > j:
            madm[:, r, :] = NEG
        elif r == j:
            madm[0:64, r, 64:128] = NEG
    c[:, C_MADM:C_MADM + 512] = madm.reshape(128, 512)
    for i in range(NI):
        g = 4 * i + j
        c[0:64, C_KTAB + i] = min(256, (2 * g + 1) * 64)
        c[64:128, C_KTAB + i] = min(256, (2 * g + 2) * 64)
    return c


def run_module(inputs, S, L):
    x = np.asarray(inputs['x'], np.float32)
    B = x.shape[0]
    TL = S // 4
    NBLK = S // 128
    in_maps = []
    lnp = np.stack([np.asarray(inputs[k], np.float32)[:L] for k in ('ln1_g', 'ln1_b', 'ln2_g', 'ln2_b', 'ln3_g', 'ln3_b')], axis=1)
    for c in range(8):
        b, j = c // 4, c % 4
        xs = x[b].reshape(NBLK, 128, D)[j::4].reshape(TL, D)
        ps_ = np.asarray(inputs['positions'])[b].reshape(NBLK, 128)[j::4].reshape(1, TL).astype(np.int32)
        m = {
            'xT': np.ascontiguousarray(xs.T),
            'pos': np.ascontiguousarray(ps_),
            'cst': make_consts(S, j),
            'memT': np.ascontiguousarray(np.asarray(inputs['mem'], np.float32)[b].T),
            'lnp': np.ascontiguousarray(lnp.reshape(L * 6, 8, 128).transpose(2, 0, 1)),
            'conv_w': np.ascontiguousarray(np.asarray(inputs['conv_w'], np.float32)[:L].reshape(L * 3, 2, 128).transpose(2, 0, 1)),
        }
        for k in ('w_in', 'w_mix_out', 'w_mq', 'w_mkv', 'w_mo', 'w_ff1', 'w_ff2'):
            m[k] = np.ascontiguousarray(np.asarray(inputs[k], np.float32)[:L])
        in_maps.append(m)
    rb = np.asarray(inputs['rel_bias'], np.float32)[:L]
    e = np.arange(767)
    ext = rb[:, :, np.clip(e - 127, -63, 128) + 63]
    s_ = np.arange(128)[:, None, None, None]
    dl_ = np.arange(5)[None, :, None, None]
    hs_ = np.arange(6)[None, None, :, None]
    q_ = np.arange(128)[None, None, None, :]
    head = 2 * (hs_ % 3) + hs_ // 3
    idx = 128 * dl_ + 127 + q_ - s_
    relb = np.ascontiguousarray(ext[:, head, idx].reshape(L, 128, 5 * 6 * 128))
    for m in in_maps:
        m['relb'] = relb
    nc = build_program(S, L)
    res = run_bass_kernel_spmd(nc, in_maps, core_ids=list(range(8)))
    out = np.zeros((B, NBLK, 128, D), np.float32)
    for c in range(8):
        b, j = c // 4, c % 4
        o = np.asarray(res.results[c]['outT'], np.float32).T.reshape(TL // 128, 128, D)
        out[b, j::4] = o
    return out.reshape(B, S, D)


def kernel(**inputs):
    return run_module(inputs, 16384, 4)
```
